# Optimizing a Trainium2 kernel written in Bass

```python
import math
import jax, jax.numpy as jnp
from jax import lax
import numpy as np

D_MODEL = 1024
BATCH = 8
SEQ = 8192
DEPTH = 1

GRID_W = 64
CTX_LEN = 256
MIX_WIDTH = D_MODEL
SSM_WIDTH = MIX_WIDTH // 2
SSM_GROUP = 16
SSM_GROUPS = SSM_WIDTH // SSM_GROUP
SSM_STATE = 64
ATTN_WIDTH = MIX_WIDTH - SSM_WIDTH
DIFF_HEADS = 4
DIFF_HEAD_DIM = ATTN_WIDTH // (2 * DIFF_HEADS)
ROT_FREQS = DIFF_HEAD_DIM // 4
ROPE_BASE = 10000.0
IN_WIDTH = SSM_WIDTH + 3 * ATTN_WIDTH
D_FF = 128 * ((8 * D_MODEL // 3 + 127) // 128)
Q_BLOCK = 128
N_MOD = 9
EPS = 1e-6

kernel_name = 'hymba_s5_diffattn_macaron_dit'


def rms_norm(x, g):
    xf = x.astype(jnp.float32)
    y = xf * lax.rsqrt(jnp.mean(xf * xf, axis=-1, keepdims=True) + EPS)
    return (y * g.astype(jnp.float32)).astype(x.dtype)


def adaln(h, g, mod, i):
    return rms_norm(h, g) * (1.0 + mod[:, :, 3 * i + 1]) + mod[:, :, 3 * i]


def swiglu(h, w_in, w_out):
    gate, up = jnp.split(h @ w_in, 2, axis=-1)
    return (jax.nn.silu(gate) * up) @ w_out


def axial_rope_tables(rows):
    row = jnp.repeat(jnp.arange(rows, dtype=jnp.float32), GRID_W)
    col = jnp.tile(jnp.arange(GRID_W, dtype=jnp.float32), rows)
    inv_freq = ROPE_BASE ** (-jnp.arange(ROT_FREQS, dtype=jnp.float32) / ROT_FREQS)
    ang = jnp.stack([row[:, None] * inv_freq, col[:, None] * inv_freq], axis=1)
    return jnp.cos(ang), jnp.sin(ang)


def apply_axial_rope(t, cos, sin):
    ts = t.astype(jnp.float32).reshape(*t.shape[:-1], 2, 2, ROT_FREQS)
    x1 = ts[..., 0, :]
    x2 = ts[..., 1, :]
    cb = cos[None, :, None, None]
    sb = sin[None, :, None, None]
    out = jnp.stack([x1 * cb - x2 * sb, x2 * cb + x1 * sb], axis=-2)
    return out.reshape(t.shape).astype(t.dtype)


def cmul(ar, ai, br, bi):
    return ar * br - ai * bi, ar * bi + ai * br


def zoh_discretise(a_re, a_im, log_dt, b_re, b_im):
    dt = jnp.exp(log_dt)[:, None]
    mag = jnp.exp(dt * a_re)
    abar_re = mag * jnp.cos(dt * a_im)
    abar_im = mag * jnp.sin(dt * a_im)
    zr = abar_re - 1.0
    zi = abar_im
    den = a_re * a_re + a_im * a_im
    coef_re = (zr * a_re + zi * a_im) / den
    coef_im = (zi * a_re - zr * a_im) / den
    bbar_re, bbar_im = cmul(coef_re[..., None], coef_im[..., None], b_re, b_im)
    return abar_re, abar_im, bbar_re, bbar_im


def diag_scan(abar_re, abar_im, bu_re, bu_im, reverse):
    length = bu_re.shape[1]
    a_re = jnp.broadcast_to(abar_re, (1, length) + abar_re.shape)
    a_im = jnp.broadcast_to(abar_im, (1, length) + abar_im.shape)

    def combine(e1, e2):
        a1r, a1i, b1r, b1i = e1
        a2r, a2i, b2r, b2i = e2
        ar, ai = cmul(a2r, a2i, a1r, a1i)
        br, bi = cmul(a2r, a2i, b1r, b1i)
        return ar, ai, br + b2r, bi + b2i

    return lax.associative_scan(combine, (a_re, a_im, bu_re, bu_im), axis=1, reverse=reverse)


def s5_readout(h_re, h_im, c_re, c_im):
    return jnp.einsum('blgp,ghp->blgh', h_re, c_re) - jnp.einsum('blgp,ghp->blgh', h_im, c_im)


def s5_bidirectional(u_lat, u_ctx, a_re, a_im, log_dt, b_re, b_im, c_re, c_im, d_skip,
                     w_glu, b_glu, with_ctx_out):
    f32 = jnp.float32
    out_dtype = u_lat.dtype
    u_lat = u_lat.astype(f32)
    u_ctx = u_ctx.astype(f32)
    d_skip = d_skip.astype(f32)
    y_lat = d_skip * u_lat
    y_ctx = d_skip * u_ctx if with_ctx_out else None
    for direction in range(2):
        reverse = direction == 1
        abr, abi, bbr, bbi = zoh_discretise(a_re[direction].astype(f32), a_im[direction].astype(f32),
                                            log_dt[direction].astype(f32),
                                            b_re[direction].astype(f32), b_im[direction].astype(f32))
        cr = c_re[direction].astype(f32)
        ci = c_im[direction].astype(f32)
        buc_r = jnp.einsum('blgh,gph->blgp', u_ctx, bbr)
        buc_i = jnp.einsum('blgh,gph->blgp', u_ctx, bbi)
        _, _, hc_r, hc_i = diag_scan(abr, abi, buc_r, buc_i, reverse)
        end = 0 if reverse else -1
        h0_r = hc_r[:, end][:, None]
        h0_i = hc_i[:, end][:, None]
        bul_r = jnp.einsum('blgh,gph->blgp', u_lat, bbr)
        bul_i = jnp.einsum('blgh,gph->blgp', u_lat, bbi)
        ap_r, ap_i, hl_r, hl_i = diag_scan(abr, abi, bul_r, bul_i, reverse)
        carry_r, carry_i = cmul(ap_r, ap_i, h0_r, h0_i)
        y_lat = y_lat + s5_readout(hl_r + carry_r, hl_i + carry_i, cr, ci)
        if with_ctx_out:
            y_ctx = y_ctx + s5_readout(hc_r, hc_i, cr, ci)

    def glu(y):
        g = jax.nn.gelu(y).reshape(*y.shape[:2], SSM_WIDTH)
        return g * jax.nn.sigmoid(g @ w_glu.astype(f32) + b_glu.astype(f32))

    out_ctx = glu(y_ctx).astype(out_dtype) if with_ctx_out else None
    return glu(y_lat).astype(out_dtype), out_ctx


def diff_softmax_attend(q, k, v, lam):
    s = jnp.einsum('bqhcd,bkhcd->bhcqk', q, k).astype(jnp.float32) * (DIFF_HEAD_DIM ** -0.5)
    p = jax.nn.softmax(s, axis=-1)
    p_diff = p[:, :, 0] - lam * p[:, :, 1]
    return jnp.einsum('bhqk,bkhe->bqhe', p_diff.astype(v.dtype), v)


def differential_attention(q_lat, k_lat, v_lat, q_ctx, k_ctx, v_ctx, lam_q, lam_k, subln_g,
                           lam_init, with_ctx_out):
    f32 = jnp.float32
    lam = (jnp.exp(jnp.sum(lam_q[0].astype(f32) * lam_k[0].astype(f32)))
           - jnp.exp(jnp.sum(lam_q[1].astype(f32) * lam_k[1].astype(f32))) + lam_init)
    B, L = q_lat.shape[:2]
    k_all = jnp.concatenate([k_lat, k_ctx], axis=1)
    v_all = jnp.concatenate([v_lat, v_ctx], axis=1)
    nb = L // Q_BLOCK
    q_blocks = q_lat.reshape(B, nb, Q_BLOCK, DIFF_HEADS, 2, DIFF_HEAD_DIM).swapaxes(0, 1)
    o = lax.map(lambda qb: diff_softmax_attend(qb, k_all, v_all, lam), q_blocks)
    o_lat = o.swapaxes(0, 1).reshape(B, L, DIFF_HEADS, 2 * DIFF_HEAD_DIM)

    def finish(out):
        return (rms_norm(out, subln_g) * (1.0 - lam_init)).reshape(*out.shape[:2], ATTN_WIDTH)

    out_ctx = finish(diff_softmax_attend(q_ctx, k_ctx, v_ctx, lam)) if with_ctx_out else None
    return finish(o_lat), out_ctx


def split_mixer_inputs(p):
    b, n, _ = p.shape
    o1 = SSM_WIDTH
    o2 = o1 + ATTN_WIDTH
    o3 = o2 + ATTN_WIDTH
    u = p[..., :o1].reshape(b, n, SSM_GROUPS, SSM_GROUP)
    q = p[..., o1:o2].reshape(b, n, DIFF_HEADS, 2, DIFF_HEAD_DIM)
    k = p[..., o2:o3].reshape(b, n, DIFF_HEADS, 2, DIFF_HEAD_DIM)
    v = p[..., o3:].reshape(b, n, DIFF_HEADS, 2 * DIFF_HEAD_DIM)
    return u, q, k, v


def hybrid_mixer(h_lat, h_ctx, cos, sin, w_in, w_out, a_re, a_im, log_dt, b_re, b_im, c_re, c_im,
                 d_skip, w_glu, b_glu, lam_q, lam_k, subln_g, lam_init, with_ctx_out):
    u_lat, q_lat, k_lat, v_lat = split_mixer_inputs(h_lat @ w_in)
    u_ctx, q_ctx, k_ctx, v_ctx = split_mixer_inputs(h_ctx @ w_in)
    q_lat = apply_axial_rope(q_lat, cos, sin)
    k_lat = apply_axial_rope(k_lat, cos, sin)
    s_lat, s_ctx = s5_bidirectional(u_lat, u_ctx, a_re, a_im, log_dt, b_re, b_im, c_re, c_im,
                                    d_skip, w_glu, b_glu, with_ctx_out)
    a_lat, a_ctx = differential_attention(q_lat, k_lat, v_lat, q_ctx, k_ctx, v_ctx, lam_q, lam_k,
                                          subln_g, lam_init, with_ctx_out)
    y_lat = jnp.concatenate([s_lat, a_lat], axis=-1) @ w_out
    y_ctx = jnp.concatenate([s_ctx, a_ctx], axis=-1) @ w_out if with_ctx_out else None
    return y_lat, y_ctx


def setup_inputs(seed: int = 0) -> dict:
    key = jax.random.key(seed)
    ks = jax.random.split(key, 26)
    f32 = jnp.float32
    nrm = lambda k, shape, s: jax.random.normal(k, shape, f32) * s
    D, G, P, H = D_MODEL, SSM_GROUPS, SSM_STATE, SSM_GROUP
    a_im0 = math.pi * jnp.arange(P, dtype=f32)
    log_lo, log_hi = math.log(0.001), math.log(0.1)
    return {
        'x': nrm(ks[0], (BATCH, SEQ, D), 1.0),
        'c': nrm(ks[1], (BATCH, D), 1.0),
        'ctx': nrm(ks[2], (BATCH, CTX_LEN, D), 1.0),
        'c_ctx': nrm(ks[3], (D,), 1.0),
        'w_mod': nrm(ks[4], (DEPTH, D, N_MOD * D), 0.5 * D ** -0.5),
        'b_mod': nrm(ks[5], (DEPTH, N_MOD * D), 0.02),
        'norm_g': 1.0 + nrm(ks[6], (DEPTH, 3, D), 0.02),
        'ffn_w_in': nrm(ks[7], (DEPTH, 2, D, 2 * D_FF), D ** -0.5),
        'ffn_w_out': nrm(ks[8], (DEPTH, 2, D_FF, D), D_FF ** -0.5),
        'w_in': nrm(ks[9], (DEPTH, D, IN_WIDTH), D ** -0.5),
        'w_out': nrm(ks[10], (DEPTH, MIX_WIDTH, D), MIX_WIDTH ** -0.5),
        'ssm_a_re': -0.5 + nrm(ks[11], (DEPTH, 2, G, P), 0.01),
        'ssm_a_im': a_im0 + nrm(ks[12], (DEPTH, 2, G, P), 0.01),
        'ssm_log_dt': log_lo + (log_hi - log_lo) * jax.random.uniform(ks[13], (DEPTH, 2, G), f32),
        'ssm_b_re': nrm(ks[14], (DEPTH, 2, G, P, H), (2 * H) ** -0.5),
        'ssm_b_im': nrm(ks[15], (DEPTH, 2, G, P, H), (2 * H) ** -0.5),
        'ssm_c_re': nrm(ks[16], (DEPTH, 2, G, H, P), (2 * P) ** -0.5),
        'ssm_c_im': nrm(ks[17], (DEPTH, 2, G, H, P), (2 * P) ** -0.5),
        'ssm_d': nrm(ks[18], (DEPTH, G, H), 1.0),
        'w_glu': nrm(ks[19], (DEPTH, SSM_WIDTH, SSM_WIDTH), SSM_WIDTH ** -0.5),
        'b_glu': nrm(ks[20], (DEPTH, SSM_WIDTH), 0.02),
        'lam_q': nrm(ks[21], (DEPTH, 2, DIFF_HEAD_DIM), 0.1),
        'lam_k': nrm(ks[22], (DEPTH, 2, DIFF_HEAD_DIM), 0.1),
        'subln_g': 1.0 + nrm(ks[23], (DEPTH, 2 * DIFF_HEAD_DIM), 0.02),
        'final_g': 1.0 + nrm(ks[24], (D,), 0.02),
    }


def reference(x, c, ctx, c_ctx, w_mod, b_mod, norm_g, ffn_w_in, ffn_w_out, w_in, w_out,
              ssm_a_re, ssm_a_im, ssm_log_dt, ssm_b_re, ssm_b_im, ssm_c_re, ssm_c_im, ssm_d,
              w_glu, b_glu, lam_q, lam_k, subln_g, final_g):
    B, L, _ = x.shape
    rows = L // GRID_W
    cos, sin = axial_rope_tables(rows)
    silu_c = jax.nn.silu(c)
    silu_cc = jax.nn.silu(c_ctx)
    for l in range(DEPTH):
        update_ctx = l < DEPTH - 1
        lam_init = 0.8 - 0.6 * math.exp(-0.3 * l)
        mod_lat = (silu_c @ w_mod[l] + b_mod[l]).reshape(B, 1, N_MOD, D_MODEL)
        mod_ctx = (silu_cc @ w_mod[l] + b_mod[l]).reshape(1, 1, N_MOD, D_MODEL)
        x = x + 0.5 * mod_lat[:, :, 2] * swiglu(adaln(x, norm_g[l, 0], mod_lat, 0),
                                                ffn_w_in[l, 0], ffn_w_out[l, 0])
        ctx = ctx + 0.5 * mod_ctx[:, :, 2] * swiglu(adaln(ctx, norm_g[l, 0], mod_ctx, 0),
                                                    ffn_w_in[l, 0], ffn_w_out[l, 0])
        y_lat, y_ctx = hybrid_mixer(adaln(x, norm_g[l, 1], mod_lat, 1),
                                    adaln(ctx, norm_g[l, 1], mod_ctx, 1),
                                    cos, sin, w_in[l], w_out[l], ssm_a_re[l], ssm_a_im[l],
                                    ssm_log_dt[l], ssm_b_re[l], ssm_b_im[l], ssm_c_re[l],
                                    ssm_c_im[l], ssm_d[l], w_glu[l], b_glu[l], lam_q[l], lam_k[l],
                                    subln_g[l], lam_init, update_ctx)
        x = x + mod_lat[:, :, 5] * y_lat
        x = x + 0.5 * mod_lat[:, :, 8] * swiglu(adaln(x, norm_g[l, 2], mod_lat, 2),
                                                ffn_w_in[l, 1], ffn_w_out[l, 1])
        if update_ctx:
            ctx = ctx + mod_ctx[:, :, 5] * y_ctx
            ctx = ctx + 0.5 * mod_ctx[:, :, 8] * swiglu(adaln(ctx, norm_g[l, 2], mod_ctx, 2),
                                                        ffn_w_in[l, 1], ffn_w_out[l, 1])
    return rms_norm(x, final_g)
```

```python
import numpy as np
import concourse.bass as bass
import concourse.mybir as mybir
from concourse.bass_utils import run_bass_kernel_spmd

F32 = mybir.dt.float32
BF16 = mybir.dt.bfloat16
U8 = mybir.dt.uint8
AF = mybir.ActivationFunctionType
ALU = mybir.AluOpType

D = 1024
KD = 8
FF = 2816
NFF = 22
LC = 256
TS = 64
WARM_N = 24
BURST_EVERY = 22
BURST_N = 10
HEAD_WARM_N = 10
ATTACH_WAIT = True
DEFER_N = 20
EPS = 1e-6
LAM_INIT = 0.2


class Buf:
    __slots__ = ("name", "writers", "readers", "prev_readers")

    def __init__(self, name=""):
        self.name = name
        self.writers = []
        self.readers = []
        self.prev_readers = []


class Op:
    __slots__ = ("eng", "fn", "deps", "idx", "is_dma", "sem", "sem_val", "signals", "full")

    def __init__(self, eng, fn, is_dma):
        self.eng = eng
        self.fn = fn
        self.deps = []
        self.is_dma = is_dma
        self.sem = None
        self.sem_val = None
        self.signals = False
        self.idx = 0


class Prog:
    ENGS = ("pe", "act", "dve", "pool", "sp")

    def __init__(self, nc):
        self.nc = nc
        self.ops = {e: [] for e in self.ENGS}
        self.all_ops = []
        self.eng_obj = {"pe": nc.tensor, "act": nc.scalar, "dve": nc.vector,
                        "pool": nc.gpsimd, "sp": nc.sync}

    def op(self, eng, fn, reads=(), writes=(), dma=False, part=False, semkey=None):
        o = Op(eng, fn, dma)
        o.idx = len(self.ops[eng])
        o.full = not (dma or part)
        deps = set()
        for b in reads:
            deps.update(b.writers)
        for b in writes:
            cont = (dma or part) and not b.readers
            if cont:
                deps.update(b.prev_readers)
                deps.update(w for w in b.writers if w.full)
            else:
                deps.update(b.writers)
                deps.update(b.readers)
        best = {}
        out = []
        for d in deps:
            if d is o:
                continue
            if d.is_dma:
                out.append(d)
                continue
            if d.eng == "pe" and eng == "pe" and not dma:
                continue
            if d.eng not in best or best[d.eng].idx < d.idx:
                best[d.eng] = d
        o.deps = out + list(best.values())
        for b in reads:
            b.readers.append(o)
        for b in writes:
            cont = (dma or part) and not b.readers
            if cont:
                b.writers.append(o)
            else:
                b.prev_readers = b.readers
                b.writers = [o]
            b.readers = []
        if dma:
            o.sem = semkey
        self.ops[eng].append(o)
        self.all_ops.append(o)
        return o

    def barrier(self):
        deps = []
        for e in self.ENGS:
            for o in reversed(self.ops[e]):
                if not o.is_dma and o.fn is not None:
                    deps.append(o)
                    break
        dma_last = {}
        for o in self.all_ops:
            if o.is_dma:
                dma_last[id(o.sem)] = o
        deps = deps + list(dma_last.values())
        for e in self.ENGS:
            w = Op(e, None, False)
            w.full = True
            w.deps = [d for d in deps if d.eng != e or d.is_dma]
            self.ops[e].append(w)
            self.all_ops.append(w)

    def emit(self):
        nc = self.nc
        for o in self.all_ops:
            for d in o.deps:
                d.signals = True
        esem = {e: nc.alloc_semaphore(name="sem_" + e) for e in self.ENGS}
        cnt = {e: 0 for e in self.ENGS}
        bufsem = {}
        bufcnt = {}
        for o in self.all_ops:
            if o.is_dma:
                key = id(o.sem)
                if key not in bufsem:
                    bufsem[key] = nc.alloc_semaphore(name="dsem_%d" % len(bufsem))
                    bufcnt[key] = 0
                bufcnt[key] += 16
                o.sem_val = bufcnt[key]
                o.sem = bufsem[key]
            elif o.signals:
                cnt[o.eng] += 1
                o.sem_val = cnt[o.eng]
                o.sem = esem[o.eng]
        self.stats = dict(nsem=len(bufsem) + 5, cnt=cnt, maxdma=max(bufcnt.values()),
                          nops={e: len(self.ops[e]) for e in self.ENGS})
        for e in self.ENGS:
            eng = self.eng_obj[e]
            waited = {}
            for o in self.ops[e]:
                need = {}
                for d in o.deps:
                    k = id(d.sem)
                    if waited.get(k, 0) >= d.sem_val:
                        continue
                    if k not in need or need[k][1] < d.sem_val:
                        need[k] = (d.sem, d.sem_val)
                attach = None
                if e == "pe" and o.fn is not None and len(need) == 1 and ATTACH_WAIT:
                    (k, (s, v)), = need.items()
                    attach = (s, v)
                    waited[k] = v
                else:
                    for k, (s, v) in need.items():
                        eng.wait_ge(s, v)
                        waited[k] = v
                if o.fn is None:
                    continue
                ins = o.fn(eng)
                if attach is not None:
                    ins._wait_ge(attach[0], attach[1])
                if o.is_dma:
                    ins.then_inc(o.sem, 16)
                elif o.signals:
                    ins.then_inc(o.sem, 1)


class T:
    __slots__ = ("ap", "b")

    def __init__(self, ap, name=""):
        self.ap = ap
        self.b = Buf(name)


def build(L):
    LX = LC + L + LC
    NCH = LX // TS
    NS = NCH - LC // TS
    C0B = LC // TS
    LK = L + LC
    NKB = LK // 128
    nc = bass.Bass("TRN2", target_bir_lowering=False)
    P = Prog(nc)

    def din(name, shape, dt=F32):
        return nc.dram_tensor(name, list(shape), dt, kind="ExternalInput").ap()

    DRAM_LIST = []

    def dscr(name, shape, dt):
        t = T(nc.dram_tensor(name, list(shape), dt).ap(), name)
        DRAM_LIST.append(t)
        return t

    x_d = din("x", [L, D]); c_d = din("c", [D]); ctx_d = din("ctx", [LC, D]); cc_d = din("c_ctx", [D])
    wmod_d = din("w_mod", [D, 9 * D]); bmod_d = din("b_mod", [9 * D]); ng_d = din("norm_g", [3, D])
    fw1_d = din("ffn_w_in", [2, D, 2 * FF]); fw2_d = din("ffn_w_out", [2, FF, D])
    win_d = din("w_in_ext", [D, 3072]); wout_d = din("w_out", [D, D])
    are_d = din("ssm_a_re", [64, 64]); aim_d = din("ssm_a_im", [64, 64]); ldt_d = din("ssm_log_dt", [64])
    bre_d = din("ssm_b_re", [64, 64, 16]); bim_d = din("ssm_b_im", [64, 64, 16])
    cre_d = din("ssm_c_re", [64, 16, 64]); cim_d = din("ssm_c_im", [64, 16, 64])
    dsk_d = din("ssm_d", [512]); wglu_d = din("w_glu", [512, 512]); bglu_d = din("b_glu", [512])
    lq_d = din("lam_q", [128]); lk_d = din("lam_k", [128]); sg_d = din("subln_g", [128]); fg_d = din("final_g", [D])
    cos_d = din("rope_cos", [128, L]); sin_d = din("rope_sin", [128, L])
    idf_d = din("ident", [128, 128]); xsw_d = din("xswap", [128, 128]); sgn_d = din("sgn", [128, 1])
    out_d = T(nc.dram_tensor("out", [L, D], F32, kind="ExternalOutput").ap(), "out")

    mod_s = dscr("mod_s", [2, 9 * D], F32)
    x1_s = dscr("x1_s", [L + LC, D], F32)
    x2_s = dscr("x2_s", [L, D], F32)
    uT_s = dscr("uT_s", [512, LX], BF16)
    qT_s = dscr("qT_s", [512, L], BF16)
    kT_s = dscr("kT_s", [512, LK], BF16)
    v_s = dscr("v_s", [LK, 512], BF16)
    gT_s = dscr("gT_s", [512, L], BF16)
    sgT_s = dscr("sgT_s", [512, L], BF16)

    BIG = 212000
    DRAM_LIST.append(out_d)
    big = nc.alloc_sbuf_tensor("big", [128, BIG], U8)
    ps01 = nc.alloc_psum_tensor("ps01", [128, 1024], F32)
    pst = [ps01[:, 0:512], ps01[:, 512:1024]] + [nc.alloc_psum_tensor("ps%d" % i, [128, 512], F32)[:, :] for i in range(2, 8)]
    psT2 = ps01[:, :].bitcast(BF16).rearrange("p (k t) -> p k t", t=256)
    cur = [0]

    def sb(shape, dt, name=""):
        esz = 4 if dt == F32 else 2
        n = int(np.prod(shape[1:])) * esz
        v = big[0:shape[0], cur[0]:cur[0] + n].bitcast(dt)
        cur[0] += (n + 63) // 64 * 64
        assert cur[0] <= BIG, ("sbuf overflow", name, cur[0])
        if len(shape) == 3:
            v = v.rearrange("p (a b) -> p a b", b=shape[2])
        elif len(shape) == 4:
            v = v.rearrange("p (a b c) -> p a b c", b=shape[2], c=shape[3])
        return T(v, name)

    PS = [T(pst[i], "ps%d" % i) for i in range(8)]

    idF = sb([128, 128], F32, "idF"); idB = sb([128, 128], BF16, "idB")
    xsw = sb([128, 128], F32, "xsw"); sgn = sb([128, 1], F32, "sgn")
    epsT = sb([128, 1], F32, "eps")
    modT = sb([128, 72, 2], F32, "modT")
    gmL = sb([128, 3, 8], F32, "gmL"); gmC = sb([128, 3, 8], F32, "gmC")
    ngT = sb([128, 3, 8], F32, "ngT")
    stat = sb([128, 16], F32, "stat")
    PERSIST_END = cur[0]

    DRAM_T = set(id(t) for t in DRAM_LIST)

    def dma(eng, out_t, out_ap, in_t, in_ap, **kw):
        reads = [in_t.b] if in_t is not None else []
        key = in_t.b if id(out_t) in DRAM_T else out_t.b
        P.op(eng, lambda e, o=out_ap, i=in_ap, kw=kw: e.dma_start(out=o, in_=i, **kw),
             reads=reads, writes=[out_t.b], dma=True, semkey=key)

    def mm(out_t, out_ap, lhsT_t, lhsT, rhs_t, rhs, start, stop, tile_position=None):
        if tile_position is None:
            P.op("pe", lambda e, o=out_ap, l=lhsT, r=rhs, s=start, t=stop: e.matmul(o, lhsT=l, rhs=r, start=s, stop=t),
                 reads=[lhsT_t.b, rhs_t.b], writes=[out_t.b], part=True)
        else:
            P.op("pe", lambda e, o=out_ap, l=lhsT, r=rhs, s=start, t=stop, tp=tile_position: e.matmul(o, lhsT=l, rhs=r, start=s, stop=t, tile_position=tp),
                 reads=[lhsT_t.b, rhs_t.b], writes=[out_t.b], part=True)

    def act(out_t, out_ap, in_t, in_ap, func, extra_reads=(), extra_writes=(), part=False, **kw):
        P.op("act", lambda e, o=out_ap, i=in_ap, f=func, kw=kw: e.activation(out=o, in_=i, func=f, **kw),
             reads=[in_t.b] + [t.b for t in extra_reads], writes=[out_t.b] + [t.b for t in extra_writes], part=part)

    def ts(eng, out_t, out_ap, in_t, in_ap, s1, s2, op0, op1=None, extra_reads=(), part=False):
        if op1 is None:
            f = lambda e, o=out_ap, i=in_ap: e.tensor_scalar(out=o, in0=i, scalar1=s1, scalar2=None, op0=op0)
        else:
            f = lambda e, o=out_ap, i=in_ap: e.tensor_scalar(out=o, in0=i, scalar1=s1, scalar2=s2, op0=op0, op1=op1)
        P.op(eng, f, reads=[in_t.b] + [t.b for t in extra_reads], writes=[out_t.b], part=part)

    def tt(eng, out_t, out_ap, a_t, a_ap, b_t, b_ap, op, part=False):
        P.op(eng, lambda e, o=out_ap, a=a_ap, b=b_ap: e.tensor_tensor(out=o, in0=a, in1=b, op=op),
             reads=[a_t.b, b_t.b], writes=[out_t.b], part=part)

    def stt(eng, out_t, out_ap, a_t, a_ap, scalar, b_t, b_ap, op0, op1, extra_reads=(), part=False):
        P.op(eng, lambda e, o=out_ap, a=a_ap, b=b_ap: e.scalar_tensor_tensor(out=o, in0=a, scalar=scalar, in1=b, op0=op0, op1=op1),
             reads=[a_t.b, b_t.b] + [t.b for t in extra_reads], writes=[out_t.b], part=part)

    def cp(eng, out_t, out_ap, in_t, in_ap, part=False):
        if eng == "act":
            P.op("act", lambda e, o=out_ap, i=in_ap: e.copy(out=o, in_=i), reads=[in_t.b], writes=[out_t.b], part=part)
        else:
            P.op(eng, lambda e, o=out_ap, i=in_ap: e.tensor_copy(out=o, in_=i), reads=[in_t.b], writes=[out_t.b], part=part)

    def memset(eng, t, ap, val, part=False):
        P.op(eng, lambda e, a=ap: e.memset(a, val), writes=[t.b], part=part)

    def recip(out_t, out_ap, in_t, in_ap):
        P.op("dve", lambda e, o=out_ap, i=in_ap: e.reciprocal(out=o, in_=i), reads=[in_t.b], writes=[out_t.b])

    NC_ = dict(allow_slow_non_contiguous=True)
    dma("sp", idF, idF.ap, None, idf_d)
    dma("sp", xsw, xsw.ap, None, xsw_d)
    dma("sp", sgn, sgn.ap, None, sgn_d)
    dma("sp", ngT, ngT.ap, None, ng_d.rearrange("i (k p) -> p i k", p=128), **NC_)
    cp("dve", idB, idB.ap, idF, idF.ap)
    memset("dve", epsT, epsT.ap, EPS)

    cur[0] = PERSIST_END
    cT = sb([128, 8, 2], F32, "cT"); scT = sb([128, 8, 2], F32, "scT")
    modrow = sb([2, 9 * D], F32, "modrow"); bmodr = sb([2, 9 * D], F32, "bmodr")
    wst = [sb([128, 8, 512], F32, "wst%d" % i) for i in range(2)]
    id2 = idF.ap[0:2, 0:2]
    dma("sp", cT, cT.ap[:, :, 0], None, c_d.rearrange("(k p) -> p k", p=128), **NC_)
    dma("sp", cT, cT.ap[:, :, 1], None, cc_d.rearrange("(k p) -> p k", p=128), **NC_)
    dma("sp", bmodr, bmodr.ap, None, bmod_d.partition_broadcast(2))
    act(scT, scT.ap, cT, cT.ap, AF.Silu)
    for nb in range(18):
        w = wst[nb % 2]
        dma("sp", w, w.ap, None, wmod_d[:, nb * 512:(nb + 1) * 512].rearrange("(k p) n -> p k n", p=128))
        ps = PS[nb % 2]
        for k in range(8):
            mm(ps, ps.ap[0:2, :], scT, scT.ap[:, k, :], w, w.ap[:, k, :], k == 0, k == 7)
        tt("dve", modrow, modrow.ap[:, nb * 512:(nb + 1) * 512], ps, ps.ap[0:2, :], bmodr, bmodr.ap[:, nb * 512:(nb + 1) * 512], ALU.add, part=True)
    dma("sp", mod_s, mod_s.ap, modrow, modrow.ap)
    for j in range(72):
        mm(PS[2], PS[2].ap[:, 2 * j:2 * j + 2], modrow, modrow.ap[:, j * 128:(j + 1) * 128], idF, id2, True, True)
    cp("dve", modT, modT.ap, PS[2], PS[2].ap[:, 0:144].rearrange("p (a b) -> p a b", b=2))
    for i in range(3):
        for (gm, col) in ((gmL, 0), (gmC, 1)):
            stt("dve", gm, gm.ap[:, i, :], modT, modT.ap[:, (3 * i + 1) * 8:(3 * i + 2) * 8, col], 1.0,
                ngT, ngT.ap[:, i, :], ALU.add, ALU.mult, part=True)
    P.barrier()

    def sh_ap(i, col, k):
        return modT.ap[:, 3 * i * 8 + k, col:col + 1]

    def make_front(nslots=2):
        xblk = [sb([128, 2, D], F32, "xblk%d" % i) for i in range(nslots)]
        xs = [sb([128, D], BF16, "xs%d" % i) for i in range(2)]
        junk = sb([128, D], BF16, "junk")
        hT = sb([128, 8, 256], BF16, "hT")
        return dict(xblk=xblk, xs=xs, junk=junk, hT=hT, n=0, hT2=None)

    psTr = T(None, "psTr")

    def front_load(fr, src_t, src_ap):
        n = fr["n"]; fr["n"] += 1
        xb = fr["xblk"][n % len(fr["xblk"])]
        dma("sp", xb, xb.ap, src_t, src_ap.rearrange("(s p) d -> p s d", p=128))
        return xb

    def front(fr, xb, gm, i, col, hT=None):
        for s in range(2):
            xs = fr["xs"][s]
            act(fr["junk"], fr["junk"].ap, xb, xb.ap[:, s, :], AF.Square, accum_out=stat.ap[:, s:s + 1], extra_writes=(stat,))
            act(stat, stat.ap[:, 2 + s:3 + s], stat, stat.ap[:, s:s + 1], AF.Sqrt, extra_reads=(epsT,), scale=1.0 / D, bias=epsT.ap[:, 0:1])
            recip(stat, stat.ap[:, 4 + s:5 + s], stat, stat.ap[:, 2 + s:3 + s])
            ts("dve", xs, xs.ap, xb, xb.ap[:, s, :], stat.ap[:, 4 + s:5 + s], None, ALU.mult, extra_reads=(stat,))
            for k in range(8):
                P.op("pe", lambda e, o=psT2[:, k, s * 128:(s + 1) * 128], a=xs.ap[:, k * 128:(k + 1) * 128]: e.transpose(o, a, idB.ap),
                     reads=[xs.b, idB.b], writes=[psTr.b], part=True)
        if hT is None:
            hT = fr["hT"]
        for k in range(8):
            act(hT, hT.ap[:, k, :], psTr, psT2[:, k, :], AF.Identity, extra_reads=(gm, modT), part=True,
                scale=gm.ap[:, i, k:k + 1], bias=sh_ap(i, col, k))
        return xb

    def ffn_phase(which):
        cur[0] = PERSIST_END
        W1 = sb([128, 8, 2 * FF], BF16, "W1"); W2 = sb([128, NFF, D], BF16, "W2")
        gate = sb([128, D], F32, "gate"); gatec = sb([128, D], F32, "gatec")
        fgb = gatec
        fr = make_front()
        hTs = [fr["hT"], sb([128, 8, 256], BF16, "hTb")]
        GT = sb([128, NFF, 256], BF16, "GT")
        sg = [sb([128, 256], F32, "sg%d" % i) for i in range(2)]
        ytmp = [sb([128, 512], F32, "ytmp%d" % i) for i in range(2)]
        stage_off = cur[0]
        stg = [sb([128, 1408], F32, "stg%d" % i) for i in range(2)]
        gi = 2 if which == 0 else 8
        dma("sp", gate, gate.ap, mod_s, mod_s.ap[0:1, gi * D:(gi + 1) * D].partition_broadcast(128).rearrange("p a d -> p (a d)"))
        ts("dve", gate, gate.ap, gate, gate.ap, 0.5, None, ALU.mult)
        if which == 0:
            dma("sp", gatec, gatec.ap, mod_s, mod_s.ap[1:2, 2 * D:3 * D].partition_broadcast(128).rearrange("p a d -> p (a d)"))
            ts("dve", gatec, gatec.ap, gatec, gatec.ap, 0.5, None, ALU.mult)
        else:
            dma("sp", fgb, fgb.ap, None, fg_d.partition_broadcast(128))
        engs = ("dve", "pool", "act")
        n = 0
        for k in range(8):
            for hh in range(4):
                st = stg[n % 2]
                dma("sp", st, st.ap, None, fw1_d[which, k * 128:(k + 1) * 128, hh * 1408:(hh + 1) * 1408])
                cp(engs[n % 3], W1, W1.ap[:, k, hh * 1408:(hh + 1) * 1408], st, st.ap, part=True)
                n += 1
        for m2 in range(NFF):
            st = stg[n % 2]
            dma("sp", st, st.ap[:, 0:D], None, fw2_d[which, m2 * 128:(m2 + 1) * 128, :])
            cp(engs[n % 3], W2, W2.ap[:, m2, :], st, st.ap[:, 0:D], part=True)
            n += 1
        if which == 0:
            blocks = [(None, x_d, x1_s, b * 256, b * 256, gmL, 0, gate) for b in range(L // 256)]
            blocks.append((None, ctx_d, x1_s, 0, L, gmC, 1, gatec))
        else:
            blocks = [(x2_s, x2_s.ap, out_d, b * 256, b * 256, gmL, 0, gate) for b in range(L // 256)]
        ii = 0 if which == 0 else 2
        xbs = {0: front_load(fr, blocks[0][0], blocks[0][1][blocks[0][3]:blocks[0][3] + 256, :])}
        for bi, (src_t, src, dst_t, r0, w0, gm, col, gt) in enumerate(blocks):
            if bi + 1 < len(blocks):
                nb_ = blocks[bi + 1]
                xbs[bi + 1] = front_load(fr, nb_[0], nb_[1][nb_[3]:nb_[3] + 256, :])
            xb = xbs.pop(bi)
            if bi == 0:
                front(fr, xb, gm, ii, col, hTs[0])
            hT = hTs[bi % 2]
            for m in range(NFF):
                pg = PS[2 + 2 * (m % 2)]; pu = PS[3 + 2 * (m % 2)]
                for k in range(8):
                    mm(pg, pg.ap[:, 0:256], W1, W1.ap[:, k, m * 128:(m + 1) * 128], hT, hT.ap[:, k, :], k == 0, k == 7)
                for k in range(8):
                    mm(pu, pu.ap[:, 0:256], W1, W1.ap[:, k, FF + m * 128:FF + (m + 1) * 128], hT, hT.ap[:, k, :], k == 0, k == 7)
                s_ = sg[m % 2]
                act(s_, s_.ap, pg, pg.ap[:, 0:256], AF.Silu)
                tt("dve", GT, GT.ap[:, m, :], s_, s_.ap, pu, pu.ap[:, 0:256], ALU.mult, part=True)
            if bi + 1 < len(blocks):
                nb_ = blocks[bi + 1]
                front(fr, xbs[bi + 1], nb_[5], ii, nb_[6], hTs[(bi + 1) % 2])
            q = 0
            for s in range(2):
                for nh in range(2):
                    py = PS[6 + (q % 2)]
                    for m in range(NFF):
                        mm(py, py.ap, GT, GT.ap[:, m, s * 128:(s + 1) * 128], W2, W2.ap[:, m, nh * 512:(nh + 1) * 512], m == 0, m == NFF - 1)
                    yt = ytmp[q % 2]
                    tt("dve", yt, yt.ap, py, py.ap, gt, gt.ap[:, nh * 512:(nh + 1) * 512], ALU.mult)
                    tt("pool", xb, xb.ap[:, s, nh * 512:(nh + 1) * 512], xb, xb.ap[:, s, nh * 512:(nh + 1) * 512], yt, yt.ap, ALU.add)
                    q += 1
            if which == 1:
                for s in range(2):
                    act(fr["junk"], fr["junk"].ap, xb, xb.ap[:, s, :], AF.Square, accum_out=stat.ap[:, 8 + s:9 + s], extra_writes=(stat,))
                    act(stat, stat.ap[:, 10 + s:11 + s], stat, stat.ap[:, 8 + s:9 + s], AF.Sqrt, extra_reads=(epsT,), scale=1.0 / D, bias=epsT.ap[:, 0:1])
                    recip(stat, stat.ap[:, 12 + s:13 + s], stat, stat.ap[:, 10 + s:11 + s])
                    stt("dve", xb, xb.ap[:, s, :], xb, xb.ap[:, s, :], stat.ap[:, 12 + s:13 + s], fgb, fgb.ap, ALU.mult, ALU.mult, extra_reads=(stat,))
            dma("sp", dst_t, dst_t.ap[w0:w0 + 256, :].rearrange("(s p) d -> p s d", p=128), xb, xb.ap)
        P.barrier()

    def inproj_phase():
        cur[0] = PERSIST_END
        Win = sb([128, 8, 3072], BF16, "Win")
        fr = make_front()
        stg = [sb([128, 3072], F32, "stgB%d" % i) for i in range(2)]
        cosb = [sb([128, 256], F32, "cos%d" % i) for i in range(2)]
        sinb = [sb([128, 256], F32, "sin%d" % i) for i in range(2)]
        t1 = [sb([128, 256], F32, "t1_%d" % i) for i in range(2)]
        t2 = [sb([128, 256], F32, "t2_%d" % i) for i in range(2)]
        uo = [sb([128, 4, 256], BF16, "uo%d" % i) for i in range(2)]
        qo = [sb([128, 4, 256], BF16, "qo%d" % i) for i in range(2)]
        ko = [sb([128, 4, 256], BF16, "ko%d" % i) for i in range(2)]
        vo = [sb([128, 2, 512], BF16, "vo%d" % i) for i in range(2)]
        engs = ("dve", "pool", "act")
        for k in range(8):
            st = stg[k % 2]
            dma("sp", st, st.ap, None, win_d[k * 128:(k + 1) * 128, :])
            cp(engs[k % 3], Win, Win.ap[:, k, :], st, st.ap, part=True)
        nblk = L // 256

        def pre(b):
            r0_ = L if b == nblk else b * 256
            xb_ = front_load(fr, x1_s, x1_s.ap[r0_:r0_ + 256, :])
            if b != nblk:
                dma("sp", cosb[b % 2], cosb[b % 2].ap, None, cos_d[:, b * 256:(b + 1) * 256])
                dma("sp", sinb[b % 2], sinb[b % 2].ap, None, sin_d[:, b * 256:(b + 1) * 256])
            return xb_
        xbs = {0: pre(0)}
        for b in range(nblk + 1):
            isctx = b == nblk
            r0 = L if isctx else b * 256
            gm = gmC if isctx else gmL
            if b + 1 <= nblk:
                xbs[b + 1] = pre(b + 1)
            front(fr, xbs.pop(b), gm, 1, 1 if isctx else 0)
            hT = fr["hT"]
            sl = b % 2
            pcount = 0
            for j in range(4):
                ps = PS[2 + (pcount % 4)]; pcount += 1
                for k in range(8):
                    mm(ps, ps.ap[:, 0:256], Win, Win.ap[:, k, j * 128:(j + 1) * 128], hT, hT.ap[:, k, :], k == 0, k == 7)
                cp("act", uo[sl], uo[sl].ap[:, j, :], ps, ps.ap[:, 0:256], part=True)
            if isctx:
                dma("sp", uT_s, uT_s.ap[:, 0:LC].rearrange("(j p) t -> p j t", p=128), uo[sl], uo[sl].ap)
                dma("sp", uT_s, uT_s.ap[:, LC + L:LX].rearrange("(j p) t -> p j t", p=128), uo[sl], uo[sl].ap)
            else:
                dma("sp", uT_s, uT_s.ap[:, LC + b * 256:LC + (b + 1) * 256].rearrange("(j p) t -> p j t", p=128), uo[sl], uo[sl].ap)
            for (base, swb, ot, dst, isq) in ((512, 2048, qo, qT_s, True), (1024, 2560, ko, kT_s, False)):
                if isq and isctx:
                    continue
                for j in range(4):
                    pa = PS[2 + (pcount % 4)]; pcount += 1
                    for k in range(8):
                        mm(pa, pa.ap[:, 0:256], Win, Win.ap[:, k, base + j * 128:base + (j + 1) * 128], hT, hT.ap[:, k, :], k == 0, k == 7)
                    if isctx:
                        cp("act", ot[sl], ot[sl].ap[:, j, :], pa, pa.ap[:, 0:256], part=True)
                        continue
                    pb = PS[2 + (pcount % 4)]; pcount += 1
                    for k in range(8):
                        mm(pb, pb.ap[:, 0:256], Win, Win.ap[:, k, swb + j * 128:swb + (j + 1) * 128], hT, hT.ap[:, k, :], k == 0, k == 7)
                    a1 = t1[j % 2]; a2 = t2[j % 2]
                    tt("dve", a1, a1.ap, pa, pa.ap[:, 0:256], cosb[sl], cosb[sl].ap, ALU.mult)
                    tt("dve", a2, a2.ap, pb, pb.ap[:, 0:256], sinb[sl], sinb[sl].ap, ALU.mult)
                    tt("pool", ot[sl], ot[sl].ap[:, j, :], a1, a1.ap, a2, a2.ap, ALU.add, part=True)
                c0 = L if isctx else b * 256
                dma("sp", dst, dst.ap[:, c0:c0 + 256].rearrange("(j p) t -> p j t", p=128), ot[sl], ot[sl].ap)
            for s in range(2):
                ps = PS[6 + s]
                for k in range(8):
                    mm(ps, ps.ap, hT, hT.ap[:, k, s * 128:(s + 1) * 128], Win, Win.ap[:, k, 1536:2048], k == 0, k == 7)
                cp("act", vo[sl], vo[sl].ap[:, s, :], ps, ps.ap, part=True)
            dma("sp", v_s, v_s.ap[r0:r0 + 256, :].rearrange("(s p) d -> p s d", p=128), vo[sl], vo[sl].ap)
        P.barrier()

    def s5_phase():
        cur[0] = PERSIST_END
        NR = 64
        Bpad = sb([128, NR, 128], BF16, "Bpad"); Cpad = sb([128, NR, 128], BF16, "Cpad")
        PW = sb([128, 9, 2, NR], F32, "PW")
        dsk = sb([128, 4], F32, "dsk")
        setup_off = cur[0]
        arow = sb([64, 2, 128], F32, "arow")
        prm = sb([128, 16, NR], F32, "prm")
        Ball = sb([64, 2, NR * 16], F32, "Ball")
        Bbar = sb([64, 2, NR * 16], F32, "Bbar")
        btmp = sb([64, 2, NR * 16], F32, "btmp")
        BP = [sb([64, 2, 128], F32, "BP%d" % i) for i in range(2)]
        Call = sb([16, NR, 128], F32, "Call")
        dma("sp", arow, arow.ap[:, 0, 0:64], None, are_d); dma("sp", arow, arow.ap[:, 0, 64:128], None, are_d)
        dma("sp", arow, arow.ap[:, 1, 0:64], None, aim_d); dma("sp", arow, arow.ap[:, 1, 64:128], None, aim_d)
        dma("sp", prm, prm.ap[:, 2, :], None, ldt_d.partition_broadcast(128))
        dma("sp", dsk, dsk.ap, None, dsk_d.rearrange("(c p) -> p c", p=128), **NC_)
        dma("sp", Ball, Ball.ap[:, 0, :].rearrange("p (r h) -> p r h", h=16), None, bre_d.rearrange("r p h -> p r h"))
        dma("sp", Ball, Ball.ap[:, 1, :].rearrange("p (r h) -> p r h", h=16), None, bim_d.rearrange("r p h -> p r h"))
        dma("sp", Call, Call.ap[:, :, 0:64], None, cre_d.rearrange("r h p -> h r p"))
        dma("sp", Call, Call.ap[:, :, 64:128], None, cim_d.rearrange("r h p -> h r p"))
        for i in range(2):
            mm(PS[0], PS[0].ap[:, i * 64:(i + 1) * 64], arow, arow.ap[:, i, :], idF, idF.ap[0:64, 0:64], True, True)
        cp("dve", prm, prm.ap[:, 0:2, :], PS[0], PS[0].ap[:, 0:128].rearrange("p (a b) -> p a b", b=64))
        Pm = lambda i: prm.ap[:, i, :]
        ARE, AIM, LDT, DT, XR, ANG, MAG, T0, T1_, COS, SIN, AL, BE = range(13)

        def e_tt(o, a, b, op):
            tt("dve", prm, Pm(o), prm, Pm(a), prm, Pm(b), op)

        def e_ts(o, a, s1, s2, op0, op1=None):
            ts("dve", prm, Pm(o), prm, Pm(a), s1, s2, op0, op1)

        act(prm, Pm(DT), prm, Pm(LDT), AF.Exp)
        e_tt(XR, DT, ARE, ALU.mult)
        e_tt(ANG, DT, AIM, ALU.mult)
        e_ts(MAG, XR, 1.0 / 7.0, 1.0, ALU.mult, ALU.add)
        for kk in (6.0, 5.0, 4.0, 3.0, 2.0, 1.0):
            e_tt(MAG, MAG, XR, ALU.mult)
            e_ts(MAG, MAG, 1.0 / kk, 1.0, ALU.mult, ALU.add)
        TWO_PI = float(2 * np.pi)
        for (dst, shift) in ((SIN, 0.0), (COS, float(np.pi / 2))):
            e_ts(T0, ANG, shift, None, ALU.add)
            e_ts(T1_, T0, 1.0 / TWO_PI, 12582912.0, ALU.mult, ALU.add)
            e_ts(T1_, T1_, -12582912.0, None, ALU.add)
            stt("dve", prm, Pm(T0), prm, Pm(T1_), -TWO_PI, prm, Pm(T0), ALU.mult, ALU.add)
            act(prm, Pm(dst), prm, Pm(T0), AF.Sin)
        e_tt(AL, MAG, COS, ALU.mult)
        e_tt(BE, MAG, SIN, ALU.mult)
        ZR, DEN, CR, CI = 13, 14, 15, 3
        e_ts(ZR, AL, -1.0, None, ALU.add)
        e_tt(DEN, ARE, ARE, ALU.mult); e_tt(T0, AIM, AIM, ALU.mult); e_tt(DEN, DEN, T0, ALU.add)
        recip(prm, Pm(DEN), prm, Pm(DEN))
        e_tt(CR, ZR, ARE, ALU.mult); e_tt(T0, BE, AIM, ALU.mult); e_tt(CR, CR, T0, ALU.add); e_tt(CR, CR, DEN, ALU.mult)
        e_tt(CI, BE, ARE, ALU.mult); e_tt(T0, ZR, AIM, ALU.mult); e_tt(CI, CI, T0, ALU.subtract); e_tt(CI, CI, DEN, ALU.mult)
        cp("dve", PW, PW.ap[:, 0, 0, :], prm, Pm(AL))
        ts("dve", PW, PW.ap[:, 0, 1, :], prm, Pm(BE), sgn.ap[:, 0:1], None, ALU.mult, extra_reads=(sgn,))
        cur_pow = 1
        kidx = 1
        while kidx < 9:
            e_tt(T0, AL, AL, ALU.mult); e_tt(T1_, BE, BE, ALU.mult)
            e_tt(BE, AL, BE, ALU.mult); e_ts(BE, BE, 2.0, None, ALU.mult)
            e_tt(AL, T0, T1_, ALU.subtract)
            cur_pow *= 2
            if cur_pow >= TS:
                cp("dve", PW, PW.ap[:, kidx, 0, :], prm, Pm(AL))
                ts("dve", PW, PW.ap[:, kidx, 1, :], prm, Pm(BE), sgn.ap[:, 0:1], None, ALU.mult, extra_reads=(sgn,))
                kidx += 1
        crb = prm.ap[0:64, CR, :].unsqueeze(2).broadcast_to([64, NR, 16])
        cib = prm.ap[0:64, CI, :].unsqueeze(2).broadcast_to([64, NR, 16])
        B3 = lambda t, i: t.ap[:, i, :].rearrange("p (r h) -> p r h", h=16)
        tt("dve", Bbar, B3(Bbar, 0), Ball, B3(Ball, 0), prm, crb, ALU.mult)
        tt("dve", btmp, B3(btmp, 0), Ball, B3(Ball, 1), prm, cib, ALU.mult)
        tt("dve", Bbar, B3(Bbar, 0), Bbar, B3(Bbar, 0), btmp, B3(btmp, 0), ALU.subtract)
        tt("dve", Bbar, B3(Bbar, 1), Ball, B3(Ball, 1), prm, crb, ALU.mult)
        tt("dve", btmp, B3(btmp, 1), Ball, B3(Ball, 0), prm, cib, ALU.mult)
        tt("dve", Bbar, B3(Bbar, 1), Bbar, B3(Bbar, 1), btmp, B3(btmp, 1), ALU.add)
        ts("dve", Call, Call.ap[:, :, 64:128], Call, Call.ap[:, :, 64:128], -1.0, None, ALU.mult)
        memset("pool", Cpad, Cpad.ap, 0.0)
        for r in range(NR):
            gp = r % 8
            bp = BP[r % 2]
            memset("pool", bp, bp.ap, 0.0)
            for i in range(2):
                cp("pool", bp, bp.ap[:, i, gp * 16:(gp + 1) * 16], Bbar, Bbar.ap[:, i, r * 16:(r + 1) * 16], part=True)
            ps = PS[1 + (r % 2)]
            for i in range(2):
                mm(ps, ps.ap[:, i * 64:(i + 1) * 64], bp, bp.ap[:, i, :], idF, idF.ap[0:64, 0:64], True, True)
            cp("act", Bpad, Bpad.ap[:, r, :], ps, ps.ap[:, 0:128], part=True)
            pc = PS[3 + (r % 2)]
            mm(pc, pc.ap[:, 0:16], Call, Call.ap[:, r, :], idF, idF.ap[0:16, 0:16], True, True)
            cp("dve", Cpad, Cpad.ap[:, r, gp * 16:(gp + 1) * 16], pc, pc.ap[:, 0:16], part=True)
        P.barrier()
        cur[0] = setup_off
        utok = sb([128, LX], BF16, "utok")
        uJ = sb([128, TS, NCH], BF16, "uJ")
        yac = sb([128, TS, NCH], F32, "yac")
        Am = [sb([128, 128], F32, "Am%d" % i) for i in range(8)]
        Ap = [sb([128, 128], F32, "Ap%d" % i) for i in range(2)]
        hA = [sb([128, 2, NS], F32, "hA%d" % i) for i in range(4)]
        hB = [sb([128, 2, NS], F32, "hB%d" % i) for i in range(4)]
        hb16 = [[sb([128, 2, NS], BF16, "hb%d_%d" % (i, j)) for j in range(2)] for i in range(4)]
        HcU = [sb([128, 2, NS], F32, "Hc%d" % i) for i in range(4)]
        Hc = [T(HcU[g // 2].ap[:, g % 2, :], "Hcr%d" % g) for g in range(8)]
        gel = [sb([128, L // 2], F32, "gel%d" % i) for i in range(2)]
        gout = sb([128, L // 2], BF16, "gout")
        psH = [T(PS[u].ap[:, 0:2 * NS].rearrange("p (a b) -> p a b", b=NS), "psH%d" % u) for u in range(4)]
        psC = [T(PS[4 + i].ap[:, 0:NS], "psC%d" % i) for i in range(2)]
        psY = [T(PS[6 + i].ap[:, 0:NS], "psY%d" % i) for i in range(2)]
        evn = [0]

        def evac(out_t, out_ap, in_t, in_ap):
            e = ("act", "dve")[evn[0] % 2]; evn[0] += 1
            cp(e, out_t, out_ap, in_t, in_ap)

        def build_A(dst, k, r):
            ts("dve", dst, dst.ap, idF, idF.ap, PW.ap[:, k, 0, r:r + 1], None, ALU.mult, extra_reads=(PW,))
            stt("dve", dst, dst.ap, xsw, xsw.ap, PW.ap[:, k, 1, r:r + 1], dst, dst.ap, ALU.mult, ALU.add, extra_reads=(PW,))

        def readout(j, st, rows, c0, ycopy):
            py = psY[st % 2]
            for g in range(8):
                hb = hb16[g // 2][st % 2]
                mm(py, py.ap, Cpad, Cpad.ap[:, rows[g], :], hb, hb.ap[:, g % 2, :], g == 0, g == 7)
            if ycopy:
                evac(yac, yac.ap[:, j, c0:c0 + NS], py, py.ap)
            else:
                tt("dve", yac, yac.ap[:, j, c0:c0 + NS], yac, yac.ap[:, j, c0:c0 + NS], py, py.ap, ALU.add)

        for gc in range(4):
            dma("sp", utok, utok.ap, uT_s, uT_s.ap[gc * 128:(gc + 1) * 128, :])
            cp("pool", uJ, uJ.ap, utok, utok.ap.rearrange("p (c j) -> p j c", j=TS))
            for dr in range(2):
                c0 = 0 if dr == 0 else C0B
                rows = [dr * 32 + gc * 8 + g for g in range(8)]
                ycopy = (dr == 0)
                for g in range(8):
                    build_A(Am[g], 0, rows[g])
                jorder = list(range(TS)) if dr == 0 else list(range(TS - 1, -1, -1))
                for pss in range(2):
                    if pss == 1:
                        for g in range(8):
                            H = Hc[g]
                            k = 1
                            sft = 1
                            while sft < NS:
                                ap_ = Ap[(g + k) % 2]
                                build_A(ap_, k, rows[g])
                                pc = psC[(g + k) % 2]
                                if dr == 0:
                                    mm(pc, pc.ap[:, 0:NS - sft], ap_, ap_.ap, H, H.ap[:, 0:NS - sft], True, True)
                                    tt("dve", H, H.ap[:, sft:NS], H, H.ap[:, sft:NS], pc, pc.ap[:, 0:NS - sft], ALU.add)
                                else:
                                    mm(pc, pc.ap[:, 0:NS - sft], ap_, ap_.ap, H, H.ap[:, sft:NS], True, True)
                                    tt("dve", H, H.ap[:, 0:NS - sft], H, H.ap[:, 0:NS - sft], pc, pc.ap[:, 0:NS - sft], ALU.add)
                                sft *= 2
                                k += 1
                        for u in range(4):
                            h0 = hA[u]
                            memset("pool", h0, h0.ap, 0.0)
                            for r2 in range(2):
                                H = Hc[2 * u + r2]
                                if dr == 0:
                                    cp("pool", h0, h0.ap[:, r2, 1:NS], H, H.ap[:, 0:NS - 1], part=True)
                                else:
                                    cp("pool", h0, h0.ap[:, r2, 0:NS - 1], H, H.ap[:, 1:NS], part=True)
                    hcur = list(hA); hnxt = list(hB)
                    pend = None
                    for st, j in enumerate(jorder):
                        for u in range(4):
                            first = (pss == 0 and st == 0)
                            for r2 in range(2):
                                g = 2 * u + r2
                                r = rows[g]
                                if not first:
                                    mm(psH[u], psH[u].ap[:, r2, :], Am[g], Am[g].ap, hcur[u], hcur[u].ap[:, r2, :], True, False)
                                mm(psH[u], psH[u].ap[:, r2, :], Bpad, Bpad.ap[:, r, :], uJ, uJ.ap[:, j, c0:c0 + NS], first, True)
                            if pss == 0 and st == TS - 1:
                                e = ("act", "dve")[evn[0] % 2]; evn[0] += 1
                                if e == "act":
                                    P.op("act", lambda e_, o=HcU[u].ap, i=psH[u].ap: e_.copy(out=o, in_=i), reads=[psH[u].b], writes=[Hc[2 * u].b, Hc[2 * u + 1].b])
                                else:
                                    P.op("dve", lambda e_, o=HcU[u].ap, i=psH[u].ap: e_.tensor_copy(out=o, in_=i), reads=[psH[u].b], writes=[Hc[2 * u].b, Hc[2 * u + 1].b])
                            else:
                                evac(hnxt[u], hnxt[u].ap, psH[u], psH[u].ap)
                            if pss == 1:
                                hb = hb16[u][st % 2]
                                cp("pool", hb, hb.ap, hnxt[u], hnxt[u].ap)
                        if pss == 1:
                            if pend is not None:
                                readout(pend[0], pend[1], rows, c0, ycopy)
                            pend = (j, st)
                        hcur, hnxt = hnxt, hcur
                    if pss == 1:
                        readout(pend[0], pend[1], rows, c0, ycopy)
            LH = L // 2
            for hf in range(2):
                CL = slice(C0B + hf * (LH // TS), C0B + (hf + 1) * (LH // TS))
                g0 = gel[0]; g1 = gel[1]
                v3 = lambda t: t.ap.rearrange("p (c j) -> p j c", j=TS)
                stt("dve", g0, v3(g0), uJ, uJ.ap[:, :, CL], dsk.ap[:, gc:gc + 1], yac, yac.ap[:, :, CL], ALU.mult, ALU.add, extra_reads=(dsk,))
                tt("pool", g1, g1.ap, g0, g0.ap, g0, g0.ap, ALU.mult)
                ts("dve", g1, g1.ap, g1, g1.ap, 0.044715, 1.0, ALU.mult, ALU.add)
                tt("pool", g1, g1.ap, g1, g1.ap, g0, g0.ap, ALU.mult)
                act(g1, g1.ap, g1, g1.ap, AF.Sigmoid, scale=1.5957691216057308)
                tt("dve", gout, gout.ap, g0, g0.ap, g1, g1.ap, ALU.mult)
                dma("sp", gT_s, gT_s.ap[gc * 128:(gc + 1) * 128, hf * LH:(hf + 1) * LH], gout, gout.ap)
        P.barrier()

    def emit_readout(j, st, rows, hb16, psY, Cpad, yac, c0, ycopy, evac):
        py = psY[st % 2]
        for g in range(4):
            hb = hb16[g][st % 2]
            mm(py, py.ap, Cpad, Cpad.ap[:, rows[g], :], hb, hb.ap, g == 0, g == 3)
        NSl = py.ap.shape[1]
        if ycopy:
            evac(yac, yac.ap[:, j, c0:c0 + NSl], py, py.ap)
        else:
            tt("dve", yac, yac.ap[:, j, c0:c0 + NSl], yac, yac.ap[:, j, c0:c0 + NSl], py, py.ap, ALU.add)

    def glu_phase():
        cur[0] = PERSIST_END
        Wg = sb([128, 4, 512], BF16, "Wg"); wst_ = sb([128, 4, 512], F32, "wgst")
        bg = sb([128, 4], F32, "bg")
        gb = [sb([128, 4, 512], BF16, "gb%d" % i) for i in range(2)]
        so = [sb([128, 4, 512], BF16, "so%d" % i) for i in range(2)]
        sgm = [sb([128, 512], F32, "sgm%d" % i) for i in range(2)]
        dma("sp", wst_, wst_.ap, None, wglu_d.rearrange("(k p) n -> p k n", p=128))
        cp("dve", Wg, Wg.ap, wst_, wst_.ap)
        dma("sp", bg, bg.ap, None, bglu_d.rearrange("(c p) -> p c", p=128), **NC_)
        def gload(tb):
            dma("sp", gb[tb % 2], gb[tb % 2].ap, gT_s, gT_s.ap[:, tb * 512:(tb + 1) * 512].rearrange("(k p) t -> p k t", p=128))
        gload(0)
        for tb in range(L // 512):
            g_ = gb[tb % 2]; s_ = so[tb % 2]
            if tb + 1 < L // 512:
                gload(tb + 1)
            for m in range(4):
                ps = PS[m % 2]
                for k in range(4):
                    mm(ps, ps.ap, Wg, Wg.ap[:, k, m * 128:(m + 1) * 128], g_, g_.ap[:, k, :], k == 0, k == 3)
                sm = sgm[m % 2]
                act(sm, sm.ap, ps, ps.ap, AF.Sigmoid, extra_reads=(bg,), bias=bg.ap[:, m:m + 1])
                tt("dve", s_, s_.ap[:, m, :], g_, g_.ap[:, m, :], sm, sm.ap, ALU.mult, part=True)
            dma("sp", sgT_s, sgT_s.ap[:, tb * 512:(tb + 1) * 512].rearrange("(k p) t -> p k t", p=128), s_, s_.ap)
        P.barrier()

    def attn_phase():
        cur[0] = PERSIST_END
        kT = sb([128, 4, LK], BF16, "kT")
        vA = sb([128, NKB, 512], BF16, "vA")
        Wo = sb([128, 8, D], BF16, "Wo")
        gate5 = sb([128, D], F32, "gate5")
        gsubT = sb([128, 1], F32, "gsubT")
        onesB = sb([128, 128], BF16, "onesB"); onesF = sb([128, 128], F32, "onesF")
        lam = sb([128, 8], F32, "lam"); lqk = sb([128, 2, 128], F32, "lqk")
        qT = [sb([128, 4, 512], BF16, "qT%d" % i) for i in range(2)]
        catT = sb([128, 8, 512], BF16, "catT")
        eT = [sb([128, 512], BF16, "eT%d" % i) for i in range(6)]
        wq = sb([128, 512], F32, "wq"); onesQ = sb([128, 128], F32, "onesQ"); onesQ1 = sb([128, 128], F32, "onesQ1")
        w0 = sb([128, 512], F32, "w0"); w1 = sb([128, 512], F32, "w1"); w2 = sb([128, 512], F32, "w2")
        w3 = sb([128, 512], F32, "w3"); w4 = sb([128, 512], F32, "w4"); w5 = sb([128, 512], F32, "w5")
        x1t = [sb([128, D], F32, "x1t%d" % i) for i in range(2)]
        yall = sb([128, D], F32, "stgD0")
        ytmp = [T(yall.ap[:, 0:512], "ytmpD0"), T(yall.ap[:, 512:1024], "ytmpD1")]
        stg = [yall]
        dma("sp", gate5, gate5.ap, mod_s, mod_s.ap[0:1, 5 * D:6 * D].partition_broadcast(128).rearrange("p a d -> p (a d)"))
        dma("sp", gsubT, gsubT.ap, None, sg_d.rearrange("(p o) -> p o", o=1))
        ts("dve", gsubT, gsubT.ap, gsubT, gsubT.ap, 1.0 - LAM_INIT, None, ALU.mult)
        memset("pool", onesB, onesB.ap, 1.0)
        memset("pool", onesF, onesF.ap, 1.0 / 128.0)
        memset("pool", onesQ, onesQ.ap, 0.0)
        memset("pool", onesQ, onesQ.ap[0:64, :], 1.0 / 32.0)
        memset("pool", onesQ1, onesQ1.ap, 0.0)
        memset("pool", onesQ1, onesQ1.ap[64:128, :], 1.0 / 32.0)
        dma("sp", lqk, lqk.ap[:, 0, :], None, lq_d.partition_broadcast(128))
        dma("sp", lqk, lqk.ap[:, 1, :], None, lk_d.partition_broadcast(128))
        tt("dve", lqk, lqk.ap[:, 0, :], lqk, lqk.ap[:, 0, :], lqk, lqk.ap[:, 1, :], ALU.mult)
        P.op("dve", lambda e: e.reduce_sum(out=lam.ap[:, 0:2], in_=lqk.ap[:, 0, :].rearrange("p (a b) -> p a b", b=64), axis=mybir.AxisListType.X),
             reads=[lqk.b], writes=[lam.b])
        act(lam, lam.ap[:, 2:4], lam, lam.ap[:, 0:2], AF.Exp)
        tt("dve", lam, lam.ap[:, 4:5], lam, lam.ap[:, 2:3], lam, lam.ap[:, 3:4], ALU.subtract)
        ts("dve", lam, lam.ap[:, 5:6], lam, lam.ap[:, 4:5], LAM_INIT, -1.0, ALU.add, ALU.mult)
        for m2 in range(8):
            st = stg[0]
            dma("sp", st, st.ap, None, wout_d[m2 * 128:(m2 + 1) * 128, :])
            cp("dve", Wo, Wo.ap[:, m2, :], st, st.ap, part=True)
        for h in range(4):
            dma("sp", kT, kT.ap[:, h, :], kT_s, kT_s.ap[h * 128:(h + 1) * 128, :])
        for k0 in range(0, NKB, 6):
            k1 = min(NKB, k0 + 6)
            dma("sp", vA, vA.ap[:, k0:k1, :], v_s, v_s.ap[k0 * 128:k1 * 128, :].rearrange("(kb p) e -> p kb e", p=128))
        psS = [PS[0], PS[1], PS[2]]
        psYd = PS[7]

        def banks(h, c):
            pD = PS[6] if h % 2 == 0 else PS[7]
            if c == 1:
                return PS[5], pD
            return (PS[3] if h % 2 == 0 else PS[4]), pD
        scnt = 0

        def qload(qb):
            dma("sp", qT[qb % 2], qT[qb % 2].ap, qT_s, qT_s.ap[:, qb * 512:(qb + 1) * 512].rearrange("(h p) t -> p h t", p=128))
        qload(0)
        for qb in range(L // 512):
            q_ = qT[qb % 2]
            if qb + 1 < L // 512:
                qload(qb + 1)
            dma("sp", catT, catT.ap[:, 0:4, :], sgT_s, sgT_s.ap[:, qb * 512:(qb + 1) * 512].rearrange("(k p) t -> p k t", p=128))
            its = [(h, c, kb) for h in range(4) for c in range(2) for kb in range(NKB)]

            def s_mm(i):
                h, c, kb = its[i]
                pl = slice(c * 64, (c + 1) * 64)
                pS = psS[(scnt0 + i) % 3]
                mm(pS, pS.ap, kT, kT.ap[pl, h, kb * 128:(kb + 1) * 128], q_, q_.ap[pl, h, :], True, True)
            scnt0 = scnt

            def warm(n):
                for _ in range(n):
                    mm(PS[4], PS[4].ap, onesB, onesB.ap, kT, kT.ap[:, 0, 0:512], True, True)
            warm(WARM_N)
            s_mm(0)
            s_mm(1)
            pending = []
            pendD = []
            for i, (h, c, kb) in enumerate(its):
                for pd in list(pending):
                    pd[0] -= 1
                    if pd[0] <= 0:
                        pd[1]()
                        pending.remove(pd)
                pS = psS[(scnt0 + i) % 3]; e_ = eT[(scnt0 + i) % 6]
                act(e_, e_.ap, pS, pS.ap, AF.Exp, scale=0.125)
                if i + 2 < len(its):
                    s_mm(i + 2)
                if kb % BURST_EVERY == BURST_EVERY // 2:
                    wb = PS[4] if h % 2 == 0 else PS[3]
                    for _ in range(BURST_N):
                        mm(wb, wb.ap, onesB, onesB.ap, kT, kT.ap[:, 0, 0:512], True, True)
                pO, pD = banks(h, c)
                mm(pO, pO.ap, vA, vA.ap[:, kb, h * 128:(h + 1) * 128], e_, e_.ap, kb == 0, kb == NKB - 1)
                pendD.append((kb, e_))
                if len(pendD) == 2 or kb == NKB - 1:
                    for (kbj, ej) in pendD:
                        j = 2 * c + kbj % 2
                        mm(pD, pD.ap[32 * j:32 * j + 32, :], onesB, onesB.ap[:, 32 * j:32 * j + 32], ej, ej.ap,
                           kbj < 2, kbj + 2 >= NKB, tile_position=(0, 32 * j))
                    pendD.clear()
                if not (c == 1 and kb == NKB - 1):
                    continue
                pO0, pD0 = banks(h, 0)
                pO1, pD1 = banks(h, 1)
                cp("dve", w4, w4.ap, pO0, pO0.ap)
                cp("act", wq, wq.ap, pD0, pD0.ap)
                for (selx, wx) in ((onesQ, w0), (onesQ1, w1)):
                    mm(pD0, pD0.ap, selx, selx.ap, wq, wq.ap, True, True)
                    cp("act", wx, wx.ap, pD0, pD0.ap)
                cp("dve", w5, w5.ap, pO1, pO1.ap)
                recip(w0, w0.ap, w0, w0.ap)
                tt("dve", w4, w4.ap, w4, w4.ap, w0, w0.ap, ALU.mult)
                recip(w1, w1.ap, w1, w1.ap)
                tt("dve", w5, w5.ap, w5, w5.ap, w1, w1.ap, ALU.mult)
                stt("dve", w4, w4.ap, w5, w5.ap, lam.ap[:, 5:6], w4, w4.ap, ALU.mult, ALU.add, extra_reads=(lam,))
                tt("pool", w2, w2.ap, w4, w4.ap, w4, w4.ap, ALU.mult)

                def fin(h=h, psM=pO0):
                    mm(psM, psM.ap, onesF, onesF.ap, w2, w2.ap, True, True)
                    act(w3, w3.ap, psM, psM.ap, AF.Ln, extra_reads=(epsT,), bias=epsT.ap[:, 0:1])
                    act(w3, w3.ap, w3, w3.ap, AF.Exp, scale=-0.5)
                    stt("dve", catT, catT.ap[:, 4 + h, :], w4, w4.ap, gsubT.ap[:, 0:1], w3, w3.ap, ALU.mult, ALU.mult, extra_reads=(gsubT,), part=True)
                pending.append([DEFER_N, fin])
            for pd in pending:
                pd[1]()
            pending.clear()
            scnt += len(its)
            for s in range(4):
                xt = x1t[s % 2]
                r0 = qb * 512 + s * 128
                dma("sp", xt, xt.ap, x1_s, x1_s.ap[r0:r0 + 128, :])
                for nh in range(2):
                    for kc in range(8):
                        mm(psYd, psYd.ap, catT, catT.ap[:, kc, s * 128:(s + 1) * 128], Wo, Wo.ap[:, kc, nh * 512:(nh + 1) * 512], kc == 0, kc == 7)
                    yt = ytmp[nh]
                    tt("dve", yt, yt.ap, psYd, psYd.ap, gate5, gate5.ap[:, nh * 512:(nh + 1) * 512], ALU.mult)
                    tt("pool", xt, xt.ap[:, nh * 512:(nh + 1) * 512], xt, xt.ap[:, nh * 512:(nh + 1) * 512], yt, yt.ap, ALU.add)
                dma("sp", x2_s, x2_s.ap[r0:r0 + 128, :], xt, xt.ap)
        P.barrier()

    ffn_phase(0)
    inproj_phase()
    s5_phase()
    glu_phase()
    attn_phase()
    ffn_phase(1)
    P.emit()
    return nc, P


def host_consts(L):
    f32 = np.float32
    rows = L // 64
    row = np.repeat(np.arange(rows, dtype=f32), 64)
    col = np.tile(np.arange(64, dtype=f32), rows)
    inv_freq = np.power(f32(10000.0), -(np.arange(16, dtype=f32) / f32(16))).astype(f32)
    cos = np.zeros((128, L), f32); sin = np.zeros((128, L), f32)
    for c in range(2):
        for ax, pos in enumerate((row, col)):
            ang = (pos[:, None] * inv_freq[None, :]).astype(f32)
            for half in range(2):
                p0 = c * 64 + ax * 32 + half * 16
                cos[p0:p0 + 16, :] = np.cos(ang).T
                sin[p0:p0 + 16, :] = (np.sin(ang).T) * (f32(-1.0) if half == 0 else f32(1.0))
    ident = np.eye(128, dtype=f32)
    xsw = np.zeros((128, 128), f32)
    for k in range(128):
        xsw[k, (k + 64) % 128] = 1.0
    sgn = np.ones((128, 1), f32); sgn[64:] = -1.0
    return dict(rope_cos=cos, rope_sin=sin, ident=ident, xswap=xsw, sgn=sgn)


_CACHE = {}


def make_in_maps(inputs, L, ncores):
    f32 = np.float32
    A = lambda a: np.ascontiguousarray(np.asarray(a, dtype=f32))
    w_in = A(inputs["w_in"])[0]
    perm = np.arange(512) ^ 16
    w_in_ext = np.ascontiguousarray(np.concatenate([w_in, w_in[:, 512 + perm], w_in[:, 1024 + perm]], axis=1))
    shared = dict(
        c_ctx=A(inputs["c_ctx"]), w_mod=A(inputs["w_mod"])[0], b_mod=A(inputs["b_mod"])[0], norm_g=A(inputs["norm_g"])[0],
        ffn_w_in=A(inputs["ffn_w_in"])[0], ffn_w_out=A(inputs["ffn_w_out"])[0], w_in_ext=w_in_ext, w_out=A(inputs["w_out"])[0],
        ssm_a_re=A(inputs["ssm_a_re"])[0].reshape(64, 64), ssm_a_im=A(inputs["ssm_a_im"])[0].reshape(64, 64),
        ssm_log_dt=A(inputs["ssm_log_dt"])[0].reshape(64),
        ssm_b_re=A(inputs["ssm_b_re"])[0].reshape(64, 64, 16), ssm_b_im=A(inputs["ssm_b_im"])[0].reshape(64, 64, 16),
        ssm_c_re=A(inputs["ssm_c_re"])[0].reshape(64, 16, 64), ssm_c_im=A(inputs["ssm_c_im"])[0].reshape(64, 16, 64),
        ssm_d=A(inputs["ssm_d"])[0].reshape(512), w_glu=A(inputs["w_glu"])[0], b_glu=A(inputs["b_glu"])[0],
        lam_q=A(inputs["lam_q"])[0].reshape(128), lam_k=A(inputs["lam_k"])[0].reshape(128),
        subln_g=A(inputs["subln_g"])[0], final_g=A(inputs["final_g"]),
    )
    shared.update(host_consts(L))
    x = A(inputs["x"]); c = A(inputs["c"]); ctx = A(inputs["ctx"])
    maps = []
    for b in range(ncores):
        m = dict(shared)
        m["x"] = np.ascontiguousarray(x[b, :L]); m["c"] = np.ascontiguousarray(c[b]); m["ctx"] = np.ascontiguousarray(ctx[b])
        maps.append(m)
    return maps


def kernel(**inputs):
    x = np.asarray(inputs["x"])
    B, L, _ = x.shape
    if L not in _CACHE:
        _CACHE[L] = build(L)
    nc, _ = _CACHE[L]
    maps = make_in_maps(inputs, L, B)
    res = run_bass_kernel_spmd(nc, maps, core_ids=list(range(B)))
    out = np.stack([np.asarray(r["out"], dtype=np.float32) for r in res.results], axis=0)
    return out
```

```python
import numpy as np
import concourse.bass as bass
import concourse.mybir as mybir
from concourse.bass_utils import run_bass_kernel_spmd

F32 = mybir.dt.float32
BF16 = mybir.dt.bfloat16
U8 = mybir.dt.uint8
AF = mybir.ActivationFunctionType
ALU = mybir.AluOpType

D = 1024
KD = 8
FF = 2816
NFF = 22
LC = 256
TS = 64
WARM_N = 24
BURST_EVERY = 22
BURST_N = 10
HEAD_WARM_N = 10
ATTACH_WAIT = True
DEFER_N = 20
EPS = 1e-6
LAM_INIT = 0.2


class Buf:
    __slots__ = ("name", "writers", "readers", "prev_readers")

    def __init__(self, name=""):
        self.name = name
        self.writers = []
        self.readers = []
        self.prev_readers = []


class Op:
    __slots__ = ("eng", "fn", "deps", "idx", "is_dma", "sem", "sem_val", "signals", "full")

    def __init__(self, eng, fn, is_dma):
        self.eng = eng
        self.fn = fn
        self.deps = []
        self.is_dma = is_dma
        self.sem = None
        self.sem_val = None
        self.signals = False
        self.idx = 0


class Prog:
    ENGS = ("pe", "act", "dve", "pool", "sp")

    def __init__(self, nc):
        self.nc = nc
        self.ops = {e: [] for e in self.ENGS}
        self.all_ops = []
        self.eng_obj = {"pe": nc.tensor, "act": nc.scalar, "dve": nc.vector,
                        "pool": nc.gpsimd, "sp": nc.sync}

    def op(self, eng, fn, reads=(), writes=(), dma=False, part=False, semkey=None):
        o = Op(eng, fn, dma)
        o.idx = len(self.ops[eng])
        o.full = not (dma or part)
        deps = set()
        for b in reads:
            deps.update(b.writers)
        for b in writes:
            cont = (dma or part) and not b.readers
            if cont:
                deps.update(b.prev_readers)
                deps.update(w for w in b.writers if w.full)
            else:
                deps.update(b.writers)
                deps.update(b.readers)
        best = {}
        out = []
        for d in deps:
            if d is o:
                continue
            if d.is_dma:
                out.append(d)
                continue
            if d.eng == "pe" and eng == "pe" and not dma:
                continue
            if d.eng not in best or best[d.eng].idx < d.idx:
                best[d.eng] = d
        o.deps = out + list(best.values())
        for b in reads:
            b.readers.append(o)
        for b in writes:
            cont = (dma or part) and not b.readers
            if cont:
                b.writers.append(o)
            else:
                b.prev_readers = b.readers
                b.writers = [o]
            b.readers = []
        if dma:
            o.sem = semkey
        self.ops[eng].append(o)
        self.all_ops.append(o)
        return o

    def barrier(self):
        deps = []
        for e in self.ENGS:
            for o in reversed(self.ops[e]):
                if not o.is_dma and o.fn is not None:
                    deps.append(o)
                    break
        dma_last = {}
        for o in self.all_ops:
            if o.is_dma:
                dma_last[id(o.sem)] = o
        deps = deps + list(dma_last.values())
        for e in self.ENGS:
            w = Op(e, None, False)
            w.full = True
            w.deps = [d for d in deps if d.eng != e or d.is_dma]
            self.ops[e].append(w)
            self.all_ops.append(w)

    def emit(self):
        nc = self.nc
        for o in self.all_ops:
            for d in o.deps:
                d.signals = True
        esem = {e: nc.alloc_semaphore(name="sem_" + e) for e in self.ENGS}
        cnt = {e: 0 for e in self.ENGS}
        bufsem = {}
        bufcnt = {}
        for o in self.all_ops:
            if o.is_dma:
                key = id(o.sem)
                if key not in bufsem:
                    bufsem[key] = nc.alloc_semaphore(name="dsem_%d" % len(bufsem))
                    bufcnt[key] = 0
                bufcnt[key] += 16
                o.sem_val = bufcnt[key]
                o.sem = bufsem[key]
            elif o.signals:
                cnt[o.eng] += 1
                o.sem_val = cnt[o.eng]
                o.sem = esem[o.eng]
        self.stats = dict(nsem=len(bufsem) + 5, cnt=cnt, maxdma=max(bufcnt.values()),
                          nops={e: len(self.ops[e]) for e in self.ENGS})
        for e in self.ENGS:
            eng = self.eng_obj[e]
            waited = {}
            for o in self.ops[e]:
                need = {}
                for d in o.deps:
                    k = id(d.sem)
                    if waited.get(k, 0) >= d.sem_val:
                        continue
                    if k not in need or need[k][1] < d.sem_val:
                        need[k] = (d.sem, d.sem_val)
                attach = None
                if e == "pe" and o.fn is not None and len(need) == 1 and ATTACH_WAIT:
                    (k, (s, v)), = need.items()
                    attach = (s, v)
                    waited[k] = v
                else:
                    for k, (s, v) in need.items():
                        eng.wait_ge(s, v)
                        waited[k] = v
                if o.fn is None:
                    continue
                ins = o.fn(eng)
                if attach is not None:
                    ins._wait_ge(attach[0], attach[1])
                if o.is_dma:
                    ins.then_inc(o.sem, 16)
                elif o.signals:
                    ins.then_inc(o.sem, 1)


class T:
    __slots__ = ("ap", "b")

    def __init__(self, ap, name=""):
        self.ap = ap
        self.b = Buf(name)


def build(L):
    LX = LC + L + LC
    NCH = LX // TS
    NS = NCH - LC // TS
    C0B = LC // TS
    LK = L + LC
    NKB = LK // 128
    nc = bass.Bass("TRN2", target_bir_lowering=False)
    P = Prog(nc)

    def din(name, shape, dt=F32):
        return nc.dram_tensor(name, list(shape), dt, kind="ExternalInput").ap()

    DRAM_LIST = []

    def dscr(name, shape, dt):
        t = T(nc.dram_tensor(name, list(shape), dt).ap(), name)
        DRAM_LIST.append(t)
        return t

    x_d = din("x", [L, D]); c_d = din("c", [D]); ctx_d = din("ctx", [LC, D]); cc_d = din("c_ctx", [D])
    wmod_d = din("w_mod", [D, 9 * D]); bmod_d = din("b_mod", [9 * D]); ng_d = din("norm_g", [3, D])
    fw1_d = din("ffn_w_in", [2, D, 2 * FF]); fw2_d = din("ffn_w_out", [2, FF, D])
    win_d = din("w_in_ext", [D, 3072]); wout_d = din("w_out", [D, D])
    are_d = din("ssm_a_re", [64, 64]); aim_d = din("ssm_a_im", [64, 64]); ldt_d = din("ssm_log_dt", [64])
    bre_d = din("ssm_b_re", [64, 64, 16]); bim_d = din("ssm_b_im", [64, 64, 16])
    cre_d = din("ssm_c_re", [64, 16, 64]); cim_d = din("ssm_c_im", [64, 16, 64])
    dsk_d = din("ssm_d", [512]); wglu_d = din("w_glu", [512, 512]); bglu_d = din("b_glu", [512])
    lq_d = din("lam_q", [128]); lk_d = din("lam_k", [128]); sg_d = din("subln_g", [128]); fg_d = din("final_g", [D])
    cos_d = din("rope_cos", [128, L]); sin_d = din("rope_sin", [128, L])
    idf_d = din("ident", [128, 128]); xsw_d = din("xswap", [128, 128]); sgn_d = din("sgn", [128, 1])
    out_d = T(nc.dram_tensor("out", [L, D], F32, kind="ExternalOutput").ap(), "out")

    mod_s = dscr("mod_s", [2, 9 * D], F32)
    x1_s = dscr("x1_s", [L + LC, D], F32)
    x2_s = dscr("x2_s", [L, D], F32)
    uT_s = dscr("uT_s", [512, LX], BF16)
    qT_s = dscr("qT_s", [512, L], BF16)
    kT_s = dscr("kT_s", [512, LK], BF16)
    v_s = dscr("v_s", [LK, 512], BF16)
    gT_s = dscr("gT_s", [512, L], BF16)
    sgT_s = dscr("sgT_s", [512, L], BF16)

    BIG = 212000
    DRAM_LIST.append(out_d)
    big = nc.alloc_sbuf_tensor("big", [128, BIG], U8)
    ps01 = nc.alloc_psum_tensor("ps01", [128, 1024], F32)
    pst = [ps01[:, 0:512], ps01[:, 512:1024]] + [nc.alloc_psum_tensor("ps%d" % i, [128, 512], F32)[:, :] for i in range(2, 8)]
    psT2 = ps01[:, :].bitcast(BF16).rearrange("p (k t) -> p k t", t=256)
    cur = [0]

    def sb(shape, dt, name=""):
        esz = 4 if dt == F32 else 2
        n = int(np.prod(shape[1:])) * esz
        v = big[0:shape[0], cur[0]:cur[0] + n].bitcast(dt)
        cur[0] += (n + 63) // 64 * 64
        assert cur[0] <= BIG, ("sbuf overflow", name, cur[0])
        if len(shape) == 3:
            v = v.rearrange("p (a b) -> p a b", b=shape[2])
        elif len(shape) == 4:
            v = v.rearrange("p (a b c) -> p a b c", b=shape[2], c=shape[3])
        return T(v, name)

    PS = [T(pst[i], "ps%d" % i) for i in range(8)]

    idF = sb([128, 128], F32, "idF"); idB = sb([128, 128], BF16, "idB")
    xsw = sb([128, 128], F32, "xsw"); sgn = sb([128, 1], F32, "sgn")
    epsT = sb([128, 1], F32, "eps")
    modT = sb([128, 72, 2], F32, "modT")
    gmL = sb([128, 3, 8], F32, "gmL"); gmC = sb([128, 3, 8], F32, "gmC")
    ngT = sb([128, 3, 8], F32, "ngT")
    stat = sb([128, 16], F32, "stat")
    PERSIST_END = cur[0]

    DRAM_T = set(id(t) for t in DRAM_LIST)

    def dma(eng, out_t, out_ap, in_t, in_ap, **kw):
        reads = [in_t.b] if in_t is not None else []
        key = in_t.b if id(out_t) in DRAM_T else out_t.b
        P.op(eng, lambda e, o=out_ap, i=in_ap, kw=kw: e.dma_start(out=o, in_=i, **kw),
             reads=reads, writes=[out_t.b], dma=True, semkey=key)

    def mm(out_t, out_ap, lhsT_t, lhsT, rhs_t, rhs, start, stop):
        P.op("pe", lambda e, o=out_ap, l=lhsT, r=rhs, s=start, t=stop: e.matmul(o, lhsT=l, rhs=r, start=s, stop=t),
             reads=[lhsT_t.b, rhs_t.b], writes=[out_t.b], part=True)

    def act(out_t, out_ap, in_t, in_ap, func, extra_reads=(), extra_writes=(), part=False, **kw):
        P.op("act", lambda e, o=out_ap, i=in_ap, f=func, kw=kw: e.activation(out=o, in_=i, func=f, **kw),
             reads=[in_t.b] + [t.b for t in extra_reads], writes=[out_t.b] + [t.b for t in extra_writes], part=part)

    def ts(eng, out_t, out_ap, in_t, in_ap, s1, s2, op0, op1=None, extra_reads=(), part=False):
        if op1 is None:
            f = lambda e, o=out_ap, i=in_ap: e.tensor_scalar(out=o, in0=i, scalar1=s1, scalar2=None, op0=op0)
        else:
            f = lambda e, o=out_ap, i=in_ap: e.tensor_scalar(out=o, in0=i, scalar1=s1, scalar2=s2, op0=op0, op1=op1)
        P.op(eng, f, reads=[in_t.b] + [t.b for t in extra_reads], writes=[out_t.b], part=part)

    def tt(eng, out_t, out_ap, a_t, a_ap, b_t, b_ap, op, part=False):
        P.op(eng, lambda e, o=out_ap, a=a_ap, b=b_ap: e.tensor_tensor(out=o, in0=a, in1=b, op=op),
             reads=[a_t.b, b_t.b], writes=[out_t.b], part=part)

    def stt(eng, out_t, out_ap, a_t, a_ap, scalar, b_t, b_ap, op0, op1, extra_reads=(), part=False):
        P.op(eng, lambda e, o=out_ap, a=a_ap, b=b_ap: e.scalar_tensor_tensor(out=o, in0=a, scalar=scalar, in1=b, op0=op0, op1=op1),
             reads=[a_t.b, b_t.b] + [t.b for t in extra_reads], writes=[out_t.b], part=part)

    def cp(eng, out_t, out_ap, in_t, in_ap, part=False):
        if eng == "act":
            P.op("act", lambda e, o=out_ap, i=in_ap: e.copy(out=o, in_=i), reads=[in_t.b], writes=[out_t.b], part=part)
        else:
            P.op(eng, lambda e, o=out_ap, i=in_ap: e.tensor_copy(out=o, in_=i), reads=[in_t.b], writes=[out_t.b], part=part)

    def memset(eng, t, ap, val, part=False):
        P.op(eng, lambda e, a=ap: e.memset(a, val), writes=[t.b], part=part)

    def recip(out_t, out_ap, in_t, in_ap):
        P.op("dve", lambda e, o=out_ap, i=in_ap: e.reciprocal(out=o, in_=i), reads=[in_t.b], writes=[out_t.b])

    NC_ = dict(allow_slow_non_contiguous=True)
    dma("sp", idF, idF.ap, None, idf_d)
    dma("sp", xsw, xsw.ap, None, xsw_d)
    dma("sp", sgn, sgn.ap, None, sgn_d)
    dma("sp", ngT, ngT.ap, None, ng_d.rearrange("i (k p) -> p i k", p=128), **NC_)
    cp("dve", idB, idB.ap, idF, idF.ap)
    memset("dve", epsT, epsT.ap, EPS)

    cur[0] = PERSIST_END
    cT = sb([128, 8, 2], F32, "cT"); scT = sb([128, 8, 2], F32, "scT")
    modrow = sb([2, 9 * D], F32, "modrow"); bmodr = sb([2, 9 * D], F32, "bmodr")
    wst = [sb([128, 8, 512], F32, "wst%d" % i) for i in range(2)]
    id2 = idF.ap[0:2, 0:2]
    dma("sp", cT, cT.ap[:, :, 0], None, c_d.rearrange("(k p) -> p k", p=128), **NC_)
    dma("sp", cT, cT.ap[:, :, 1], None, cc_d.rearrange("(k p) -> p k", p=128), **NC_)
    dma("sp", bmodr, bmodr.ap, None, bmod_d.partition_broadcast(2))
    act(scT, scT.ap, cT, cT.ap, AF.Silu)
    for nb in range(18):
        w = wst[nb % 2]
        dma("sp", w, w.ap, None, wmod_d[:, nb * 512:(nb + 1) * 512].rearrange("(k p) n -> p k n", p=128))
        ps = PS[nb % 2]
        for k in range(8):
            mm(ps, ps.ap[0:2, :], scT, scT.ap[:, k, :], w, w.ap[:, k, :], k == 0, k == 7)
        tt("dve", modrow, modrow.ap[:, nb * 512:(nb + 1) * 512], ps, ps.ap[0:2, :], bmodr, bmodr.ap[:, nb * 512:(nb + 1) * 512], ALU.add, part=True)
    dma("sp", mod_s, mod_s.ap, modrow, modrow.ap)
    for j in range(72):
        mm(PS[2], PS[2].ap[:, 2 * j:2 * j + 2], modrow, modrow.ap[:, j * 128:(j + 1) * 128], idF, id2, True, True)
    cp("dve", modT, modT.ap, PS[2], PS[2].ap[:, 0:144].rearrange("p (a b) -> p a b", b=2))
    for i in range(3):
        for (gm, col) in ((gmL, 0), (gmC, 1)):
            stt("dve", gm, gm.ap[:, i, :], modT, modT.ap[:, (3 * i + 1) * 8:(3 * i + 2) * 8, col], 1.0,
                ngT, ngT.ap[:, i, :], ALU.add, ALU.mult, part=True)
    P.barrier()

    def sh_ap(i, col, k):
        return modT.ap[:, 3 * i * 8 + k, col:col + 1]

    def make_front(nslots=2):
        xblk = [sb([128, 2, D], F32, "xblk%d" % i) for i in range(nslots)]
        xs = [sb([128, D], BF16, "xs%d" % i) for i in range(2)]
        junk = sb([128, D], BF16, "junk")
        hT = sb([128, 8, 256], BF16, "hT")
        return dict(xblk=xblk, xs=xs, junk=junk, hT=hT, n=0, hT2=None)

    psTr = T(None, "psTr")

    def front_load(fr, src_t, src_ap):
        n = fr["n"]; fr["n"] += 1
        xb = fr["xblk"][n % len(fr["xblk"])]
        dma("sp", xb, xb.ap, src_t, src_ap.rearrange("(s p) d -> p s d", p=128))
        return xb

    def front(fr, xb, gm, i, col, hT=None):
        for s in range(2):
            xs = fr["xs"][s]
            act(fr["junk"], fr["junk"].ap, xb, xb.ap[:, s, :], AF.Square, accum_out=stat.ap[:, s:s + 1], extra_writes=(stat,))
            act(stat, stat.ap[:, 2 + s:3 + s], stat, stat.ap[:, s:s + 1], AF.Sqrt, extra_reads=(epsT,), scale=1.0 / D, bias=epsT.ap[:, 0:1])
            recip(stat, stat.ap[:, 4 + s:5 + s], stat, stat.ap[:, 2 + s:3 + s])
            ts("dve", xs, xs.ap, xb, xb.ap[:, s, :], stat.ap[:, 4 + s:5 + s], None, ALU.mult, extra_reads=(stat,))
            for k in range(8):
                P.op("pe", lambda e, o=psT2[:, k, s * 128:(s + 1) * 128], a=xs.ap[:, k * 128:(k + 1) * 128]: e.transpose(o, a, idB.ap),
                     reads=[xs.b, idB.b], writes=[psTr.b], part=True)
        if hT is None:
            hT = fr["hT"]
        for k in range(8):
            act(hT, hT.ap[:, k, :], psTr, psT2[:, k, :], AF.Identity, extra_reads=(gm, modT), part=True,
                scale=gm.ap[:, i, k:k + 1], bias=sh_ap(i, col, k))
        return xb

    def ffn_phase(which):
        cur[0] = PERSIST_END
        W1 = sb([128, 8, 2 * FF], BF16, "W1"); W2 = sb([128, NFF, D], BF16, "W2")
        gate = sb([128, D], F32, "gate"); gatec = sb([128, D], F32, "gatec")
        fgb = gatec
        fr = make_front()
        hTs = [fr["hT"], sb([128, 8, 256], BF16, "hTb")]
        GT = sb([128, NFF, 256], BF16, "GT")
        sg = [sb([128, 256], F32, "sg%d" % i) for i in range(2)]
        ytmp = [sb([128, 512], F32, "ytmp%d" % i) for i in range(2)]
        stage_off = cur[0]
        stg = [sb([128, 1408], F32, "stg%d" % i) for i in range(2)]
        gi = 2 if which == 0 else 8
        dma("sp", gate, gate.ap, mod_s, mod_s.ap[0:1, gi * D:(gi + 1) * D].partition_broadcast(128).rearrange("p a d -> p (a d)"))
        ts("dve", gate, gate.ap, gate, gate.ap, 0.5, None, ALU.mult)
        if which == 0:
            dma("sp", gatec, gatec.ap, mod_s, mod_s.ap[1:2, 2 * D:3 * D].partition_broadcast(128).rearrange("p a d -> p (a d)"))
            ts("dve", gatec, gatec.ap, gatec, gatec.ap, 0.5, None, ALU.mult)
        else:
            dma("sp", fgb, fgb.ap, None, fg_d.partition_broadcast(128))
        engs = ("dve", "pool", "act")
        n = 0
        for k in range(8):
            for hh in range(4):
                st = stg[n % 2]
                dma("sp", st, st.ap, None, fw1_d[which, k * 128:(k + 1) * 128, hh * 1408:(hh + 1) * 1408])
                cp(engs[n % 3], W1, W1.ap[:, k, hh * 1408:(hh + 1) * 1408], st, st.ap, part=True)
                n += 1
        for m2 in range(NFF):
            st = stg[n % 2]
            dma("sp", st, st.ap[:, 0:D], None, fw2_d[which, m2 * 128:(m2 + 1) * 128, :])
            cp(engs[n % 3], W2, W2.ap[:, m2, :], st, st.ap[:, 0:D], part=True)
            n += 1
        if which == 0:
            blocks = [(None, x_d, x1_s, b * 256, b * 256, gmL, 0, gate) for b in range(L // 256)]
            blocks.append((None, ctx_d, x1_s, 0, L, gmC, 1, gatec))
        else:
            blocks = [(x2_s, x2_s.ap, out_d, b * 256, b * 256, gmL, 0, gate) for b in range(L // 256)]
        ii = 0 if which == 0 else 2
        xbs = {0: front_load(fr, blocks[0][0], blocks[0][1][blocks[0][3]:blocks[0][3] + 256, :])}
        for bi, (src_t, src, dst_t, r0, w0, gm, col, gt) in enumerate(blocks):
            if bi + 1 < len(blocks):
                nb_ = blocks[bi + 1]
                xbs[bi + 1] = front_load(fr, nb_[0], nb_[1][nb_[3]:nb_[3] + 256, :])
            xb = xbs.pop(bi)
            if bi == 0:
                front(fr, xb, gm, ii, col, hTs[0])
            hT = hTs[bi % 2]
            for m in range(NFF):
                pg = PS[2 + 2 * (m % 2)]; pu = PS[3 + 2 * (m % 2)]
                for k in range(8):
                    mm(pg, pg.ap[:, 0:256], W1, W1.ap[:, k, m * 128:(m + 1) * 128], hT, hT.ap[:, k, :], k == 0, k == 7)
                for k in range(8):
                    mm(pu, pu.ap[:, 0:256], W1, W1.ap[:, k, FF + m * 128:FF + (m + 1) * 128], hT, hT.ap[:, k, :], k == 0, k == 7)
                s_ = sg[m % 2]
                act(s_, s_.ap, pg, pg.ap[:, 0:256], AF.Silu)
                tt("dve", GT, GT.ap[:, m, :], s_, s_.ap, pu, pu.ap[:, 0:256], ALU.mult, part=True)
            if bi + 1 < len(blocks):
                nb_ = blocks[bi + 1]
                front(fr, xbs[bi + 1], nb_[5], ii, nb_[6], hTs[(bi + 1) % 2])
            q = 0
            for s in range(2):
                for nh in range(2):
                    py = PS[6 + (q % 2)]
                    for m in range(NFF):
                        mm(py, py.ap, GT, GT.ap[:, m, s * 128:(s + 1) * 128], W2, W2.ap[:, m, nh * 512:(nh + 1) * 512], m == 0, m == NFF - 1)
                    yt = ytmp[q % 2]
                    tt("dve", yt, yt.ap, py, py.ap, gt, gt.ap[:, nh * 512:(nh + 1) * 512], ALU.mult)
                    tt("pool", xb, xb.ap[:, s, nh * 512:(nh + 1) * 512], xb, xb.ap[:, s, nh * 512:(nh + 1) * 512], yt, yt.ap, ALU.add)
                    q += 1
            if which == 1:
                for s in range(2):
                    act(fr["junk"], fr["junk"].ap, xb, xb.ap[:, s, :], AF.Square, accum_out=stat.ap[:, 8 + s:9 + s], extra_writes=(stat,))
                    act(stat, stat.ap[:, 10 + s:11 + s], stat, stat.ap[:, 8 + s:9 + s], AF.Sqrt, extra_reads=(epsT,), scale=1.0 / D, bias=epsT.ap[:, 0:1])
                    recip(stat, stat.ap[:, 12 + s:13 + s], stat, stat.ap[:, 10 + s:11 + s])
                    stt("dve", xb, xb.ap[:, s, :], xb, xb.ap[:, s, :], stat.ap[:, 12 + s:13 + s], fgb, fgb.ap, ALU.mult, ALU.mult, extra_reads=(stat,))
            dma("sp", dst_t, dst_t.ap[w0:w0 + 256, :].rearrange("(s p) d -> p s d", p=128), xb, xb.ap)
        P.barrier()

    def inproj_phase():
        cur[0] = PERSIST_END
        Win = sb([128, 8, 3072], BF16, "Win")
        fr = make_front()
        stg = [sb([128, 3072], F32, "stgB%d" % i) for i in range(2)]
        cosb = [sb([128, 256], F32, "cos%d" % i) for i in range(2)]
        sinb = [sb([128, 256], F32, "sin%d" % i) for i in range(2)]
        t1 = [sb([128, 256], F32, "t1_%d" % i) for i in range(2)]
        t2 = [sb([128, 256], F32, "t2_%d" % i) for i in range(2)]
        uo = [sb([128, 4, 256], BF16, "uo%d" % i) for i in range(2)]
        qo = [sb([128, 4, 256], BF16, "qo%d" % i) for i in range(2)]
        ko = [sb([128, 4, 256], BF16, "ko%d" % i) for i in range(2)]
        vo = [sb([128, 2, 512], BF16, "vo%d" % i) for i in range(2)]
        engs = ("dve", "pool", "act")
        for k in range(8):
            st = stg[k % 2]
            dma("sp", st, st.ap, None, win_d[k * 128:(k + 1) * 128, :])
            cp(engs[k % 3], Win, Win.ap[:, k, :], st, st.ap, part=True)
        nblk = L // 256

        def pre(b):
            r0_ = L if b == nblk else b * 256
            xb_ = front_load(fr, x1_s, x1_s.ap[r0_:r0_ + 256, :])
            if b != nblk:
                dma("sp", cosb[b % 2], cosb[b % 2].ap, None, cos_d[:, b * 256:(b + 1) * 256])
                dma("sp", sinb[b % 2], sinb[b % 2].ap, None, sin_d[:, b * 256:(b + 1) * 256])
            return xb_
        xbs = {0: pre(0)}
        for b in range(nblk + 1):
            isctx = b == nblk
            r0 = L if isctx else b * 256
            gm = gmC if isctx else gmL
            if b + 1 <= nblk:
                xbs[b + 1] = pre(b + 1)
            front(fr, xbs.pop(b), gm, 1, 1 if isctx else 0)
            hT = fr["hT"]
            sl = b % 2
            pcount = 0
            for j in range(4):
                ps = PS[2 + (pcount % 4)]; pcount += 1
                for k in range(8):
                    mm(ps, ps.ap[:, 0:256], Win, Win.ap[:, k, j * 128:(j + 1) * 128], hT, hT.ap[:, k, :], k == 0, k == 7)
                cp("act", uo[sl], uo[sl].ap[:, j, :], ps, ps.ap[:, 0:256], part=True)
            if isctx:
                dma("sp", uT_s, uT_s.ap[:, 0:LC].rearrange("(j p) t -> p j t", p=128), uo[sl], uo[sl].ap)
                dma("sp", uT_s, uT_s.ap[:, LC + L:LX].rearrange("(j p) t -> p j t", p=128), uo[sl], uo[sl].ap)
            else:
                dma("sp", uT_s, uT_s.ap[:, LC + b * 256:LC + (b + 1) * 256].rearrange("(j p) t -> p j t", p=128), uo[sl], uo[sl].ap)
            for (base, swb, ot, dst, isq) in ((512, 2048, qo, qT_s, True), (1024, 2560, ko, kT_s, False)):
                if isq and isctx:
                    continue
                for j in range(4):
                    pa = PS[2 + (pcount % 4)]; pcount += 1
                    for k in range(8):
                        mm(pa, pa.ap[:, 0:256], Win, Win.ap[:, k, base + j * 128:base + (j + 1) * 128], hT, hT.ap[:, k, :], k == 0, k == 7)
                    if isctx:
                        cp("act", ot[sl], ot[sl].ap[:, j, :], pa, pa.ap[:, 0:256], part=True)
                        continue
                    pb = PS[2 + (pcount % 4)]; pcount += 1
                    for k in range(8):
                        mm(pb, pb.ap[:, 0:256], Win, Win.ap[:, k, swb + j * 128:swb + (j + 1) * 128], hT, hT.ap[:, k, :], k == 0, k == 7)
                    a1 = t1[j % 2]; a2 = t2[j % 2]
                    tt("dve", a1, a1.ap, pa, pa.ap[:, 0:256], cosb[sl], cosb[sl].ap, ALU.mult)
                    tt("dve", a2, a2.ap, pb, pb.ap[:, 0:256], sinb[sl], sinb[sl].ap, ALU.mult)
                    tt("pool", ot[sl], ot[sl].ap[:, j, :], a1, a1.ap, a2, a2.ap, ALU.add, part=True)
                c0 = L if isctx else b * 256
                dma("sp", dst, dst.ap[:, c0:c0 + 256].rearrange("(j p) t -> p j t", p=128), ot[sl], ot[sl].ap)
            for s in range(2):
                ps = PS[6 + s]
                for k in range(8):
                    mm(ps, ps.ap, hT, hT.ap[:, k, s * 128:(s + 1) * 128], Win, Win.ap[:, k, 1536:2048], k == 0, k == 7)
                cp("act", vo[sl], vo[sl].ap[:, s, :], ps, ps.ap, part=True)
            dma("sp", v_s, v_s.ap[r0:r0 + 256, :].rearrange("(s p) d -> p s d", p=128), vo[sl], vo[sl].ap)
        P.barrier()

    def s5_phase():
        cur[0] = PERSIST_END
        NR = 64
        Bpad = sb([128, NR, 128], BF16, "Bpad"); Cpad = sb([128, NR, 128], BF16, "Cpad"); ABpad = sb([128, NR, 128], BF16, "ABpad")
        PW = sb([128, 10, 2, NR], F32, "PW")
        dsk = sb([128, 4], F32, "dsk")
        setup_off = cur[0]
        arow = sb([64, 2, 128], F32, "arow")
        prm = sb([128, 16, NR], F32, "prm")
        Ball = sb([64, 2, NR * 16], F32, "Ball")
        Bbar = sb([64, 2, NR * 16], F32, "Bbar")
        btmp = sb([64, 2, NR * 16], F32, "btmp")
        Bbar2 = sb([64, 2, NR * 16], F32, "Bbar2")
        BP = [sb([64, 2, 128], F32, "BP%d" % i) for i in range(2)]
        BP2 = [sb([64, 2, 128], F32, "BP2_%d" % i) for i in range(2)]
        Call = sb([16, NR, 128], F32, "Call")
        dma("sp", arow, arow.ap[:, 0, 0:64], None, are_d); dma("sp", arow, arow.ap[:, 0, 64:128], None, are_d)
        dma("sp", arow, arow.ap[:, 1, 0:64], None, aim_d); dma("sp", arow, arow.ap[:, 1, 64:128], None, aim_d)
        dma("sp", prm, prm.ap[:, 2, :], None, ldt_d.partition_broadcast(128))
        dma("sp", dsk, dsk.ap, None, dsk_d.rearrange("(c p) -> p c", p=128), **NC_)
        dma("sp", Ball, Ball.ap[:, 0, :].rearrange("p (r h) -> p r h", h=16), None, bre_d.rearrange("r p h -> p r h"))
        dma("sp", Ball, Ball.ap[:, 1, :].rearrange("p (r h) -> p r h", h=16), None, bim_d.rearrange("r p h -> p r h"))
        dma("sp", Call, Call.ap[:, :, 0:64], None, cre_d.rearrange("r h p -> h r p"))
        dma("sp", Call, Call.ap[:, :, 64:128], None, cim_d.rearrange("r h p -> h r p"))
        for i in range(2):
            mm(PS[0], PS[0].ap[:, i * 64:(i + 1) * 64], arow, arow.ap[:, i, :], idF, idF.ap[0:64, 0:64], True, True)
        cp("dve", prm, prm.ap[:, 0:2, :], PS[0], PS[0].ap[:, 0:128].rearrange("p (a b) -> p a b", b=64))
        Pm = lambda i: prm.ap[:, i, :]
        ARE, AIM, LDT, DT, XR, ANG, MAG, T0, T1_, COS, SIN, AL, BE = range(13)

        def e_tt(o, a, b, op):
            tt("dve", prm, Pm(o), prm, Pm(a), prm, Pm(b), op)

        def e_ts(o, a, s1, s2, op0, op1=None):
            ts("dve", prm, Pm(o), prm, Pm(a), s1, s2, op0, op1)

        act(prm, Pm(DT), prm, Pm(LDT), AF.Exp)
        e_tt(XR, DT, ARE, ALU.mult)
        e_tt(ANG, DT, AIM, ALU.mult)
        e_ts(MAG, XR, 1.0 / 7.0, 1.0, ALU.mult, ALU.add)
        for kk in (6.0, 5.0, 4.0, 3.0, 2.0, 1.0):
            e_tt(MAG, MAG, XR, ALU.mult)
            e_ts(MAG, MAG, 1.0 / kk, 1.0, ALU.mult, ALU.add)
        TWO_PI = float(2 * np.pi)
        for (dst, shift) in ((SIN, 0.0), (COS, float(np.pi / 2))):
            e_ts(T0, ANG, shift, None, ALU.add)
            e_ts(T1_, T0, 1.0 / TWO_PI, 12582912.0, ALU.mult, ALU.add)
            e_ts(T1_, T1_, -12582912.0, None, ALU.add)
            stt("dve", prm, Pm(T0), prm, Pm(T1_), -TWO_PI, prm, Pm(T0), ALU.mult, ALU.add)
            act(prm, Pm(dst), prm, Pm(T0), AF.Sin)
        e_tt(AL, MAG, COS, ALU.mult)
        e_tt(BE, MAG, SIN, ALU.mult)
        ZR, DEN, CR, CI = 13, 14, 15, 3
        e_ts(ZR, AL, -1.0, None, ALU.add)
        e_tt(DEN, ARE, ARE, ALU.mult); e_tt(T0, AIM, AIM, ALU.mult); e_tt(DEN, DEN, T0, ALU.add)
        recip(prm, Pm(DEN), prm, Pm(DEN))
        e_tt(CR, ZR, ARE, ALU.mult); e_tt(T0, BE, AIM, ALU.mult); e_tt(CR, CR, T0, ALU.add); e_tt(CR, CR, DEN, ALU.mult)
        e_tt(CI, BE, ARE, ALU.mult); e_tt(T0, ZR, AIM, ALU.mult); e_tt(CI, CI, T0, ALU.subtract); e_tt(CI, CI, DEN, ALU.mult)
        cp("dve", PW, PW.ap[:, 0, 0, :], prm, Pm(AL))
        ts("dve", PW, PW.ap[:, 0, 1, :], prm, Pm(BE), sgn.ap[:, 0:1], None, ALU.mult, extra_reads=(sgn,))
        cur_pow = 1
        kidx = 1
        while kidx < 9:
            e_tt(T0, AL, AL, ALU.mult); e_tt(T1_, BE, BE, ALU.mult)
            e_tt(BE, AL, BE, ALU.mult); e_ts(BE, BE, 2.0, None, ALU.mult)
            e_tt(AL, T0, T1_, ALU.subtract)
            cur_pow *= 2
            if cur_pow == 2:
                cp("dve", PW, PW.ap[:, 9, 0, :], prm, Pm(AL))
                ts("dve", PW, PW.ap[:, 9, 1, :], prm, Pm(BE), sgn.ap[:, 0:1], None, ALU.mult, extra_reads=(sgn,))
            if cur_pow >= TS:
                cp("dve", PW, PW.ap[:, kidx, 0, :], prm, Pm(AL))
                ts("dve", PW, PW.ap[:, kidx, 1, :], prm, Pm(BE), sgn.ap[:, 0:1], None, ALU.mult, extra_reads=(sgn,))
                kidx += 1
        crb = prm.ap[0:64, CR, :].unsqueeze(2).broadcast_to([64, NR, 16])
        cib = prm.ap[0:64, CI, :].unsqueeze(2).broadcast_to([64, NR, 16])
        B3 = lambda t, i: t.ap[:, i, :].rearrange("p (r h) -> p r h", h=16)
        tt("dve", Bbar, B3(Bbar, 0), Ball, B3(Ball, 0), prm, crb, ALU.mult)
        tt("dve", btmp, B3(btmp, 0), Ball, B3(Ball, 1), prm, cib, ALU.mult)
        tt("dve", Bbar, B3(Bbar, 0), Bbar, B3(Bbar, 0), btmp, B3(btmp, 0), ALU.subtract)
        tt("dve", Bbar, B3(Bbar, 1), Ball, B3(Ball, 1), prm, crb, ALU.mult)
        tt("dve", btmp, B3(btmp, 1), Ball, B3(Ball, 0), prm, cib, ALU.mult)
        tt("dve", Bbar, B3(Bbar, 1), Bbar, B3(Bbar, 1), btmp, B3(btmp, 1), ALU.add)
        alb = PW.ap[0:64, 0, 0, :].unsqueeze(2).broadcast_to([64, NR, 16])
        beb = PW.ap[0:64, 0, 1, :].unsqueeze(2).broadcast_to([64, NR, 16])
        tt("dve", Bbar2, B3(Bbar2, 0), Bbar, B3(Bbar, 0), PW, alb, ALU.mult)
        tt("dve", btmp, B3(btmp, 0), Bbar, B3(Bbar, 1), PW, beb, ALU.mult)
        tt("dve", Bbar2, B3(Bbar2, 0), Bbar2, B3(Bbar2, 0), btmp, B3(btmp, 0), ALU.subtract)
        tt("dve", Bbar2, B3(Bbar2, 1), Bbar, B3(Bbar, 1), PW, alb, ALU.mult)
        tt("dve", btmp, B3(btmp, 1), Bbar, B3(Bbar, 0), PW, beb, ALU.mult)
        tt("dve", Bbar2, B3(Bbar2, 1), Bbar2, B3(Bbar2, 1), btmp, B3(btmp, 1), ALU.add)
        ts("dve", Call, Call.ap[:, :, 64:128], Call, Call.ap[:, :, 64:128], -1.0, None, ALU.mult)
        memset("pool", Cpad, Cpad.ap, 0.0)
        for r in range(NR):
            gp = r % 8
            bp = BP[r % 2]
            memset("pool", bp, bp.ap, 0.0)
            for i in range(2):
                cp("pool", bp, bp.ap[:, i, gp * 16:(gp + 1) * 16], Bbar, Bbar.ap[:, i, r * 16:(r + 1) * 16], part=True)
            ps = PS[1 + (r % 2)]
            for i in range(2):
                mm(ps, ps.ap[:, i * 64:(i + 1) * 64], bp, bp.ap[:, i, :], idF, idF.ap[0:64, 0:64], True, True)
            cp("act", Bpad, Bpad.ap[:, r, :], ps, ps.ap[:, 0:128], part=True)
            bp2 = BP2[r % 2]
            memset("pool", bp2, bp2.ap, 0.0)
            for i in range(2):
                cp("pool", bp2, bp2.ap[:, i, gp * 16:(gp + 1) * 16], Bbar2, Bbar2.ap[:, i, r * 16:(r + 1) * 16], part=True)
            ps2 = PS[5 + (r % 2)]
            for i in range(2):
                mm(ps2, ps2.ap[:, i * 64:(i + 1) * 64], bp2, bp2.ap[:, i, :], idF, idF.ap[0:64, 0:64], True, True)
            cp("act", ABpad, ABpad.ap[:, r, :], ps2, ps2.ap[:, 0:128], part=True)
            pc = PS[3 + (r % 2)]
            mm(pc, pc.ap[:, 0:16], Call, Call.ap[:, r, :], idF, idF.ap[0:16, 0:16], True, True)
            cp("dve", Cpad, Cpad.ap[:, r, gp * 16:(gp + 1) * 16], pc, pc.ap[:, 0:16], part=True)
        P.barrier()
        cur[0] = setup_off
        utok = sb([128, LX], BF16, "utok")
        uJ = sb([128, TS, NCH], BF16, "uJ")
        yac = sb([128, TS, NCH], F32, "yac")
        Am = [sb([128, 128], F32, "Am%d" % i) for i in range(8)]
        A2m = [sb([128, 128], F32, "A2m%d" % i) for i in range(8)]
        Ap = [sb([128, 128], F32, "Ap%d" % i) for i in range(2)]
        hA = [sb([128, 2, NS], F32, "hA%d" % i) for i in range(4)]
        hB = [sb([128, 2, NS], F32, "hB%d" % i) for i in range(4)]
        hb16 = [[sb([128, 2, NS], BF16, "hb%d_%d" % (i, j)) for j in range(2)] for i in range(4)]
        HcU = [sb([128, 2, NS], F32, "Hc%d" % i) for i in range(4)]
        Hc = [T(HcU[g // 2].ap[:, g % 2, :], "Hcr%d" % g) for g in range(8)]
        gel = [sb([128, L // 2], F32, "gel%d" % i) for i in range(2)]
        gout = sb([128, L // 2], BF16, "gout")
        psH = [T(PS[u].ap[:, 0:2 * NS].rearrange("p (a b) -> p a b", b=NS), "psH%d" % u) for u in range(4)]
        psC = [T(PS[4 + i].ap[:, 0:NS], "psC%d" % i) for i in range(2)]
        psY = [T(PS[6 + i].ap[:, 0:NS], "psY%d" % i) for i in range(2)]
        evn = [0]

        def evac(out_t, out_ap, in_t, in_ap):
            e = ("act", "dve")[evn[0] % 2]; evn[0] += 1
            cp(e, out_t, out_ap, in_t, in_ap)

        def build_A(dst, k, r):
            ts("dve", dst, dst.ap, idF, idF.ap, PW.ap[:, k, 0, r:r + 1], None, ALU.mult, extra_reads=(PW,))
            stt("dve", dst, dst.ap, xsw, xsw.ap, PW.ap[:, k, 1, r:r + 1], dst, dst.ap, ALU.mult, ALU.add, extra_reads=(PW,))

        def readout(j, st, rows, c0, ycopy):
            py = psY[st % 2]
            for g in range(8):
                hb = hb16[g // 2][st % 2]
                mm(py, py.ap, Cpad, Cpad.ap[:, rows[g], :], hb, hb.ap[:, g % 2, :], g == 0, g == 7)
            if ycopy:
                evac(yac, yac.ap[:, j, c0:c0 + NS], py, py.ap)
            else:
                tt("dve", yac, yac.ap[:, j, c0:c0 + NS], yac, yac.ap[:, j, c0:c0 + NS], py, py.ap, ALU.add)

        for gc in range(4):
            dma("sp", utok, utok.ap, uT_s, uT_s.ap[gc * 128:(gc + 1) * 128, :])
            cp("pool", uJ, uJ.ap, utok, utok.ap.rearrange("p (c j) -> p j c", j=TS))
            for dr in range(2):
                c0 = 0 if dr == 0 else C0B
                rows = [dr * 32 + gc * 8 + g for g in range(8)]
                ycopy = (dr == 0)
                for g in range(8):
                    build_A(Am[g], 0, rows[g])
                    build_A(A2m[g], 9, rows[g])
                jorder = list(range(TS)) if dr == 0 else list(range(TS - 1, -1, -1))
                for pss in range(2):
                    if pss == 1:
                        for g in range(8):
                            H = Hc[g]
                            k = 1
                            sft = 1
                            while sft < NS:
                                ap_ = Ap[(g + k) % 2]
                                build_A(ap_, k, rows[g])
                                pc = psC[(g + k) % 2]
                                if dr == 0:
                                    mm(pc, pc.ap[:, 0:NS - sft], ap_, ap_.ap, H, H.ap[:, 0:NS - sft], True, True)
                                    tt("dve", H, H.ap[:, sft:NS], H, H.ap[:, sft:NS], pc, pc.ap[:, 0:NS - sft], ALU.add)
                                else:
                                    mm(pc, pc.ap[:, 0:NS - sft], ap_, ap_.ap, H, H.ap[:, sft:NS], True, True)
                                    tt("dve", H, H.ap[:, 0:NS - sft], H, H.ap[:, 0:NS - sft], pc, pc.ap[:, 0:NS - sft], ALU.add)
                                sft *= 2
                                k += 1
                        for u in range(4):
                            h0 = hA[u]
                            memset("pool", h0, h0.ap, 0.0)
                            for r2 in range(2):
                                H = Hc[2 * u + r2]
                                if dr == 0:
                                    cp("pool", h0, h0.ap[:, r2, 1:NS], H, H.ap[:, 0:NS - 1], part=True)
                                else:
                                    cp("pool", h0, h0.ap[:, r2, 0:NS - 1], H, H.ap[:, 1:NS], part=True)
                    hcur = list(hA); hnxt = list(hB)
                    pend = None
                    if pss == 0:
                        for t2 in range(TS // 2):
                            ja, jb = jorder[2 * t2], jorder[2 * t2 + 1]
                            for u in range(4):
                                for r2 in range(2):
                                    g = 2 * u + r2
                                    r = rows[g]
                                    if t2 > 0:
                                        mm(psH[u], psH[u].ap[:, r2, :], A2m[g], A2m[g].ap, hcur[u], hcur[u].ap[:, r2, :], True, False)
                                    mm(psH[u], psH[u].ap[:, r2, :], ABpad, ABpad.ap[:, r, :], uJ, uJ.ap[:, ja, c0:c0 + NS], t2 == 0, False)
                                    mm(psH[u], psH[u].ap[:, r2, :], Bpad, Bpad.ap[:, r, :], uJ, uJ.ap[:, jb, c0:c0 + NS], False, True)
                                if t2 == TS // 2 - 1:
                                    e = ("act", "dve")[evn[0] % 2]; evn[0] += 1
                                    if e == "act":
                                        P.op("act", lambda e_, o=HcU[u].ap, i=psH[u].ap: e_.copy(out=o, in_=i), reads=[psH[u].b], writes=[Hc[2 * u].b, Hc[2 * u + 1].b])
                                    else:
                                        P.op("dve", lambda e_, o=HcU[u].ap, i=psH[u].ap: e_.tensor_copy(out=o, in_=i), reads=[psH[u].b], writes=[Hc[2 * u].b, Hc[2 * u + 1].b])
                                else:
                                    evac(hnxt[u], hnxt[u].ap, psH[u], psH[u].ap)
                            hcur, hnxt = hnxt, hcur
                        continue
                    for st, j in enumerate(jorder):
                        for u in range(4):
                            first = (pss == 0 and st == 0)
                            for r2 in range(2):
                                g = 2 * u + r2
                                r = rows[g]
                                if not first:
                                    mm(psH[u], psH[u].ap[:, r2, :], Am[g], Am[g].ap, hcur[u], hcur[u].ap[:, r2, :], True, False)
                                mm(psH[u], psH[u].ap[:, r2, :], Bpad, Bpad.ap[:, r, :], uJ, uJ.ap[:, j, c0:c0 + NS], first, True)
                            if pss == 0 and st == TS - 1:
                                e = ("act", "dve")[evn[0] % 2]; evn[0] += 1
                                if e == "act":
                                    P.op("act", lambda e_, o=HcU[u].ap, i=psH[u].ap: e_.copy(out=o, in_=i), reads=[psH[u].b], writes=[Hc[2 * u].b, Hc[2 * u + 1].b])
                                else:
                                    P.op("dve", lambda e_, o=HcU[u].ap, i=psH[u].ap: e_.tensor_copy(out=o, in_=i), reads=[psH[u].b], writes=[Hc[2 * u].b, Hc[2 * u + 1].b])
                            else:
                                evac(hnxt[u], hnxt[u].ap, psH[u], psH[u].ap)
                            if pss == 1:
                                hb = hb16[u][st % 2]
                                cp("pool", hb, hb.ap, hnxt[u], hnxt[u].ap)
                        if pss == 1:
                            if pend is not None:
                                readout(pend[0], pend[1], rows, c0, ycopy)
                            pend = (j, st)
                        hcur, hnxt = hnxt, hcur
                    if pss == 1:
                        readout(pend[0], pend[1], rows, c0, ycopy)
            LH = L // 2
            for hf in range(2):
                CL = slice(C0B + hf * (LH // TS), C0B + (hf + 1) * (LH // TS))
                g0 = gel[0]; g1 = gel[1]
                v3 = lambda t: t.ap.rearrange("p (c j) -> p j c", j=TS)
                stt("dve", g0, v3(g0), uJ, uJ.ap[:, :, CL], dsk.ap[:, gc:gc + 1], yac, yac.ap[:, :, CL], ALU.mult, ALU.add, extra_reads=(dsk,))
                tt("pool", g1, g1.ap, g0, g0.ap, g0, g0.ap, ALU.mult)
                ts("dve", g1, g1.ap, g1, g1.ap, 0.044715, 1.0, ALU.mult, ALU.add)
                tt("pool", g1, g1.ap, g1, g1.ap, g0, g0.ap, ALU.mult)
                act(g1, g1.ap, g1, g1.ap, AF.Sigmoid, scale=1.5957691216057308)
                tt("dve", gout, gout.ap, g0, g0.ap, g1, g1.ap, ALU.mult)
                dma("sp", gT_s, gT_s.ap[gc * 128:(gc + 1) * 128, hf * LH:(hf + 1) * LH], gout, gout.ap)
        P.barrier()

    def emit_readout(j, st, rows, hb16, psY, Cpad, yac, c0, ycopy, evac):
        py = psY[st % 2]
        for g in range(4):
            hb = hb16[g][st % 2]
            mm(py, py.ap, Cpad, Cpad.ap[:, rows[g], :], hb, hb.ap, g == 0, g == 3)
        NSl = py.ap.shape[1]
        if ycopy:
            evac(yac, yac.ap[:, j, c0:c0 + NSl], py, py.ap)
        else:
            tt("dve", yac, yac.ap[:, j, c0:c0 + NSl], yac, yac.ap[:, j, c0:c0 + NSl], py, py.ap, ALU.add)

    def glu_phase():
        cur[0] = PERSIST_END
        Wg = sb([128, 4, 512], BF16, "Wg"); wst_ = sb([128, 4, 512], F32, "wgst")
        bg = sb([128, 4], F32, "bg")
        gb = [sb([128, 4, 512], BF16, "gb%d" % i) for i in range(2)]
        so = [sb([128, 4, 512], BF16, "so%d" % i) for i in range(2)]
        sgm = [sb([128, 512], F32, "sgm%d" % i) for i in range(2)]
        dma("sp", wst_, wst_.ap, None, wglu_d.rearrange("(k p) n -> p k n", p=128))
        cp("dve", Wg, Wg.ap, wst_, wst_.ap)
        dma("sp", bg, bg.ap, None, bglu_d.rearrange("(c p) -> p c", p=128), **NC_)
        def gload(tb):
            dma("sp", gb[tb % 2], gb[tb % 2].ap, gT_s, gT_s.ap[:, tb * 512:(tb + 1) * 512].rearrange("(k p) t -> p k t", p=128))
        gload(0)
        for tb in range(L // 512):
            g_ = gb[tb % 2]; s_ = so[tb % 2]
            if tb + 1 < L // 512:
                gload(tb + 1)
            for m in range(4):
                ps = PS[m % 2]
                for k in range(4):
                    mm(ps, ps.ap, Wg, Wg.ap[:, k, m * 128:(m + 1) * 128], g_, g_.ap[:, k, :], k == 0, k == 3)
                sm = sgm[m % 2]
                act(sm, sm.ap, ps, ps.ap, AF.Sigmoid, extra_reads=(bg,), bias=bg.ap[:, m:m + 1])
                tt("dve", s_, s_.ap[:, m, :], g_, g_.ap[:, m, :], sm, sm.ap, ALU.mult, part=True)
            dma("sp", sgT_s, sgT_s.ap[:, tb * 512:(tb + 1) * 512].rearrange("(k p) t -> p k t", p=128), s_, s_.ap)
        P.barrier()

    def attn_phase():
        cur[0] = PERSIST_END
        kT = sb([128, 4, LK], BF16, "kT")
        vA = sb([128, NKB, 512], BF16, "vA")
        Wo = sb([128, 8, D], BF16, "Wo")
        gate5 = sb([128, D], F32, "gate5")
        gsubT = sb([128, 1], F32, "gsubT")
        onesB = sb([128, 128], BF16, "onesB"); onesF = sb([128, 128], F32, "onesF")
        lam = sb([128, 8], F32, "lam"); lqk = sb([128, 2, 128], F32, "lqk")
        qT = [sb([128, 4, 512], BF16, "qT%d" % i) for i in range(2)]
        catT = sb([128, 8, 512], BF16, "catT")
        eT = [sb([128, 512], BF16, "eT%d" % i) for i in range(3)]
        w0 = sb([128, 512], F32, "w0"); w1 = sb([128, 512], F32, "w1"); w2 = sb([128, 512], F32, "w2")
        w3 = sb([128, 512], F32, "w3"); w4 = sb([128, 512], F32, "w4"); w5 = sb([128, 512], F32, "w5")
        x1t = [sb([128, D], F32, "x1t%d" % i) for i in range(2)]
        yall = sb([128, D], F32, "stgD0")
        ytmp = [T(yall.ap[:, 0:512], "ytmpD0"), T(yall.ap[:, 512:1024], "ytmpD1")]
        stg = [yall]
        dma("sp", gate5, gate5.ap, mod_s, mod_s.ap[0:1, 5 * D:6 * D].partition_broadcast(128).rearrange("p a d -> p (a d)"))
        dma("sp", gsubT, gsubT.ap, None, sg_d.rearrange("(p o) -> p o", o=1))
        ts("dve", gsubT, gsubT.ap, gsubT, gsubT.ap, 1.0 - LAM_INIT, None, ALU.mult)
        memset("pool", onesB, onesB.ap, 1.0)
        memset("pool", onesF, onesF.ap, 1.0 / 128.0)
        dma("sp", lqk, lqk.ap[:, 0, :], None, lq_d.partition_broadcast(128))
        dma("sp", lqk, lqk.ap[:, 1, :], None, lk_d.partition_broadcast(128))
        tt("dve", lqk, lqk.ap[:, 0, :], lqk, lqk.ap[:, 0, :], lqk, lqk.ap[:, 1, :], ALU.mult)
        P.op("dve", lambda e: e.reduce_sum(out=lam.ap[:, 0:2], in_=lqk.ap[:, 0, :].rearrange("p (a b) -> p a b", b=64), axis=mybir.AxisListType.X),
             reads=[lqk.b], writes=[lam.b])
        act(lam, lam.ap[:, 2:4], lam, lam.ap[:, 0:2], AF.Exp)
        tt("dve", lam, lam.ap[:, 4:5], lam, lam.ap[:, 2:3], lam, lam.ap[:, 3:4], ALU.subtract)
        ts("dve", lam, lam.ap[:, 5:6], lam, lam.ap[:, 4:5], LAM_INIT, -1.0, ALU.add, ALU.mult)
        for m2 in range(8):
            st = stg[0]
            dma("sp", st, st.ap, None, wout_d[m2 * 128:(m2 + 1) * 128, :])
            cp("dve", Wo, Wo.ap[:, m2, :], st, st.ap, part=True)
        for h in range(4):
            dma("sp", kT, kT.ap[:, h, :], kT_s, kT_s.ap[h * 128:(h + 1) * 128, :])
        for k0 in range(0, NKB, 6):
            k1 = min(NKB, k0 + 6)
            dma("sp", vA, vA.ap[:, k0:k1, :], v_s, v_s.ap[k0 * 128:k1 * 128, :].rearrange("(kb p) e -> p kb e", p=128))
        psS = [PS[0], PS[1]]
        psYd = PS[7]

        def banks(h, c):
            if c == 1:
                return PS[3], PS[5]
            return (PS[2], PS[4]) if h % 2 == 0 else (PS[6], PS[7])
        scnt = 0

        def qload(qb):
            dma("sp", qT[qb % 2], qT[qb % 2].ap, qT_s, qT_s.ap[:, qb * 512:(qb + 1) * 512].rearrange("(h p) t -> p h t", p=128))
        qload(0)
        for qb in range(L // 512):
            q_ = qT[qb % 2]
            if qb + 1 < L // 512:
                qload(qb + 1)
            dma("sp", catT, catT.ap[:, 0:4, :], sgT_s, sgT_s.ap[:, qb * 512:(qb + 1) * 512].rearrange("(k p) t -> p k t", p=128))
            its = [(h, c, kb) for h in range(4) for c in range(2) for kb in range(NKB)]

            def s_mm(i):
                h, c, kb = its[i]
                pl = slice(c * 64, (c + 1) * 64)
                pS = psS[(scnt0 + i) % 2]
                mm(pS, pS.ap, kT, kT.ap[pl, h, kb * 128:(kb + 1) * 128], q_, q_.ap[pl, h, :], True, True)
            scnt0 = scnt

            def warm(n):
                for _ in range(n):
                    mm(PS[6], PS[6].ap, onesB, onesB.ap, kT, kT.ap[:, 0, 0:512], True, True)
            warm(WARM_N)
            s_mm(0)
            pending = []
            for i, (h, c, kb) in enumerate(its):
                for pd in list(pending):
                    pd[0] -= 1
                    if pd[0] <= 0:
                        pd[1]()
                        pending.remove(pd)
                pS = psS[(scnt0 + i) % 2]; e_ = eT[(scnt0 + i) % 3]
                act(e_, e_.ap, pS, pS.ap, AF.Exp, scale=0.125)
                if i + 1 < len(its):
                    s_mm(i + 1)
                if kb % BURST_EVERY == BURST_EVERY // 2:
                    wb = PS[7] if h % 2 == 0 else PS[4]
                    for _ in range(BURST_N):
                        mm(wb, wb.ap, onesB, onesB.ap, kT, kT.ap[:, 0, 0:512], True, True)
                pO, pD = banks(h, c)
                mm(pO, pO.ap, vA, vA.ap[:, kb, h * 128:(h + 1) * 128], e_, e_.ap, kb == 0, kb == NKB - 1)
                mm(pD, pD.ap, onesB, onesB.ap, e_, e_.ap, kb == 0, kb == NKB - 1)
                if not (c == 1 and kb == NKB - 1):
                    continue
                pO0, pD0 = banks(h, 0)
                pO1, pD1 = banks(h, 1)
                cp("act", w0, w0.ap, pD0, pD0.ap)
                cp("dve", w4, w4.ap, pO0, pO0.ap)
                cp("act", w1, w1.ap, pD1, pD1.ap)
                cp("dve", w5, w5.ap, pO1, pO1.ap)
                recip(w0, w0.ap, w0, w0.ap)
                tt("dve", w4, w4.ap, w4, w4.ap, w0, w0.ap, ALU.mult)
                recip(w1, w1.ap, w1, w1.ap)
                tt("dve", w5, w5.ap, w5, w5.ap, w1, w1.ap, ALU.mult)
                stt("dve", w4, w4.ap, w5, w5.ap, lam.ap[:, 5:6], w4, w4.ap, ALU.mult, ALU.add, extra_reads=(lam,))
                tt("pool", w2, w2.ap, w4, w4.ap, w4, w4.ap, ALU.mult)

                def fin(h=h, psM=pO0):
                    mm(psM, psM.ap, onesF, onesF.ap, w2, w2.ap, True, True)
                    act(w3, w3.ap, psM, psM.ap, AF.Ln, extra_reads=(epsT,), bias=epsT.ap[:, 0:1])
                    act(w3, w3.ap, w3, w3.ap, AF.Exp, scale=-0.5)
                    stt("dve", catT, catT.ap[:, 4 + h, :], w4, w4.ap, gsubT.ap[:, 0:1], w3, w3.ap, ALU.mult, ALU.mult, extra_reads=(gsubT,), part=True)
                pending.append([DEFER_N, fin])
            for pd in pending:
                pd[1]()
            pending.clear()
            scnt += len(its)
            for s in range(4):
                xt = x1t[s % 2]
                r0 = qb * 512 + s * 128
                dma("sp", xt, xt.ap, x1_s, x1_s.ap[r0:r0 + 128, :])
                for nh in range(2):
                    for kc in range(8):
                        mm(psYd, psYd.ap, catT, catT.ap[:, kc, s * 128:(s + 1) * 128], Wo, Wo.ap[:, kc, nh * 512:(nh + 1) * 512], kc == 0, kc == 7)
                    yt = ytmp[nh]
                    tt("dve", yt, yt.ap, psYd, psYd.ap, gate5, gate5.ap[:, nh * 512:(nh + 1) * 512], ALU.mult)
                    tt("pool", xt, xt.ap[:, nh * 512:(nh + 1) * 512], xt, xt.ap[:, nh * 512:(nh + 1) * 512], yt, yt.ap, ALU.add)
                dma("sp", x2_s, x2_s.ap[r0:r0 + 128, :], xt, xt.ap)
        P.barrier()

    ffn_phase(0)
    inproj_phase()
    s5_phase()
    glu_phase()
    attn_phase()
    ffn_phase(1)
    P.emit()
    return nc, P


def host_consts(L):
    f32 = np.float32
    rows = L // 64
    row = np.repeat(np.arange(rows, dtype=f32), 64)
    col = np.tile(np.arange(64, dtype=f32), rows)
    inv_freq = np.power(f32(10000.0), -(np.arange(16, dtype=f32) / f32(16))).astype(f32)
    cos = np.zeros((128, L), f32); sin = np.zeros((128, L), f32)
    for c in range(2):
        for ax, pos in enumerate((row, col)):
            ang = (pos[:, None] * inv_freq[None, :]).astype(f32)
            for half in range(2):
                p0 = c * 64 + ax * 32 + half * 16
                cos[p0:p0 + 16, :] = np.cos(ang).T
                sin[p0:p0 + 16, :] = (np.sin(ang).T) * (f32(-1.0) if half == 0 else f32(1.0))
    ident = np.eye(128, dtype=f32)
    xsw = np.zeros((128, 128), f32)
    for k in range(128):
        xsw[k, (k + 64) % 128] = 1.0
    sgn = np.ones((128, 1), f32); sgn[64:] = -1.0
    return dict(rope_cos=cos, rope_sin=sin, ident=ident, xswap=xsw, sgn=sgn)


_CACHE = {}


def make_in_maps(inputs, L, ncores):
    f32 = np.float32
    A = lambda a: np.ascontiguousarray(np.asarray(a, dtype=f32))
    w_in = A(inputs["w_in"])[0]
    perm = np.arange(512) ^ 16
    w_in_ext = np.ascontiguousarray(np.concatenate([w_in, w_in[:, 512 + perm], w_in[:, 1024 + perm]], axis=1))
    shared = dict(
        c_ctx=A(inputs["c_ctx"]), w_mod=A(inputs["w_mod"])[0], b_mod=A(inputs["b_mod"])[0], norm_g=A(inputs["norm_g"])[0],
        ffn_w_in=A(inputs["ffn_w_in"])[0], ffn_w_out=A(inputs["ffn_w_out"])[0], w_in_ext=w_in_ext, w_out=A(inputs["w_out"])[0],
        ssm_a_re=A(inputs["ssm_a_re"])[0].reshape(64, 64), ssm_a_im=A(inputs["ssm_a_im"])[0].reshape(64, 64),
        ssm_log_dt=A(inputs["ssm_log_dt"])[0].reshape(64),
        ssm_b_re=A(inputs["ssm_b_re"])[0].reshape(64, 64, 16), ssm_b_im=A(inputs["ssm_b_im"])[0].reshape(64, 64, 16),
        ssm_c_re=A(inputs["ssm_c_re"])[0].reshape(64, 16, 64), ssm_c_im=A(inputs["ssm_c_im"])[0].reshape(64, 16, 64),
        ssm_d=A(inputs["ssm_d"])[0].reshape(512), w_glu=A(inputs["w_glu"])[0], b_glu=A(inputs["b_glu"])[0],
        lam_q=A(inputs["lam_q"])[0].reshape(128), lam_k=A(inputs["lam_k"])[0].reshape(128),
        subln_g=A(inputs["subln_g"])[0], final_g=A(inputs["final_g"]),
    )
    shared.update(host_consts(L))
    x = A(inputs["x"]); c = A(inputs["c"]); ctx = A(inputs["ctx"])
    maps = []
    for b in range(ncores):
        m = dict(shared)
        m["x"] = np.ascontiguousarray(x[b, :L]); m["c"] = np.ascontiguousarray(c[b]); m["ctx"] = np.ascontiguousarray(ctx[b])
        maps.append(m)
    return maps


def kernel(**inputs):
    x = np.asarray(inputs["x"])
    B, L, _ = x.shape
    if L not in _CACHE:
        _CACHE[L] = build(L)
    nc, _ = _CACHE[L]
    maps = make_in_maps(inputs, L, B)
    res = run_bass_kernel_spmd(nc, maps, core_ids=list(range(B)))
    out = np.stack([np.asarray(r["out"], dtype=np.float32) for r in res.results], axis=0)
    return out
```

```python
import numpy as np
import concourse.bass as bass
import concourse.mybir as mybir
from concourse.bass_utils import run_bass_kernel_spmd

F32 = mybir.dt.float32
BF16 = mybir.dt.bfloat16
U8 = mybir.dt.uint8
AF = mybir.ActivationFunctionType
ALU = mybir.AluOpType

D = 1024
KD = 8
FF = 2816
NFF = 22
LC = 256
TS = 64
WARM_N = 24
BURST_EVERY = 22
BURST_N = 10
HEAD_WARM_N = 10
ATTACH_WAIT = True
DEFER_N = 20
EPS = 1e-6
LAM_INIT = 0.2


class Buf:
    __slots__ = ("name", "writers", "readers", "prev_readers")

    def __init__(self, name=""):
        self.name = name
        self.writers = []
        self.readers = []
        self.prev_readers = []


class Op:
    __slots__ = ("eng", "fn", "deps", "idx", "is_dma", "sem", "sem_val", "signals", "full")

    def __init__(self, eng, fn, is_dma):
        self.eng = eng
        self.fn = fn
        self.deps = []
        self.is_dma = is_dma
        self.sem = None
        self.sem_val = None
        self.signals = False
        self.idx = 0


class Prog:
    ENGS = ("pe", "act", "dve", "pool", "sp")

    def __init__(self, nc):
        self.nc = nc
        self.ops = {e: [] for e in self.ENGS}
        self.all_ops = []
        self.eng_obj = {"pe": nc.tensor, "act": nc.scalar, "dve": nc.vector,
                        "pool": nc.gpsimd, "sp": nc.sync}

    def op(self, eng, fn, reads=(), writes=(), dma=False, part=False, semkey=None):
        o = Op(eng, fn, dma)
        o.idx = len(self.ops[eng])
        o.full = not (dma or part)
        deps = set()
        for b in reads:
            deps.update(b.writers)
        for b in writes:
            cont = (dma or part) and not b.readers
            if cont:
                deps.update(b.prev_readers)
                deps.update(w for w in b.writers if w.full)
            else:
                deps.update(b.writers)
                deps.update(b.readers)
        best = {}
        out = []
        for d in deps:
            if d is o:
                continue
            if d.is_dma:
                out.append(d)
                continue
            if d.eng == "pe" and eng == "pe" and not dma:
                continue
            if d.eng not in best or best[d.eng].idx < d.idx:
                best[d.eng] = d
        o.deps = out + list(best.values())
        for b in reads:
            b.readers.append(o)
        for b in writes:
            cont = (dma or part) and not b.readers
            if cont:
                b.writers.append(o)
            else:
                b.prev_readers = b.readers
                b.writers = [o]
            b.readers = []
        if dma:
            o.sem = semkey
        self.ops[eng].append(o)
        self.all_ops.append(o)
        return o

    def barrier(self):
        deps = []
        for e in self.ENGS:
            for o in reversed(self.ops[e]):
                if not o.is_dma and o.fn is not None:
                    deps.append(o)
                    break
        dma_last = {}
        for o in self.all_ops:
            if o.is_dma:
                dma_last[id(o.sem)] = o
        deps = deps + list(dma_last.values())
        for e in self.ENGS:
            w = Op(e, None, False)
            w.full = True
            w.deps = [d for d in deps if d.eng != e or d.is_dma]
            self.ops[e].append(w)
            self.all_ops.append(w)

    def emit(self):
        nc = self.nc
        for o in self.all_ops:
            for d in o.deps:
                d.signals = True
        esem = {e: nc.alloc_semaphore(name="sem_" + e) for e in self.ENGS}
        cnt = {e: 0 for e in self.ENGS}
        bufsem = {}
        bufcnt = {}
        for o in self.all_ops:
            if o.is_dma:
                key = id(o.sem)
                if key not in bufsem:
                    bufsem[key] = nc.alloc_semaphore(name="dsem_%d" % len(bufsem))
                    bufcnt[key] = 0
                bufcnt[key] += 16
                o.sem_val = bufcnt[key]
                o.sem = bufsem[key]
            elif o.signals:
                cnt[o.eng] += 1
                o.sem_val = cnt[o.eng]
                o.sem = esem[o.eng]
        self.stats = dict(nsem=len(bufsem) + 5, cnt=cnt, maxdma=max(bufcnt.values()),
                          nops={e: len(self.ops[e]) for e in self.ENGS})
        for e in self.ENGS:
            eng = self.eng_obj[e]
            waited = {}
            for o in self.ops[e]:
                need = {}
                for d in o.deps:
                    k = id(d.sem)
                    if waited.get(k, 0) >= d.sem_val:
                        continue
                    if k not in need or need[k][1] < d.sem_val:
                        need[k] = (d.sem, d.sem_val)
                attach = None
                if e == "pe" and o.fn is not None and len(need) == 1 and ATTACH_WAIT:
                    (k, (s, v)), = need.items()
                    attach = (s, v)
                    waited[k] = v
                else:
                    for k, (s, v) in need.items():
                        eng.wait_ge(s, v)
                        waited[k] = v
                if o.fn is None:
                    continue
                ins = o.fn(eng)
                if attach is not None:
                    ins._wait_ge(attach[0], attach[1])
                if o.is_dma:
                    ins.then_inc(o.sem, 16)
                elif o.signals:
                    ins.then_inc(o.sem, 1)


class T:
    __slots__ = ("ap", "b")

    def __init__(self, ap, name=""):
        self.ap = ap
        self.b = Buf(name)


def build(L):
    LX = LC + L + LC
    NCH = LX // TS
    NS = NCH - LC // TS
    C0B = LC // TS
    LK = L + LC
    NKB = LK // 128
    nc = bass.Bass("TRN2", target_bir_lowering=False)
    P = Prog(nc)

    def din(name, shape, dt=F32):
        return nc.dram_tensor(name, list(shape), dt, kind="ExternalInput").ap()

    DRAM_LIST = []

    def dscr(name, shape, dt):
        t = T(nc.dram_tensor(name, list(shape), dt).ap(), name)
        DRAM_LIST.append(t)
        return t

    x_d = din("x", [L, D]); c_d = din("c", [D]); ctx_d = din("ctx", [LC, D]); cc_d = din("c_ctx", [D])
    wmod_d = din("w_mod", [D, 9 * D]); bmod_d = din("b_mod", [9 * D]); ng_d = din("norm_g", [3, D])
    fw1_d = din("ffn_w_in", [2, D, 2 * FF]); fw2_d = din("ffn_w_out", [2, FF, D])
    win_d = din("w_in_ext", [D, 3072]); wout_d = din("w_out", [D, D])
    are_d = din("ssm_a_re", [64, 64]); aim_d = din("ssm_a_im", [64, 64]); ldt_d = din("ssm_log_dt", [64])
    bre_d = din("ssm_b_re", [64, 64, 16]); bim_d = din("ssm_b_im", [64, 64, 16])
    cre_d = din("ssm_c_re", [64, 16, 64]); cim_d = din("ssm_c_im", [64, 16, 64])
    dsk_d = din("ssm_d", [512]); wglu_d = din("w_glu", [512, 512]); bglu_d = din("b_glu", [512])
    lq_d = din("lam_q", [128]); lk_d = din("lam_k", [128]); sg_d = din("subln_g", [128]); fg_d = din("final_g", [D])
    cos_d = din("rope_cos", [128, L]); sin_d = din("rope_sin", [128, L])
    idf_d = din("ident", [128, 128]); xsw_d = din("xswap", [128, 128]); sgn_d = din("sgn", [128, 1])
    out_d = T(nc.dram_tensor("out", [L, D], F32, kind="ExternalOutput").ap(), "out")

    mod_s = dscr("mod_s", [2, 9 * D], F32)
    x1_s = dscr("x1_s", [L + LC, D], F32)
    x2_s = dscr("x2_s", [L, D], F32)
    uT_s = dscr("uT_s", [512, LX], BF16)
    qT_s = dscr("qT_s", [512, L], BF16)
    kT_s = dscr("kT_s", [512, LK], BF16)
    v_s = dscr("v_s", [LK, 512], BF16)
    gT_s = dscr("gT_s", [512, L], BF16)
    sgT_s = dscr("sgT_s", [512, L], BF16)

    BIG = 212000
    DRAM_LIST.append(out_d)
    big = nc.alloc_sbuf_tensor("big", [128, BIG], U8)
    ps01 = nc.alloc_psum_tensor("ps01", [128, 1024], F32)
    pst = [ps01[:, 0:512], ps01[:, 512:1024]] + [nc.alloc_psum_tensor("ps%d" % i, [128, 512], F32)[:, :] for i in range(2, 8)]
    psT2 = ps01[:, :].bitcast(BF16).rearrange("p (k t) -> p k t", t=256)
    cur = [0]

    def sb(shape, dt, name=""):
        esz = 4 if dt == F32 else 2
        n = int(np.prod(shape[1:])) * esz
        v = big[0:shape[0], cur[0]:cur[0] + n].bitcast(dt)
        cur[0] += (n + 63) // 64 * 64
        assert cur[0] <= BIG, ("sbuf overflow", name, cur[0])
        if len(shape) == 3:
            v = v.rearrange("p (a b) -> p a b", b=shape[2])
        elif len(shape) == 4:
            v = v.rearrange("p (a b c) -> p a b c", b=shape[2], c=shape[3])
        return T(v, name)

    PS = [T(pst[i], "ps%d" % i) for i in range(8)]

    idF = sb([128, 128], F32, "idF"); idB = sb([128, 128], BF16, "idB")
    xsw = sb([128, 128], F32, "xsw"); sgn = sb([128, 1], F32, "sgn")
    epsT = sb([128, 1], F32, "eps")
    modT = sb([128, 72, 2], F32, "modT")
    gmL = sb([128, 3, 8], F32, "gmL"); gmC = sb([128, 3, 8], F32, "gmC")
    ngT = sb([128, 3, 8], F32, "ngT")
    stat = sb([128, 16], F32, "stat")
    PERSIST_END = cur[0]

    DRAM_T = set(id(t) for t in DRAM_LIST)

    def dma(eng, out_t, out_ap, in_t, in_ap, **kw):
        reads = [in_t.b] if in_t is not None else []
        key = in_t.b if id(out_t) in DRAM_T else out_t.b
        P.op(eng, lambda e, o=out_ap, i=in_ap, kw=kw: e.dma_start(out=o, in_=i, **kw),
             reads=reads, writes=[out_t.b], dma=True, semkey=key)

    def mm(out_t, out_ap, lhsT_t, lhsT, rhs_t, rhs, start, stop):
        P.op("pe", lambda e, o=out_ap, l=lhsT, r=rhs, s=start, t=stop: e.matmul(o, lhsT=l, rhs=r, start=s, stop=t),
             reads=[lhsT_t.b, rhs_t.b], writes=[out_t.b], part=True)

    def act(out_t, out_ap, in_t, in_ap, func, extra_reads=(), extra_writes=(), part=False, **kw):
        P.op("act", lambda e, o=out_ap, i=in_ap, f=func, kw=kw: e.activation(out=o, in_=i, func=f, **kw),
             reads=[in_t.b] + [t.b for t in extra_reads], writes=[out_t.b] + [t.b for t in extra_writes], part=part)

    def ts(eng, out_t, out_ap, in_t, in_ap, s1, s2, op0, op1=None, extra_reads=(), part=False):
        if op1 is None:
            f = lambda e, o=out_ap, i=in_ap: e.tensor_scalar(out=o, in0=i, scalar1=s1, scalar2=None, op0=op0)
        else:
            f = lambda e, o=out_ap, i=in_ap: e.tensor_scalar(out=o, in0=i, scalar1=s1, scalar2=s2, op0=op0, op1=op1)
        P.op(eng, f, reads=[in_t.b] + [t.b for t in extra_reads], writes=[out_t.b], part=part)

    def tt(eng, out_t, out_ap, a_t, a_ap, b_t, b_ap, op, part=False):
        P.op(eng, lambda e, o=out_ap, a=a_ap, b=b_ap: e.tensor_tensor(out=o, in0=a, in1=b, op=op),
             reads=[a_t.b, b_t.b], writes=[out_t.b], part=part)

    def stt(eng, out_t, out_ap, a_t, a_ap, scalar, b_t, b_ap, op0, op1, extra_reads=(), part=False):
        P.op(eng, lambda e, o=out_ap, a=a_ap, b=b_ap: e.scalar_tensor_tensor(out=o, in0=a, scalar=scalar, in1=b, op0=op0, op1=op1),
             reads=[a_t.b, b_t.b] + [t.b for t in extra_reads], writes=[out_t.b], part=part)

    def cp(eng, out_t, out_ap, in_t, in_ap, part=False):
        if eng == "act":
            P.op("act", lambda e, o=out_ap, i=in_ap: e.copy(out=o, in_=i), reads=[in_t.b], writes=[out_t.b], part=part)
        else:
            P.op(eng, lambda e, o=out_ap, i=in_ap: e.tensor_copy(out=o, in_=i), reads=[in_t.b], writes=[out_t.b], part=part)

    def memset(eng, t, ap, val, part=False):
        P.op(eng, lambda e, a=ap: e.memset(a, val), writes=[t.b], part=part)

    def recip(out_t, out_ap, in_t, in_ap):
        P.op("dve", lambda e, o=out_ap, i=in_ap: e.reciprocal(out=o, in_=i), reads=[in_t.b], writes=[out_t.b])

    NC_ = dict(allow_slow_non_contiguous=True)
    dma("sp", idF, idF.ap, None, idf_d)
    dma("sp", xsw, xsw.ap, None, xsw_d)
    dma("sp", sgn, sgn.ap, None, sgn_d)
    dma("sp", ngT, ngT.ap, None, ng_d.rearrange("i (k p) -> p i k", p=128), **NC_)
    cp("dve", idB, idB.ap, idF, idF.ap)
    memset("dve", epsT, epsT.ap, EPS)

    cur[0] = PERSIST_END
    cT = sb([128, 8, 2], F32, "cT"); scT = sb([128, 8, 2], F32, "scT")
    modrow = sb([2, 9 * D], F32, "modrow"); bmodr = sb([2, 9 * D], F32, "bmodr")
    wst = [sb([128, 8, 512], F32, "wst%d" % i) for i in range(2)]
    id2 = idF.ap[0:2, 0:2]
    dma("sp", cT, cT.ap[:, :, 0], None, c_d.rearrange("(k p) -> p k", p=128), **NC_)
    dma("sp", cT, cT.ap[:, :, 1], None, cc_d.rearrange("(k p) -> p k", p=128), **NC_)
    dma("sp", bmodr, bmodr.ap, None, bmod_d.partition_broadcast(2))
    act(scT, scT.ap, cT, cT.ap, AF.Silu)
    for nb in range(18):
        w = wst[nb % 2]
        dma("sp", w, w.ap, None, wmod_d[:, nb * 512:(nb + 1) * 512].rearrange("(k p) n -> p k n", p=128))
        ps = PS[nb % 2]
        for k in range(8):
            mm(ps, ps.ap[0:2, :], scT, scT.ap[:, k, :], w, w.ap[:, k, :], k == 0, k == 7)
        tt("dve", modrow, modrow.ap[:, nb * 512:(nb + 1) * 512], ps, ps.ap[0:2, :], bmodr, bmodr.ap[:, nb * 512:(nb + 1) * 512], ALU.add, part=True)
    dma("sp", mod_s, mod_s.ap, modrow, modrow.ap)
    for j in range(72):
        mm(PS[2], PS[2].ap[:, 2 * j:2 * j + 2], modrow, modrow.ap[:, j * 128:(j + 1) * 128], idF, id2, True, True)
    cp("dve", modT, modT.ap, PS[2], PS[2].ap[:, 0:144].rearrange("p (a b) -> p a b", b=2))
    for i in range(3):
        for (gm, col) in ((gmL, 0), (gmC, 1)):
            stt("dve", gm, gm.ap[:, i, :], modT, modT.ap[:, (3 * i + 1) * 8:(3 * i + 2) * 8, col], 1.0,
                ngT, ngT.ap[:, i, :], ALU.add, ALU.mult, part=True)
    P.barrier()

    def sh_ap(i, col, k):
        return modT.ap[:, 3 * i * 8 + k, col:col + 1]

    def make_front(nslots=2):
        xblk = [sb([128, 2, D], F32, "xblk%d" % i) for i in range(nslots)]
        xs = [sb([128, D], BF16, "xs%d" % i) for i in range(2)]
        junk = sb([128, D], BF16, "junk")
        hT = sb([128, 8, 256], BF16, "hT")
        return dict(xblk=xblk, xs=xs, junk=junk, hT=hT, n=0, hT2=None)

    psTr = T(None, "psTr")

    def front_load(fr, src_t, src_ap):
        n = fr["n"]; fr["n"] += 1
        xb = fr["xblk"][n % len(fr["xblk"])]
        dma("sp", xb, xb.ap, src_t, src_ap.rearrange("(s p) d -> p s d", p=128))
        return xb

    def front(fr, xb, gm, i, col, hT=None):
        for s in range(2):
            xs = fr["xs"][s]
            act(fr["junk"], fr["junk"].ap, xb, xb.ap[:, s, :], AF.Square, accum_out=stat.ap[:, s:s + 1], extra_writes=(stat,))
            act(stat, stat.ap[:, 2 + s:3 + s], stat, stat.ap[:, s:s + 1], AF.Sqrt, extra_reads=(epsT,), scale=1.0 / D, bias=epsT.ap[:, 0:1])
            recip(stat, stat.ap[:, 4 + s:5 + s], stat, stat.ap[:, 2 + s:3 + s])
            ts("dve", xs, xs.ap, xb, xb.ap[:, s, :], stat.ap[:, 4 + s:5 + s], None, ALU.mult, extra_reads=(stat,))
            for k in range(8):
                P.op("pe", lambda e, o=psT2[:, k, s * 128:(s + 1) * 128], a=xs.ap[:, k * 128:(k + 1) * 128]: e.transpose(o, a, idB.ap),
                     reads=[xs.b, idB.b], writes=[psTr.b], part=True)
        if hT is None:
            hT = fr["hT"]
        for k in range(8):
            act(hT, hT.ap[:, k, :], psTr, psT2[:, k, :], AF.Identity, extra_reads=(gm, modT), part=True,
                scale=gm.ap[:, i, k:k + 1], bias=sh_ap(i, col, k))
        return xb

    def ffn_phase(which):
        cur[0] = PERSIST_END
        W1 = sb([128, 8, 2 * FF], BF16, "W1"); W2 = sb([128, NFF, D], BF16, "W2")
        gate = sb([128, D], F32, "gate"); gatec = sb([128, D], F32, "gatec")
        fgb = gatec
        fr = make_front()
        hTs = [fr["hT"], sb([128, 8, 256], BF16, "hTb")]
        GT = sb([128, NFF, 256], BF16, "GT")
        sg = [sb([128, 256], F32, "sg%d" % i) for i in range(2)]
        ytmp = [sb([128, 512], F32, "ytmp%d" % i) for i in range(2)]
        stage_off = cur[0]
        stg = [sb([128, 1408], F32, "stg%d" % i) for i in range(2)]
        gi = 2 if which == 0 else 8
        dma("sp", gate, gate.ap, mod_s, mod_s.ap[0:1, gi * D:(gi + 1) * D].partition_broadcast(128).rearrange("p a d -> p (a d)"))
        ts("dve", gate, gate.ap, gate, gate.ap, 0.5, None, ALU.mult)
        if which == 0:
            dma("sp", gatec, gatec.ap, mod_s, mod_s.ap[1:2, 2 * D:3 * D].partition_broadcast(128).rearrange("p a d -> p (a d)"))
            ts("dve", gatec, gatec.ap, gatec, gatec.ap, 0.5, None, ALU.mult)
        else:
            dma("sp", fgb, fgb.ap, None, fg_d.partition_broadcast(128))
        engs = ("dve", "pool", "act")
        n = 0
        for k in range(8):
            for hh in range(4):
                st = stg[n % 2]
                dma("sp", st, st.ap, None, fw1_d[which, k * 128:(k + 1) * 128, hh * 1408:(hh + 1) * 1408])
                cp(engs[n % 3], W1, W1.ap[:, k, hh * 1408:(hh + 1) * 1408], st, st.ap, part=True)
                n += 1
        for m2 in range(NFF):
            st = stg[n % 2]
            dma("sp", st, st.ap[:, 0:D], None, fw2_d[which, m2 * 128:(m2 + 1) * 128, :])
            cp(engs[n % 3], W2, W2.ap[:, m2, :], st, st.ap[:, 0:D], part=True)
            n += 1
        if which == 0:
            blocks = [(None, x_d, x1_s, b * 256, b * 256, gmL, 0, gate) for b in range(L // 256)]
            blocks.append((None, ctx_d, x1_s, 0, L, gmC, 1, gatec))
        else:
            blocks = [(x2_s, x2_s.ap, out_d, b * 256, b * 256, gmL, 0, gate) for b in range(L // 256)]
        ii = 0 if which == 0 else 2
        xbs = {0: front_load(fr, blocks[0][0], blocks[0][1][blocks[0][3]:blocks[0][3] + 256, :])}
        for bi, (src_t, src, dst_t, r0, w0, gm, col, gt) in enumerate(blocks):
            if bi + 1 < len(blocks):
                nb_ = blocks[bi + 1]
                xbs[bi + 1] = front_load(fr, nb_[0], nb_[1][nb_[3]:nb_[3] + 256, :])
            xb = xbs.pop(bi)
            if bi == 0:
                front(fr, xb, gm, ii, col, hTs[0])
            hT = hTs[bi % 2]
            for m in range(NFF):
                pg = PS[2 + 2 * (m % 2)]; pu = PS[3 + 2 * (m % 2)]
                for k in range(8):
                    mm(pg, pg.ap[:, 0:256], W1, W1.ap[:, k, m * 128:(m + 1) * 128], hT, hT.ap[:, k, :], k == 0, k == 7)
                for k in range(8):
                    mm(pu, pu.ap[:, 0:256], W1, W1.ap[:, k, FF + m * 128:FF + (m + 1) * 128], hT, hT.ap[:, k, :], k == 0, k == 7)
                s_ = sg[m % 2]
                act(s_, s_.ap, pg, pg.ap[:, 0:256], AF.Silu)
                tt("dve", GT, GT.ap[:, m, :], s_, s_.ap, pu, pu.ap[:, 0:256], ALU.mult, part=True)
            if bi + 1 < len(blocks):
                nb_ = blocks[bi + 1]
                front(fr, xbs[bi + 1], nb_[5], ii, nb_[6], hTs[(bi + 1) % 2])
            q = 0
            for s in range(2):
                for nh in range(2):
                    py = PS[6 + (q % 2)]
                    for m in range(NFF):
                        mm(py, py.ap, GT, GT.ap[:, m, s * 128:(s + 1) * 128], W2, W2.ap[:, m, nh * 512:(nh + 1) * 512], m == 0, m == NFF - 1)
                    yt = ytmp[q % 2]
                    tt("dve", yt, yt.ap, py, py.ap, gt, gt.ap[:, nh * 512:(nh + 1) * 512], ALU.mult)
                    tt("pool", xb, xb.ap[:, s, nh * 512:(nh + 1) * 512], xb, xb.ap[:, s, nh * 512:(nh + 1) * 512], yt, yt.ap, ALU.add)
                    q += 1
            if which == 1:
                for s in range(2):
                    act(fr["junk"], fr["junk"].ap, xb, xb.ap[:, s, :], AF.Square, accum_out=stat.ap[:, 8 + s:9 + s], extra_writes=(stat,))
                    act(stat, stat.ap[:, 10 + s:11 + s], stat, stat.ap[:, 8 + s:9 + s], AF.Sqrt, extra_reads=(epsT,), scale=1.0 / D, bias=epsT.ap[:, 0:1])
                    recip(stat, stat.ap[:, 12 + s:13 + s], stat, stat.ap[:, 10 + s:11 + s])
                    stt("dve", xb, xb.ap[:, s, :], xb, xb.ap[:, s, :], stat.ap[:, 12 + s:13 + s], fgb, fgb.ap, ALU.mult, ALU.mult, extra_reads=(stat,))
            dma("sp", dst_t, dst_t.ap[w0:w0 + 256, :].rearrange("(s p) d -> p s d", p=128), xb, xb.ap)
        P.barrier()

    def inproj_phase():
        cur[0] = PERSIST_END
        Win = sb([128, 8, 3072], BF16, "Win")
        fr = make_front()
        hTsB = [fr["hT"], sb([128, 8, 256], BF16, "hTbB")]
        stg = [sb([128, 3072], F32, "stgB%d" % i) for i in range(2)]
        cosb = [sb([128, 256], F32, "cos%d" % i) for i in range(2)]
        sinb = [sb([128, 256], F32, "sin%d" % i) for i in range(2)]
        t1 = [sb([128, 256], F32, "t1_%d" % i) for i in range(2)]
        t2 = [sb([128, 256], F32, "t2_%d" % i) for i in range(2)]
        uo = [sb([128, 4, 256], BF16, "uo%d" % i) for i in range(2)]
        qo = [sb([128, 4, 256], BF16, "qo%d" % i) for i in range(2)]
        ko = [sb([128, 4, 256], BF16, "ko%d" % i) for i in range(2)]
        vo = [sb([128, 2, 512], BF16, "vo%d" % i) for i in range(2)]
        engs = ("dve", "pool", "act")
        for k in range(8):
            st = stg[k % 2]
            dma("sp", st, st.ap, None, win_d[k * 128:(k + 1) * 128, :])
            cp(engs[k % 3], Win, Win.ap[:, k, :], st, st.ap, part=True)
        nblk = L // 256

        def pre(b):
            r0_ = L if b == nblk else b * 256
            xb_ = front_load(fr, x1_s, x1_s.ap[r0_:r0_ + 256, :])
            if b != nblk:
                dma("sp", cosb[b % 2], cosb[b % 2].ap, None, cos_d[:, b * 256:(b + 1) * 256])
                dma("sp", sinb[b % 2], sinb[b % 2].ap, None, sin_d[:, b * 256:(b + 1) * 256])
            return xb_
        xbs = {0: pre(0)}
        for b in range(nblk + 1):
            isctx = b == nblk
            r0 = L if isctx else b * 256
            gm = gmC if isctx else gmL
            if b + 1 <= nblk:
                xbs[b + 1] = pre(b + 1)
            if b == 0:
                front(fr, xbs[0], gm, 1, 0, hTsB[0])
            xbs.pop(b)
            hT = hTsB[b % 2]
            sl = b % 2
            pcount = 0
            for j in range(4):
                ps = PS[2 + (pcount % 4)]; pcount += 1
                for k in range(8):
                    mm(ps, ps.ap[:, 0:256], Win, Win.ap[:, k, j * 128:(j + 1) * 128], hT, hT.ap[:, k, :], k == 0, k == 7)
                cp("act", uo[sl], uo[sl].ap[:, j, :], ps, ps.ap[:, 0:256], part=True)
            if isctx:
                dma("sp", uT_s, uT_s.ap[:, 0:LC].rearrange("(j p) t -> p j t", p=128), uo[sl], uo[sl].ap)
                dma("sp", uT_s, uT_s.ap[:, LC + L:LX].rearrange("(j p) t -> p j t", p=128), uo[sl], uo[sl].ap)
            else:
                dma("sp", uT_s, uT_s.ap[:, LC + b * 256:LC + (b + 1) * 256].rearrange("(j p) t -> p j t", p=128), uo[sl], uo[sl].ap)
            for (base, swb, ot, dst, isq) in ((512, 2048, qo, qT_s, True), (1024, 2560, ko, kT_s, False)):
                if isq and isctx:
                    continue
                for j in range(4):
                    pa = PS[2 + (pcount % 4)]; pcount += 1
                    for k in range(8):
                        mm(pa, pa.ap[:, 0:256], Win, Win.ap[:, k, base + j * 128:base + (j + 1) * 128], hT, hT.ap[:, k, :], k == 0, k == 7)
                    if isctx:
                        cp("act", ot[sl], ot[sl].ap[:, j, :], pa, pa.ap[:, 0:256], part=True)
                        continue
                    pb = PS[2 + (pcount % 4)]; pcount += 1
                    for k in range(8):
                        mm(pb, pb.ap[:, 0:256], Win, Win.ap[:, k, swb + j * 128:swb + (j + 1) * 128], hT, hT.ap[:, k, :], k == 0, k == 7)
                    a1 = t1[j % 2]; a2 = t2[j % 2]
                    tt("dve", a1, a1.ap, pa, pa.ap[:, 0:256], cosb[sl], cosb[sl].ap, ALU.mult)
                    tt("dve", a2, a2.ap, pb, pb.ap[:, 0:256], sinb[sl], sinb[sl].ap, ALU.mult)
                    tt("pool", ot[sl], ot[sl].ap[:, j, :], a1, a1.ap, a2, a2.ap, ALU.add, part=True)
                c0 = L if isctx else b * 256
                dma("sp", dst, dst.ap[:, c0:c0 + 256].rearrange("(j p) t -> p j t", p=128), ot[sl], ot[sl].ap)
            if b + 1 <= nblk:
                nctx = (b + 1) == nblk
                front(fr, xbs[b + 1], gmC if nctx else gmL, 1, 1 if nctx else 0, hTsB[(b + 1) % 2])
            for s in range(2):
                ps = PS[6 + s]
                for k in range(8):
                    mm(ps, ps.ap, hT, hT.ap[:, k, s * 128:(s + 1) * 128], Win, Win.ap[:, k, 1536:2048], k == 0, k == 7)
                cp("act", vo[sl], vo[sl].ap[:, s, :], ps, ps.ap, part=True)
            dma("sp", v_s, v_s.ap[r0:r0 + 256, :].rearrange("(s p) d -> p s d", p=128), vo[sl], vo[sl].ap)
        P.barrier()

    def s5_phase():
        cur[0] = PERSIST_END
        NR = 64
        Bpad = sb([128, NR, 128], BF16, "Bpad"); Cpad = sb([128, NR, 128], BF16, "Cpad"); ABpad = sb([128, NR, 128], BF16, "ABpad")
        PW = sb([128, 10, 2, NR], F32, "PW")
        dsk = sb([128, 4], F32, "dsk")
        setup_off = cur[0]
        arow = sb([64, 2, 128], F32, "arow")
        prm = sb([128, 16, NR], F32, "prm")
        Ball = sb([64, 2, NR * 16], F32, "Ball")
        Bbar = sb([64, 2, NR * 16], F32, "Bbar")
        btmp = sb([64, 2, NR * 16], F32, "btmp")
        Bbar2 = sb([64, 2, NR * 16], F32, "Bbar2")
        BP = [sb([64, 2, 128], F32, "BP%d" % i) for i in range(2)]
        BP2 = [sb([64, 2, 128], F32, "BP2_%d" % i) for i in range(2)]
        Call = sb([16, NR, 128], F32, "Call")
        dma("sp", arow, arow.ap[:, 0, 0:64], None, are_d); dma("sp", arow, arow.ap[:, 0, 64:128], None, are_d)
        dma("sp", arow, arow.ap[:, 1, 0:64], None, aim_d); dma("sp", arow, arow.ap[:, 1, 64:128], None, aim_d)
        dma("sp", prm, prm.ap[:, 2, :], None, ldt_d.partition_broadcast(128))
        dma("sp", dsk, dsk.ap, None, dsk_d.rearrange("(c p) -> p c", p=128), **NC_)
        dma("sp", Ball, Ball.ap[:, 0, :].rearrange("p (r h) -> p r h", h=16), None, bre_d.rearrange("r p h -> p r h"))
        dma("sp", Ball, Ball.ap[:, 1, :].rearrange("p (r h) -> p r h", h=16), None, bim_d.rearrange("r p h -> p r h"))
        dma("sp", Call, Call.ap[:, :, 0:64], None, cre_d.rearrange("r h p -> h r p"))
        dma("sp", Call, Call.ap[:, :, 64:128], None, cim_d.rearrange("r h p -> h r p"))
        for i in range(2):
            mm(PS[0], PS[0].ap[:, i * 64:(i + 1) * 64], arow, arow.ap[:, i, :], idF, idF.ap[0:64, 0:64], True, True)
        cp("dve", prm, prm.ap[:, 0:2, :], PS[0], PS[0].ap[:, 0:128].rearrange("p (a b) -> p a b", b=64))
        Pm = lambda i: prm.ap[:, i, :]
        ARE, AIM, LDT, DT, XR, ANG, MAG, T0, T1_, COS, SIN, AL, BE = range(13)

        def e_tt(o, a, b, op):
            tt("dve", prm, Pm(o), prm, Pm(a), prm, Pm(b), op)

        def e_ts(o, a, s1, s2, op0, op1=None):
            ts("dve", prm, Pm(o), prm, Pm(a), s1, s2, op0, op1)

        act(prm, Pm(DT), prm, Pm(LDT), AF.Exp)
        e_tt(XR, DT, ARE, ALU.mult)
        e_tt(ANG, DT, AIM, ALU.mult)
        e_ts(MAG, XR, 1.0 / 7.0, 1.0, ALU.mult, ALU.add)
        for kk in (6.0, 5.0, 4.0, 3.0, 2.0, 1.0):
            e_tt(MAG, MAG, XR, ALU.mult)
            e_ts(MAG, MAG, 1.0 / kk, 1.0, ALU.mult, ALU.add)
        TWO_PI = float(2 * np.pi)
        for (dst, shift) in ((SIN, 0.0), (COS, float(np.pi / 2))):
            e_ts(T0, ANG, shift, None, ALU.add)
            e_ts(T1_, T0, 1.0 / TWO_PI, 12582912.0, ALU.mult, ALU.add)
            e_ts(T1_, T1_, -12582912.0, None, ALU.add)
            stt("dve", prm, Pm(T0), prm, Pm(T1_), -TWO_PI, prm, Pm(T0), ALU.mult, ALU.add)
            act(prm, Pm(dst), prm, Pm(T0), AF.Sin)
        e_tt(AL, MAG, COS, ALU.mult)
        e_tt(BE, MAG, SIN, ALU.mult)
        ZR, DEN, CR, CI = 13, 14, 15, 3
        e_ts(ZR, AL, -1.0, None, ALU.add)
        e_tt(DEN, ARE, ARE, ALU.mult); e_tt(T0, AIM, AIM, ALU.mult); e_tt(DEN, DEN, T0, ALU.add)
        recip(prm, Pm(DEN), prm, Pm(DEN))
        e_tt(CR, ZR, ARE, ALU.mult); e_tt(T0, BE, AIM, ALU.mult); e_tt(CR, CR, T0, ALU.add); e_tt(CR, CR, DEN, ALU.mult)
        e_tt(CI, BE, ARE, ALU.mult); e_tt(T0, ZR, AIM, ALU.mult); e_tt(CI, CI, T0, ALU.subtract); e_tt(CI, CI, DEN, ALU.mult)
        cp("dve", PW, PW.ap[:, 0, 0, :], prm, Pm(AL))
        ts("dve", PW, PW.ap[:, 0, 1, :], prm, Pm(BE), sgn.ap[:, 0:1], None, ALU.mult, extra_reads=(sgn,))
        cur_pow = 1
        kidx = 1
        while kidx < 9:
            e_tt(T0, AL, AL, ALU.mult); e_tt(T1_, BE, BE, ALU.mult)
            e_tt(BE, AL, BE, ALU.mult); e_ts(BE, BE, 2.0, None, ALU.mult)
            e_tt(AL, T0, T1_, ALU.subtract)
            cur_pow *= 2
            if cur_pow == 2:
                cp("dve", PW, PW.ap[:, 9, 0, :], prm, Pm(AL))
                ts("dve", PW, PW.ap[:, 9, 1, :], prm, Pm(BE), sgn.ap[:, 0:1], None, ALU.mult, extra_reads=(sgn,))
            if cur_pow >= TS:
                cp("dve", PW, PW.ap[:, kidx, 0, :], prm, Pm(AL))
                ts("dve", PW, PW.ap[:, kidx, 1, :], prm, Pm(BE), sgn.ap[:, 0:1], None, ALU.mult, extra_reads=(sgn,))
                kidx += 1
        crb = prm.ap[0:64, CR, :].unsqueeze(2).broadcast_to([64, NR, 16])
        cib = prm.ap[0:64, CI, :].unsqueeze(2).broadcast_to([64, NR, 16])
        B3 = lambda t, i: t.ap[:, i, :].rearrange("p (r h) -> p r h", h=16)
        tt("dve", Bbar, B3(Bbar, 0), Ball, B3(Ball, 0), prm, crb, ALU.mult)
        tt("dve", btmp, B3(btmp, 0), Ball, B3(Ball, 1), prm, cib, ALU.mult)
        tt("dve", Bbar, B3(Bbar, 0), Bbar, B3(Bbar, 0), btmp, B3(btmp, 0), ALU.subtract)
        tt("dve", Bbar, B3(Bbar, 1), Ball, B3(Ball, 1), prm, crb, ALU.mult)
        tt("dve", btmp, B3(btmp, 1), Ball, B3(Ball, 0), prm, cib, ALU.mult)
        tt("dve", Bbar, B3(Bbar, 1), Bbar, B3(Bbar, 1), btmp, B3(btmp, 1), ALU.add)
        alb = PW.ap[0:64, 0, 0, :].unsqueeze(2).broadcast_to([64, NR, 16])
        beb = PW.ap[0:64, 0, 1, :].unsqueeze(2).broadcast_to([64, NR, 16])
        tt("dve", Bbar2, B3(Bbar2, 0), Bbar, B3(Bbar, 0), PW, alb, ALU.mult)
        tt("dve", btmp, B3(btmp, 0), Bbar, B3(Bbar, 1), PW, beb, ALU.mult)
        tt("dve", Bbar2, B3(Bbar2, 0), Bbar2, B3(Bbar2, 0), btmp, B3(btmp, 0), ALU.subtract)
        tt("dve", Bbar2, B3(Bbar2, 1), Bbar, B3(Bbar, 1), PW, alb, ALU.mult)
        tt("dve", btmp, B3(btmp, 1), Bbar, B3(Bbar, 0), PW, beb, ALU.mult)
        tt("dve", Bbar2, B3(Bbar2, 1), Bbar2, B3(Bbar2, 1), btmp, B3(btmp, 1), ALU.add)
        ts("dve", Call, Call.ap[:, :, 64:128], Call, Call.ap[:, :, 64:128], -1.0, None, ALU.mult)
        memset("pool", Cpad, Cpad.ap, 0.0)
        for r in range(NR):
            gp = r % 8
            bp = BP[r % 2]
            memset("pool", bp, bp.ap, 0.0)
            for i in range(2):
                cp("pool", bp, bp.ap[:, i, gp * 16:(gp + 1) * 16], Bbar, Bbar.ap[:, i, r * 16:(r + 1) * 16], part=True)
            ps = PS[1 + (r % 2)]
            for i in range(2):
                mm(ps, ps.ap[:, i * 64:(i + 1) * 64], bp, bp.ap[:, i, :], idF, idF.ap[0:64, 0:64], True, True)
            cp("act", Bpad, Bpad.ap[:, r, :], ps, ps.ap[:, 0:128], part=True)
            bp2 = BP2[r % 2]
            memset("pool", bp2, bp2.ap, 0.0)
            for i in range(2):
                cp("pool", bp2, bp2.ap[:, i, gp * 16:(gp + 1) * 16], Bbar2, Bbar2.ap[:, i, r * 16:(r + 1) * 16], part=True)
            ps2 = PS[5 + (r % 2)]
            for i in range(2):
                mm(ps2, ps2.ap[:, i * 64:(i + 1) * 64], bp2, bp2.ap[:, i, :], idF, idF.ap[0:64, 0:64], True, True)
            cp("act", ABpad, ABpad.ap[:, r, :], ps2, ps2.ap[:, 0:128], part=True)
            pc = PS[3 + (r % 2)]
            mm(pc, pc.ap[:, 0:16], Call, Call.ap[:, r, :], idF, idF.ap[0:16, 0:16], True, True)
            cp("dve", Cpad, Cpad.ap[:, r, gp * 16:(gp + 1) * 16], pc, pc.ap[:, 0:16], part=True)
        P.barrier()
        cur[0] = setup_off
        utok = sb([128, LX], BF16, "utok")
        uJ = sb([128, TS, NCH], BF16, "uJ")
        yac = sb([128, TS, NCH], F32, "yac")
        Am = [sb([128, 128], F32, "Am%d" % i) for i in range(8)]
        A2m = [sb([128, 128], F32, "A2m%d" % i) for i in range(8)]
        Ap = [sb([128, 128], F32, "Ap%d" % i) for i in range(2)]
        hA = [sb([128, 2, NS], F32, "hA%d" % i) for i in range(4)]
        hB = [sb([128, 2, NS], F32, "hB%d" % i) for i in range(4)]
        hb16 = [[sb([128, 2, NS], BF16, "hb%d_%d" % (i, j)) for j in range(2)] for i in range(4)]
        HcU = [sb([128, 2, NS], F32, "Hc%d" % i) for i in range(4)]
        Hc = [T(HcU[g // 2].ap[:, g % 2, :], "Hcr%d" % g) for g in range(8)]
        gel = [sb([128, L // 2], F32, "gel%d" % i) for i in range(2)]
        gout = sb([128, L // 2], BF16, "gout")
        psH = [T(PS[u].ap[:, 0:2 * NS].rearrange("p (a b) -> p a b", b=NS), "psH%d" % u) for u in range(4)]
        psC = [T(PS[4 + i].ap[:, 0:NS], "psC%d" % i) for i in range(2)]
        psY = [T(PS[6 + i].ap[:, 0:NS], "psY%d" % i) for i in range(2)]
        evn = [0]

        def evac(out_t, out_ap, in_t, in_ap):
            e = ("act", "dve")[evn[0] % 2]; evn[0] += 1
            cp(e, out_t, out_ap, in_t, in_ap)

        def build_A(dst, k, r):
            ts("dve", dst, dst.ap, idF, idF.ap, PW.ap[:, k, 0, r:r + 1], None, ALU.mult, extra_reads=(PW,))
            stt("dve", dst, dst.ap, xsw, xsw.ap, PW.ap[:, k, 1, r:r + 1], dst, dst.ap, ALU.mult, ALU.add, extra_reads=(PW,))

        def readout(j, st, rows, c0, ycopy):
            py = psY[st % 2]
            for g in range(8):
                hb = hb16[g // 2][st % 2]
                mm(py, py.ap, Cpad, Cpad.ap[:, rows[g], :], hb, hb.ap[:, g % 2, :], g == 0, g == 7)
            if ycopy:
                evac(yac, yac.ap[:, j, c0:c0 + NS], py, py.ap)
            else:
                tt("dve", yac, yac.ap[:, j, c0:c0 + NS], yac, yac.ap[:, j, c0:c0 + NS], py, py.ap, ALU.add)

        for gc in range(4):
            dma("sp", utok, utok.ap, uT_s, uT_s.ap[gc * 128:(gc + 1) * 128, :])
            cp("pool", uJ, uJ.ap, utok, utok.ap.rearrange("p (c j) -> p j c", j=TS))
            for dr in range(2):
                c0 = 0 if dr == 0 else C0B
                rows = [dr * 32 + gc * 8 + g for g in range(8)]
                ycopy = (dr == 0)
                for g in range(8):
                    build_A(Am[g], 0, rows[g])
                    build_A(A2m[g], 9, rows[g])
                jorder = list(range(TS)) if dr == 0 else list(range(TS - 1, -1, -1))
                for pss in range(2):
                    if pss == 1:
                        for g in range(8):
                            H = Hc[g]
                            k = 1
                            sft = 1
                            while sft < NS:
                                ap_ = Ap[(g + k) % 2]
                                build_A(ap_, k, rows[g])
                                pc = psC[(g + k) % 2]
                                if dr == 0:
                                    mm(pc, pc.ap[:, 0:NS - sft], ap_, ap_.ap, H, H.ap[:, 0:NS - sft], True, True)
                                    tt("dve", H, H.ap[:, sft:NS], H, H.ap[:, sft:NS], pc, pc.ap[:, 0:NS - sft], ALU.add)
                                else:
                                    mm(pc, pc.ap[:, 0:NS - sft], ap_, ap_.ap, H, H.ap[:, sft:NS], True, True)
                                    tt("dve", H, H.ap[:, 0:NS - sft], H, H.ap[:, 0:NS - sft], pc, pc.ap[:, 0:NS - sft], ALU.add)
                                sft *= 2
                                k += 1
                        for u in range(4):
                            h0 = hA[u]
                            memset("pool", h0, h0.ap, 0.0)
                            for r2 in range(2):
                                H = Hc[2 * u + r2]
                                if dr == 0:
                                    cp("pool", h0, h0.ap[:, r2, 1:NS], H, H.ap[:, 0:NS - 1], part=True)
                                else:
                                    cp("pool", h0, h0.ap[:, r2, 0:NS - 1], H, H.ap[:, 1:NS], part=True)
                    hcur = list(hA); hnxt = list(hB)
                    pend = None
                    if pss == 0:
                        for t2 in range(TS // 2):
                            ja, jb = jorder[2 * t2], jorder[2 * t2 + 1]
                            for u in range(4):
                                for r2 in range(2):
                                    g = 2 * u + r2
                                    r = rows[g]
                                    if t2 > 0:
                                        mm(psH[u], psH[u].ap[:, r2, :], A2m[g], A2m[g].ap, hcur[u], hcur[u].ap[:, r2, :], True, False)
                                    mm(psH[u], psH[u].ap[:, r2, :], ABpad, ABpad.ap[:, r, :], uJ, uJ.ap[:, ja, c0:c0 + NS], t2 == 0, False)
                                    mm(psH[u], psH[u].ap[:, r2, :], Bpad, Bpad.ap[:, r, :], uJ, uJ.ap[:, jb, c0:c0 + NS], False, True)
                                if t2 == TS // 2 - 1:
                                    e = ("act", "dve")[evn[0] % 2]; evn[0] += 1
                                    if e == "act":
                                        P.op("act", lambda e_, o=HcU[u].ap, i=psH[u].ap: e_.copy(out=o, in_=i), reads=[psH[u].b], writes=[Hc[2 * u].b, Hc[2 * u + 1].b])
                                    else:
                                        P.op("dve", lambda e_, o=HcU[u].ap, i=psH[u].ap: e_.tensor_copy(out=o, in_=i), reads=[psH[u].b], writes=[Hc[2 * u].b, Hc[2 * u + 1].b])
                                else:
                                    evac(hnxt[u], hnxt[u].ap, psH[u], psH[u].ap)
                            hcur, hnxt = hnxt, hcur
                        continue
                    for st, j in enumerate(jorder):
                        for u in range(4):
                            first = (pss == 0 and st == 0)
                            for r2 in range(2):
                                g = 2 * u + r2
                                r = rows[g]
                                if not first:
                                    mm(psH[u], psH[u].ap[:, r2, :], Am[g], Am[g].ap, hcur[u], hcur[u].ap[:, r2, :], True, False)
                                mm(psH[u], psH[u].ap[:, r2, :], Bpad, Bpad.ap[:, r, :], uJ, uJ.ap[:, j, c0:c0 + NS], first, True)
                            if pss == 0 and st == TS - 1:
                                e = ("act", "dve")[evn[0] % 2]; evn[0] += 1
                                if e == "act":
                                    P.op("act", lambda e_, o=HcU[u].ap, i=psH[u].ap: e_.copy(out=o, in_=i), reads=[psH[u].b], writes=[Hc[2 * u].b, Hc[2 * u + 1].b])
                                else:
                                    P.op("dve", lambda e_, o=HcU[u].ap, i=psH[u].ap: e_.tensor_copy(out=o, in_=i), reads=[psH[u].b], writes=[Hc[2 * u].b, Hc[2 * u + 1].b])
                            else:
                                evac(hnxt[u], hnxt[u].ap, psH[u], psH[u].ap)
                            if pss == 1:
                                hb = hb16[u][st % 2]
                                cp("pool", hb, hb.ap, hnxt[u], hnxt[u].ap)
                        if pss == 1:
                            if pend is not None:
                                readout(pend[0], pend[1], rows, c0, ycopy)
                            pend = (j, st)
                        hcur, hnxt = hnxt, hcur
                    if pss == 1:
                        readout(pend[0], pend[1], rows, c0, ycopy)
            LH = L // 2
            for hf in range(2):
                CL = slice(C0B + hf * (LH // TS), C0B + (hf + 1) * (LH // TS))
                g0 = gel[0]; g1 = gel[1]
                v3 = lambda t: t.ap.rearrange("p (c j) -> p j c", j=TS)
                stt("dve", g0, v3(g0), uJ, uJ.ap[:, :, CL], dsk.ap[:, gc:gc + 1], yac, yac.ap[:, :, CL], ALU.mult, ALU.add, extra_reads=(dsk,))
                tt("pool", g1, g1.ap, g0, g0.ap, g0, g0.ap, ALU.mult)
                ts("dve", g1, g1.ap, g1, g1.ap, 0.044715, 1.0, ALU.mult, ALU.add)
                tt("pool", g1, g1.ap, g1, g1.ap, g0, g0.ap, ALU.mult)
                act(g1, g1.ap, g1, g1.ap, AF.Sigmoid, scale=1.5957691216057308)
                tt("dve", gout, gout.ap, g0, g0.ap, g1, g1.ap, ALU.mult)
                dma("sp", gT_s, gT_s.ap[gc * 128:(gc + 1) * 128, hf * LH:(hf + 1) * LH], gout, gout.ap)
        P.barrier()

    def emit_readout(j, st, rows, hb16, psY, Cpad, yac, c0, ycopy, evac):
        py = psY[st % 2]
        for g in range(4):
            hb = hb16[g][st % 2]
            mm(py, py.ap, Cpad, Cpad.ap[:, rows[g], :], hb, hb.ap, g == 0, g == 3)
        NSl = py.ap.shape[1]
        if ycopy:
            evac(yac, yac.ap[:, j, c0:c0 + NSl], py, py.ap)
        else:
            tt("dve", yac, yac.ap[:, j, c0:c0 + NSl], yac, yac.ap[:, j, c0:c0 + NSl], py, py.ap, ALU.add)

    def glu_phase():
        cur[0] = PERSIST_END
        Wg = sb([128, 4, 512], BF16, "Wg"); wst_ = sb([128, 4, 512], F32, "wgst")
        bg = sb([128, 4], F32, "bg")
        gb = [sb([128, 4, 512], BF16, "gb%d" % i) for i in range(2)]
        so = [sb([128, 4, 512], BF16, "so%d" % i) for i in range(2)]
        sgm = [sb([128, 512], F32, "sgm%d" % i) for i in range(2)]
        dma("sp", wst_, wst_.ap, None, wglu_d.rearrange("(k p) n -> p k n", p=128))
        cp("dve", Wg, Wg.ap, wst_, wst_.ap)
        dma("sp", bg, bg.ap, None, bglu_d.rearrange("(c p) -> p c", p=128), **NC_)
        def gload(tb):
            dma("sp", gb[tb % 2], gb[tb % 2].ap, gT_s, gT_s.ap[:, tb * 512:(tb + 1) * 512].rearrange("(k p) t -> p k t", p=128))
        gload(0)
        for tb in range(L // 512):
            g_ = gb[tb % 2]; s_ = so[tb % 2]
            if tb + 1 < L // 512:
                gload(tb + 1)
            for m in range(4):
                ps = PS[m % 2]
                for k in range(4):
                    mm(ps, ps.ap, Wg, Wg.ap[:, k, m * 128:(m + 1) * 128], g_, g_.ap[:, k, :], k == 0, k == 3)
                sm = sgm[m % 2]
                act(sm, sm.ap, ps, ps.ap, AF.Sigmoid, extra_reads=(bg,), bias=bg.ap[:, m:m + 1])
                tt("dve", s_, s_.ap[:, m, :], g_, g_.ap[:, m, :], sm, sm.ap, ALU.mult, part=True)
            dma("sp", sgT_s, sgT_s.ap[:, tb * 512:(tb + 1) * 512].rearrange("(k p) t -> p k t", p=128), s_, s_.ap)
        P.barrier()

    def attn_phase():
        cur[0] = PERSIST_END
        kT = sb([128, 4, LK], BF16, "kT")
        vA = sb([128, NKB, 512], BF16, "vA")
        Wo = sb([128, 8, D], BF16, "Wo")
        gate5 = sb([128, D], F32, "gate5")
        gsubT = sb([128, 1], F32, "gsubT")
        onesB = sb([128, 128], BF16, "onesB"); onesF = sb([128, 128], F32, "onesF")
        lam = sb([128, 8], F32, "lam"); lqk = sb([128, 2, 128], F32, "lqk")
        qT = [sb([128, 4, 512], BF16, "qT%d" % i) for i in range(2)]
        catT = sb([128, 8, 512], BF16, "catT")
        eT = [sb([128, 512], BF16, "eT%d" % i) for i in range(3)]
        w0 = sb([128, 512], F32, "w0"); w1 = sb([128, 512], F32, "w1"); w2 = sb([128, 512], F32, "w2")
        w3 = sb([128, 512], F32, "w3"); w4 = sb([128, 512], F32, "w4"); w5 = sb([128, 512], F32, "w5")
        x1t = [sb([128, D], F32, "x1t%d" % i) for i in range(2)]
        yall = sb([128, D], F32, "stgD0")
        ytmp = [T(yall.ap[:, 0:512], "ytmpD0"), T(yall.ap[:, 512:1024], "ytmpD1")]
        stg = [yall]
        dma("sp", gate5, gate5.ap, mod_s, mod_s.ap[0:1, 5 * D:6 * D].partition_broadcast(128).rearrange("p a d -> p (a d)"))
        dma("sp", gsubT, gsubT.ap, None, sg_d.rearrange("(p o) -> p o", o=1))
        ts("dve", gsubT, gsubT.ap, gsubT, gsubT.ap, 1.0 - LAM_INIT, None, ALU.mult)
        memset("pool", onesB, onesB.ap, 1.0)
        memset("pool", onesF, onesF.ap, 1.0 / 128.0)
        dma("sp", lqk, lqk.ap[:, 0, :], None, lq_d.partition_broadcast(128))
        dma("sp", lqk, lqk.ap[:, 1, :], None, lk_d.partition_broadcast(128))
        tt("dve", lqk, lqk.ap[:, 0, :], lqk, lqk.ap[:, 0, :], lqk, lqk.ap[:, 1, :], ALU.mult)
        P.op("dve", lambda e: e.reduce_sum(out=lam.ap[:, 0:2], in_=lqk.ap[:, 0, :].rearrange("p (a b) -> p a b", b=64), axis=mybir.AxisListType.X),
             reads=[lqk.b], writes=[lam.b])
        act(lam, lam.ap[:, 2:4], lam, lam.ap[:, 0:2], AF.Exp)
        tt("dve", lam, lam.ap[:, 4:5], lam, lam.ap[:, 2:3], lam, lam.ap[:, 3:4], ALU.subtract)
        ts("dve", lam, lam.ap[:, 5:6], lam, lam.ap[:, 4:5], LAM_INIT, -1.0, ALU.add, ALU.mult)
        for m2 in range(8):
            st = stg[0]
            dma("sp", st, st.ap, None, wout_d[m2 * 128:(m2 + 1) * 128, :])
            cp("dve", Wo, Wo.ap[:, m2, :], st, st.ap, part=True)
        for h in range(4):
            dma("sp", kT, kT.ap[:, h, :], kT_s, kT_s.ap[h * 128:(h + 1) * 128, :])
        for k0 in range(0, NKB, 6):
            k1 = min(NKB, k0 + 6)
            dma("sp", vA, vA.ap[:, k0:k1, :], v_s, v_s.ap[k0 * 128:k1 * 128, :].rearrange("(kb p) e -> p kb e", p=128))
        psS = [PS[0], PS[1]]
        psYd = PS[7]

        def banks(h, c):
            if c == 1:
                return PS[3], PS[5]
            return (PS[2], PS[4]) if h % 2 == 0 else (PS[6], PS[7])
        scnt = 0

        def qload(qb):
            dma("sp", qT[qb % 2], qT[qb % 2].ap, qT_s, qT_s.ap[:, qb * 512:(qb + 1) * 512].rearrange("(h p) t -> p h t", p=128))
        qload(0)
        for qb in range(L // 512):
            q_ = qT[qb % 2]
            if qb + 1 < L // 512:
                qload(qb + 1)
            dma("sp", catT, catT.ap[:, 0:4, :], sgT_s, sgT_s.ap[:, qb * 512:(qb + 1) * 512].rearrange("(k p) t -> p k t", p=128))
            its = [(h, c, kb) for h in range(4) for c in range(2) for kb in range(NKB)]

            def s_mm(i):
                h, c, kb = its[i]
                pl = slice(c * 64, (c + 1) * 64)
                pS = psS[(scnt0 + i) % 2]
                mm(pS, pS.ap, kT, kT.ap[pl, h, kb * 128:(kb + 1) * 128], q_, q_.ap[pl, h, :], True, True)
            scnt0 = scnt

            def warm(n):
                for _ in range(n):
                    mm(PS[6], PS[6].ap, onesB, onesB.ap, kT, kT.ap[:, 0, 0:512], True, True)
            warm(WARM_N)
            s_mm(0)
            pending = []
            for i, (h, c, kb) in enumerate(its):
                for pd in list(pending):
                    pd[0] -= 1
                    if pd[0] <= 0:
                        pd[1]()
                        pending.remove(pd)
                pS = psS[(scnt0 + i) % 2]; e_ = eT[(scnt0 + i) % 3]
                act(e_, e_.ap, pS, pS.ap, AF.Exp, scale=0.125)
                if i + 1 < len(its):
                    s_mm(i + 1)
                if kb % BURST_EVERY == BURST_EVERY // 2:
                    wb = PS[7] if h % 2 == 0 else PS[4]
                    for _ in range(BURST_N):
                        mm(wb, wb.ap, onesB, onesB.ap, kT, kT.ap[:, 0, 0:512], True, True)
                pO, pD = banks(h, c)
                mm(pO, pO.ap, vA, vA.ap[:, kb, h * 128:(h + 1) * 128], e_, e_.ap, kb == 0, kb == NKB - 1)
                mm(pD, pD.ap, onesB, onesB.ap, e_, e_.ap, kb == 0, kb == NKB - 1)
                if not (c == 1 and kb == NKB - 1):
                    continue
                pO0, pD0 = banks(h, 0)
                pO1, pD1 = banks(h, 1)
                cp("act", w0, w0.ap, pD0, pD0.ap)
                cp("dve", w4, w4.ap, pO0, pO0.ap)
                cp("act", w1, w1.ap, pD1, pD1.ap)
                cp("dve", w5, w5.ap, pO1, pO1.ap)
                recip(w0, w0.ap, w0, w0.ap)
                tt("dve", w4, w4.ap, w4, w4.ap, w0, w0.ap, ALU.mult)
                recip(w1, w1.ap, w1, w1.ap)
                tt("dve", w5, w5.ap, w5, w5.ap, w1, w1.ap, ALU.mult)
                stt("dve", w4, w4.ap, w5, w5.ap, lam.ap[:, 5:6], w4, w4.ap, ALU.mult, ALU.add, extra_reads=(lam,))
                tt("pool", w2, w2.ap, w4, w4.ap, w4, w4.ap, ALU.mult)

                def fin(h=h, psM=pO0):
                    mm(psM, psM.ap, onesF, onesF.ap, w2, w2.ap, True, True)
                    act(w3, w3.ap, psM, psM.ap, AF.Ln, extra_reads=(epsT,), bias=epsT.ap[:, 0:1])
                    act(w3, w3.ap, w3, w3.ap, AF.Exp, scale=-0.5)
                    stt("dve", catT, catT.ap[:, 4 + h, :], w4, w4.ap, gsubT.ap[:, 0:1], w3, w3.ap, ALU.mult, ALU.mult, extra_reads=(gsubT,), part=True)
                pending.append([DEFER_N, fin])
            for pd in pending:
                pd[1]()
            pending.clear()
            scnt += len(its)
            for s in range(4):
                xt = x1t[s % 2]
                r0 = qb * 512 + s * 128
                dma("sp", xt, xt.ap, x1_s, x1_s.ap[r0:r0 + 128, :])
                for nh in range(2):
                    for kc in range(8):
                        mm(psYd, psYd.ap, catT, catT.ap[:, kc, s * 128:(s + 1) * 128], Wo, Wo.ap[:, kc, nh * 512:(nh + 1) * 512], kc == 0, kc == 7)
                    yt = ytmp[nh]
                    tt("dve", yt, yt.ap, psYd, psYd.ap, gate5, gate5.ap[:, nh * 512:(nh + 1) * 512], ALU.mult)
                    tt("pool", xt, xt.ap[:, nh * 512:(nh + 1) * 512], xt, xt.ap[:, nh * 512:(nh + 1) * 512], yt, yt.ap, ALU.add)
                dma("sp", x2_s, x2_s.ap[r0:r0 + 128, :], xt, xt.ap)
        P.barrier()

    ffn_phase(0)
    inproj_phase()
    s5_phase()
    glu_phase()
    attn_phase()
    ffn_phase(1)
    P.emit()
    return nc, P


def host_consts(L):
    f32 = np.float32
    rows = L // 64
    row = np.repeat(np.arange(rows, dtype=f32), 64)
    col = np.tile(np.arange(64, dtype=f32), rows)
    inv_freq = np.power(f32(10000.0), -(np.arange(16, dtype=f32) / f32(16))).astype(f32)
    cos = np.zeros((128, L), f32); sin = np.zeros((128, L), f32)
    for c in range(2):
        for ax, pos in enumerate((row, col)):
            ang = (pos[:, None] * inv_freq[None, :]).astype(f32)
            for half in range(2):
                p0 = c * 64 + ax * 32 + half * 16
                cos[p0:p0 + 16, :] = np.cos(ang).T
                sin[p0:p0 + 16, :] = (np.sin(ang).T) * (f32(-1.0) if half == 0 else f32(1.0))
    ident = np.eye(128, dtype=f32)
    xsw = np.zeros((128, 128), f32)
    for k in range(128):
        xsw[k, (k + 64) % 128] = 1.0
    sgn = np.ones((128, 1), f32); sgn[64:] = -1.0
    return dict(rope_cos=cos, rope_sin=sin, ident=ident, xswap=xsw, sgn=sgn)


_CACHE = {}


def make_in_maps(inputs, L, ncores):
    f32 = np.float32
    A = lambda a: np.ascontiguousarray(np.asarray(a, dtype=f32))
    w_in = A(inputs["w_in"])[0]
    perm = np.arange(512) ^ 16
    w_in_ext = np.ascontiguousarray(np.concatenate([w_in, w_in[:, 512 + perm], w_in[:, 1024 + perm]], axis=1))
    shared = dict(
        c_ctx=A(inputs["c_ctx"]), w_mod=A(inputs["w_mod"])[0], b_mod=A(inputs["b_mod"])[0], norm_g=A(inputs["norm_g"])[0],
        ffn_w_in=A(inputs["ffn_w_in"])[0], ffn_w_out=A(inputs["ffn_w_out"])[0], w_in_ext=w_in_ext, w_out=A(inputs["w_out"])[0],
        ssm_a_re=A(inputs["ssm_a_re"])[0].reshape(64, 64), ssm_a_im=A(inputs["ssm_a_im"])[0].reshape(64, 64),
        ssm_log_dt=A(inputs["ssm_log_dt"])[0].reshape(64),
        ssm_b_re=A(inputs["ssm_b_re"])[0].reshape(64, 64, 16), ssm_b_im=A(inputs["ssm_b_im"])[0].reshape(64, 64, 16),
        ssm_c_re=A(inputs["ssm_c_re"])[0].reshape(64, 16, 64), ssm_c_im=A(inputs["ssm_c_im"])[0].reshape(64, 16, 64),
        ssm_d=A(inputs["ssm_d"])[0].reshape(512), w_glu=A(inputs["w_glu"])[0], b_glu=A(inputs["b_glu"])[0],
        lam_q=A(inputs["lam_q"])[0].reshape(128), lam_k=A(inputs["lam_k"])[0].reshape(128),
        subln_g=A(inputs["subln_g"])[0], final_g=A(inputs["final_g"]),
    )
    shared.update(host_consts(L))
    x = A(inputs["x"]); c = A(inputs["c"]); ctx = A(inputs["ctx"])
    maps = []
    for b in range(ncores):
        m = dict(shared)
        m["x"] = np.ascontiguousarray(x[b, :L]); m["c"] = np.ascontiguousarray(c[b]); m["ctx"] = np.ascontiguousarray(ctx[b])
        maps.append(m)
    return maps


def kernel(**inputs):
    x = np.asarray(inputs["x"])
    B, L, _ = x.shape
    if L not in _CACHE:
        _CACHE[L] = build(L)
    nc, _ = _CACHE[L]
    maps = make_in_maps(inputs, L, B)
    res = run_bass_kernel_spmd(nc, maps, core_ids=list(range(B)))
    out = np.stack([np.asarray(r["out"], dtype=np.float32) for r in res.results], axis=0)
    return out
```

```python
import numpy as np
import concourse.bass as bass
import concourse.mybir as mybir
from concourse.bass_utils import run_bass_kernel_spmd

F32 = mybir.dt.float32
BF16 = mybir.dt.bfloat16
U8 = mybir.dt.uint8
AF = mybir.ActivationFunctionType
ALU = mybir.AluOpType

D = 1024
KD = 8
FF = 2816
NFF = 22
LC = 256
TS = 64
WARM_N = 24
BURST_EVERY = 33
BURST_N = 8
HEAD_WARM_N = 10
ATTACH_WAIT = True
DEFER_N = 20
EPS = 1e-6
LAM_INIT = 0.2


class Buf:
    __slots__ = ("name", "writers", "readers", "prev_readers")

    def __init__(self, name=""):
        self.name = name
        self.writers = []
        self.readers = []
        self.prev_readers = []


class Op:
    __slots__ = ("eng", "fn", "deps", "idx", "is_dma", "sem", "sem_val", "signals", "full")

    def __init__(self, eng, fn, is_dma):
        self.eng = eng
        self.fn = fn
        self.deps = []
        self.is_dma = is_dma
        self.sem = None
        self.sem_val = None
        self.signals = False
        self.idx = 0


class Prog:
    ENGS = ("pe", "act", "dve", "pool", "sp")

    def __init__(self, nc):
        self.nc = nc
        self.ops = {e: [] for e in self.ENGS}
        self.all_ops = []
        self.eng_obj = {"pe": nc.tensor, "act": nc.scalar, "dve": nc.vector,
                        "pool": nc.gpsimd, "sp": nc.sync}

    def op(self, eng, fn, reads=(), writes=(), dma=False, part=False, semkey=None):
        o = Op(eng, fn, dma)
        o.idx = len(self.ops[eng])
        o.full = not (dma or part)
        deps = set()
        for b in reads:
            deps.update(b.writers)
        for b in writes:
            cont = (dma or part) and not b.readers
            if cont:
                deps.update(b.prev_readers)
                deps.update(w for w in b.writers if w.full)
            else:
                deps.update(b.writers)
                deps.update(b.readers)
        best = {}
        out = []
        for d in deps:
            if d is o:
                continue
            if d.is_dma:
                out.append(d)
                continue
            if d.eng == "pe" and eng == "pe" and not dma:
                continue
            if d.eng not in best or best[d.eng].idx < d.idx:
                best[d.eng] = d
        o.deps = out + list(best.values())
        for b in reads:
            b.readers.append(o)
        for b in writes:
            cont = (dma or part) and not b.readers
            if cont:
                b.writers.append(o)
            else:
                b.prev_readers = b.readers
                b.writers = [o]
            b.readers = []
        if dma:
            o.sem = semkey
        self.ops[eng].append(o)
        self.all_ops.append(o)
        return o

    def barrier(self):
        deps = []
        for e in self.ENGS:
            for o in reversed(self.ops[e]):
                if not o.is_dma and o.fn is not None:
                    deps.append(o)
                    break
        dma_last = {}
        for o in self.all_ops:
            if o.is_dma:
                dma_last[id(o.sem)] = o
        deps = deps + list(dma_last.values())
        for e in self.ENGS:
            w = Op(e, None, False)
            w.full = True
            w.deps = [d for d in deps if d.eng != e or d.is_dma]
            self.ops[e].append(w)
            self.all_ops.append(w)

    def emit(self):
        nc = self.nc
        for o in self.all_ops:
            for d in o.deps:
                d.signals = True
        esem = {e: nc.alloc_semaphore(name="sem_" + e) for e in self.ENGS}
        cnt = {e: 0 for e in self.ENGS}
        bufsem = {}
        bufcnt = {}
        for o in self.all_ops:
            if o.is_dma:
                key = id(o.sem)
                if key not in bufsem:
                    bufsem[key] = nc.alloc_semaphore(name="dsem_%d" % len(bufsem))
                    bufcnt[key] = 0
                bufcnt[key] += 16
                o.sem_val = bufcnt[key]
                o.sem = bufsem[key]
            elif o.signals:
                cnt[o.eng] += 1
                o.sem_val = cnt[o.eng]
                o.sem = esem[o.eng]
        self.stats = dict(nsem=len(bufsem) + 5, cnt=cnt, maxdma=max(bufcnt.values()),
                          nops={e: len(self.ops[e]) for e in self.ENGS})
        for e in self.ENGS:
            eng = self.eng_obj[e]
            waited = {}
            for o in self.ops[e]:
                need = {}
                for d in o.deps:
                    k = id(d.sem)
                    if waited.get(k, 0) >= d.sem_val:
                        continue
                    if k not in need or need[k][1] < d.sem_val:
                        need[k] = (d.sem, d.sem_val)
                attach = None
                if e == "pe" and o.fn is not None and len(need) == 1 and ATTACH_WAIT:
                    (k, (s, v)), = need.items()
                    attach = (s, v)
                    waited[k] = v
                else:
                    for k, (s, v) in need.items():
                        eng.wait_ge(s, v)
                        waited[k] = v
                if o.fn is None:
                    continue
                ins = o.fn(eng)
                if attach is not None:
                    ins._wait_ge(attach[0], attach[1])
                if o.is_dma:
                    ins.then_inc(o.sem, 16)
                elif o.signals:
                    ins.then_inc(o.sem, 1)


class T:
    __slots__ = ("ap", "b")

    def __init__(self, ap, name=""):
        self.ap = ap
        self.b = Buf(name)


def build(L):
    LX = LC + L + LC
    NCH = LX // TS
    NS = NCH - LC // TS
    C0B = LC // TS
    LK = L + LC
    NKB = LK // 128
    nc = bass.Bass("TRN2", target_bir_lowering=False)
    P = Prog(nc)

    def din(name, shape, dt=F32):
        return nc.dram_tensor(name, list(shape), dt, kind="ExternalInput").ap()

    DRAM_LIST = []

    def dscr(name, shape, dt):
        t = T(nc.dram_tensor(name, list(shape), dt).ap(), name)
        DRAM_LIST.append(t)
        return t

    x_d = din("x", [L, D]); c_d = din("c", [D]); ctx_d = din("ctx", [LC, D]); cc_d = din("c_ctx", [D])
    wmod_d = din("w_mod", [D, 9 * D]); bmod_d = din("b_mod", [9 * D]); ng_d = din("norm_g", [3, D])
    fw1_d = din("ffn_w_in", [2, D, 2 * FF]); fw2_d = din("ffn_w_out", [2, FF, D])
    win_d = din("w_in_ext", [D, 3072]); wout_d = din("w_out", [D, D])
    are_d = din("ssm_a_re", [64, 64]); aim_d = din("ssm_a_im", [64, 64]); ldt_d = din("ssm_log_dt", [64])
    bre_d = din("ssm_b_re", [64, 64, 16]); bim_d = din("ssm_b_im", [64, 64, 16])
    cre_d = din("ssm_c_re", [64, 16, 64]); cim_d = din("ssm_c_im", [64, 16, 64])
    dsk_d = din("ssm_d", [512]); wglu_d = din("w_glu", [512, 512]); bglu_d = din("b_glu", [512])
    lq_d = din("lam_q", [128]); lk_d = din("lam_k", [128]); sg_d = din("subln_g", [128]); fg_d = din("final_g", [D])
    cos_d = din("rope_cos", [128, L]); sin_d = din("rope_sin", [128, L])
    idf_d = din("ident", [128, 128]); xsw_d = din("xswap", [128, 128]); sgn_d = din("sgn", [128, 1])
    out_d = T(nc.dram_tensor("out", [L, D], F32, kind="ExternalOutput").ap(), "out")

    mod_s = dscr("mod_s", [2, 9 * D], F32)
    x1_s = dscr("x1_s", [L + LC, D], F32)
    x2_s = dscr("x2_s", [L, D], F32)
    uT_s = dscr("uT_s", [512, LX], BF16)
    qT_s = dscr("qT_s", [512, L], BF16)
    kT_s = dscr("kT_s", [512, LK], BF16)
    v_s = dscr("v_s", [LK, 512], BF16)
    gT_s = dscr("gT_s", [512, L], BF16)
    sgT_s = dscr("sgT_s", [512, L], BF16)

    BIG = 212000
    DRAM_LIST.append(out_d)
    big = nc.alloc_sbuf_tensor("big", [128, BIG], U8)
    ps01 = nc.alloc_psum_tensor("ps01", [128, 1024], F32)
    pst = [ps01[:, 0:512], ps01[:, 512:1024]] + [nc.alloc_psum_tensor("ps%d" % i, [128, 512], F32)[:, :] for i in range(2, 8)]
    psT2 = ps01[:, :].bitcast(BF16).rearrange("p (k t) -> p k t", t=256)
    cur = [0]

    def sb(shape, dt, name=""):
        esz = 4 if dt == F32 else 2
        n = int(np.prod(shape[1:])) * esz
        v = big[0:shape[0], cur[0]:cur[0] + n].bitcast(dt)
        cur[0] += (n + 63) // 64 * 64
        assert cur[0] <= BIG, ("sbuf overflow", name, cur[0])
        if len(shape) == 3:
            v = v.rearrange("p (a b) -> p a b", b=shape[2])
        elif len(shape) == 4:
            v = v.rearrange("p (a b c) -> p a b c", b=shape[2], c=shape[3])
        return T(v, name)

    PS = [T(pst[i], "ps%d" % i) for i in range(8)]

    idF = sb([128, 128], F32, "idF"); idB = sb([128, 128], BF16, "idB")
    xsw = sb([128, 128], F32, "xsw"); sgn = sb([128, 1], F32, "sgn")
    epsT = sb([128, 1], F32, "eps")
    modT = sb([128, 72, 2], F32, "modT")
    gmL = sb([128, 3, 8], F32, "gmL"); gmC = sb([128, 3, 8], F32, "gmC")
    ngT = sb([128, 3, 8], F32, "ngT")
    stat = sb([128, 16], F32, "stat")
    PERSIST_END = cur[0]

    DRAM_T = set(id(t) for t in DRAM_LIST)

    def dma(eng, out_t, out_ap, in_t, in_ap, **kw):
        reads = [in_t.b] if in_t is not None else []
        key = in_t.b if id(out_t) in DRAM_T else out_t.b
        P.op(eng, lambda e, o=out_ap, i=in_ap, kw=kw: e.dma_start(out=o, in_=i, **kw),
             reads=reads, writes=[out_t.b], dma=True, semkey=key)

    def mm(out_t, out_ap, lhsT_t, lhsT, rhs_t, rhs, start, stop):
        P.op("pe", lambda e, o=out_ap, l=lhsT, r=rhs, s=start, t=stop: e.matmul(o, lhsT=l, rhs=r, start=s, stop=t),
             reads=[lhsT_t.b, rhs_t.b], writes=[out_t.b], part=True)

    def act(out_t, out_ap, in_t, in_ap, func, extra_reads=(), extra_writes=(), part=False, **kw):
        P.op("act", lambda e, o=out_ap, i=in_ap, f=func, kw=kw: e.activation(out=o, in_=i, func=f, **kw),
             reads=[in_t.b] + [t.b for t in extra_reads], writes=[out_t.b] + [t.b for t in extra_writes], part=part)

    def ts(eng, out_t, out_ap, in_t, in_ap, s1, s2, op0, op1=None, extra_reads=(), part=False):
        if op1 is None:
            f = lambda e, o=out_ap, i=in_ap: e.tensor_scalar(out=o, in0=i, scalar1=s1, scalar2=None, op0=op0)
        else:
            f = lambda e, o=out_ap, i=in_ap: e.tensor_scalar(out=o, in0=i, scalar1=s1, scalar2=s2, op0=op0, op1=op1)
        P.op(eng, f, reads=[in_t.b] + [t.b for t in extra_reads], writes=[out_t.b], part=part)

    def tt(eng, out_t, out_ap, a_t, a_ap, b_t, b_ap, op, part=False):
        P.op(eng, lambda e, o=out_ap, a=a_ap, b=b_ap: e.tensor_tensor(out=o, in0=a, in1=b, op=op),
             reads=[a_t.b, b_t.b], writes=[out_t.b], part=part)

    def stt(eng, out_t, out_ap, a_t, a_ap, scalar, b_t, b_ap, op0, op1, extra_reads=(), part=False):
        P.op(eng, lambda e, o=out_ap, a=a_ap, b=b_ap: e.scalar_tensor_tensor(out=o, in0=a, scalar=scalar, in1=b, op0=op0, op1=op1),
             reads=[a_t.b, b_t.b] + [t.b for t in extra_reads], writes=[out_t.b], part=part)

    def cp(eng, out_t, out_ap, in_t, in_ap, part=False):
        if eng == "act":
            P.op("act", lambda e, o=out_ap, i=in_ap: e.copy(out=o, in_=i), reads=[in_t.b], writes=[out_t.b], part=part)
        else:
            P.op(eng, lambda e, o=out_ap, i=in_ap: e.tensor_copy(out=o, in_=i), reads=[in_t.b], writes=[out_t.b], part=part)

    def memset(eng, t, ap, val, part=False):
        P.op(eng, lambda e, a=ap: e.memset(a, val), writes=[t.b], part=part)

    def recip(out_t, out_ap, in_t, in_ap):
        P.op("dve", lambda e, o=out_ap, i=in_ap: e.reciprocal(out=o, in_=i), reads=[in_t.b], writes=[out_t.b])

    NC_ = dict(allow_slow_non_contiguous=True)
    dma("sp", idF, idF.ap, None, idf_d)
    dma("sp", xsw, xsw.ap, None, xsw_d)
    dma("sp", sgn, sgn.ap, None, sgn_d)
    dma("sp", ngT, ngT.ap, None, ng_d.rearrange("i (k p) -> p i k", p=128), **NC_)
    cp("dve", idB, idB.ap, idF, idF.ap)
    memset("dve", epsT, epsT.ap, EPS)

    cur[0] = PERSIST_END
    cT = sb([128, 8, 2], F32, "cT"); scT = sb([128, 8, 2], F32, "scT")
    modrow = sb([2, 9 * D], F32, "modrow"); bmodr = sb([2, 9 * D], F32, "bmodr")
    wst = [sb([128, 8, 512], F32, "wst%d" % i) for i in range(2)]
    id2 = idF.ap[0:2, 0:2]
    dma("sp", cT, cT.ap[:, :, 0], None, c_d.rearrange("(k p) -> p k", p=128), **NC_)
    dma("sp", cT, cT.ap[:, :, 1], None, cc_d.rearrange("(k p) -> p k", p=128), **NC_)
    dma("sp", bmodr, bmodr.ap, None, bmod_d.partition_broadcast(2))
    act(scT, scT.ap, cT, cT.ap, AF.Silu)
    for nb in range(18):
        w = wst[nb % 2]
        dma("sp", w, w.ap, None, wmod_d[:, nb * 512:(nb + 1) * 512].rearrange("(k p) n -> p k n", p=128))
        ps = PS[nb % 2]
        for k in range(8):
            mm(ps, ps.ap[0:2, :], scT, scT.ap[:, k, :], w, w.ap[:, k, :], k == 0, k == 7)
        tt("dve", modrow, modrow.ap[:, nb * 512:(nb + 1) * 512], ps, ps.ap[0:2, :], bmodr, bmodr.ap[:, nb * 512:(nb + 1) * 512], ALU.add, part=True)
    dma("sp", mod_s, mod_s.ap, modrow, modrow.ap)
    for j in range(72):
        mm(PS[2], PS[2].ap[:, 2 * j:2 * j + 2], modrow, modrow.ap[:, j * 128:(j + 1) * 128], idF, id2, True, True)
    cp("dve", modT, modT.ap, PS[2], PS[2].ap[:, 0:144].rearrange("p (a b) -> p a b", b=2))
    for i in range(3):
        for (gm, col) in ((gmL, 0), (gmC, 1)):
            stt("dve", gm, gm.ap[:, i, :], modT, modT.ap[:, (3 * i + 1) * 8:(3 * i + 2) * 8, col], 1.0,
                ngT, ngT.ap[:, i, :], ALU.add, ALU.mult, part=True)
    P.barrier()

    def sh_ap(i, col, k):
        return modT.ap[:, 3 * i * 8 + k, col:col + 1]

    def make_front(nslots=2):
        xblk = [sb([128, 2, D], F32, "xblk%d" % i) for i in range(nslots)]
        xs = [sb([128, D], BF16, "xs%d" % i) for i in range(2)]
        junk = sb([128, D], BF16, "junk")
        hT = sb([128, 8, 256], BF16, "hT")
        return dict(xblk=xblk, xs=xs, junk=junk, hT=hT, n=0, hT2=None)

    psTr = T(None, "psTr")

    def front_load(fr, src_t, src_ap):
        n = fr["n"]; fr["n"] += 1
        xb = fr["xblk"][n % len(fr["xblk"])]
        dma("sp", xb, xb.ap, src_t, src_ap.rearrange("(s p) d -> p s d", p=128))
        return xb

    def front(fr, xb, gm, i, col, hT=None):
        for s in range(2):
            xs = fr["xs"][s]
            act(fr["junk"], fr["junk"].ap, xb, xb.ap[:, s, :], AF.Square, accum_out=stat.ap[:, s:s + 1], extra_writes=(stat,))
            act(stat, stat.ap[:, 2 + s:3 + s], stat, stat.ap[:, s:s + 1], AF.Sqrt, extra_reads=(epsT,), scale=1.0 / D, bias=epsT.ap[:, 0:1])
            recip(stat, stat.ap[:, 4 + s:5 + s], stat, stat.ap[:, 2 + s:3 + s])
            ts("dve", xs, xs.ap, xb, xb.ap[:, s, :], stat.ap[:, 4 + s:5 + s], None, ALU.mult, extra_reads=(stat,))
            for k in range(8):
                P.op("pe", lambda e, o=psT2[:, k, s * 128:(s + 1) * 128], a=xs.ap[:, k * 128:(k + 1) * 128]: e.transpose(o, a, idB.ap),
                     reads=[xs.b, idB.b], writes=[psTr.b], part=True)
        if hT is None:
            hT = fr["hT"]
        for k in range(8):
            act(hT, hT.ap[:, k, :], psTr, psT2[:, k, :], AF.Identity, extra_reads=(gm, modT), part=True,
                scale=gm.ap[:, i, k:k + 1], bias=sh_ap(i, col, k))
        return xb

    def ffn_phase(which):
        cur[0] = PERSIST_END
        W1 = sb([128, 8, 2 * FF], BF16, "W1"); W2 = sb([128, NFF, D], BF16, "W2")
        gate = sb([128, D], F32, "gate"); gatec = sb([128, D], F32, "gatec")
        fgb = gatec
        fr = make_front()
        hTs = [fr["hT"], sb([128, 8, 256], BF16, "hTb")]
        GT = sb([128, NFF, 256], BF16, "GT")
        sg = [sb([128, 256], F32, "sg%d" % i) for i in range(2)]
        ytmp = [sb([128, 512], F32, "ytmp%d" % i) for i in range(2)]
        stage_off = cur[0]
        stg = [sb([128, 1408], F32, "stg%d" % i) for i in range(2)]
        gi = 2 if which == 0 else 8
        dma("sp", gate, gate.ap, mod_s, mod_s.ap[0:1, gi * D:(gi + 1) * D].partition_broadcast(128).rearrange("p a d -> p (a d)"))
        ts("dve", gate, gate.ap, gate, gate.ap, 0.5, None, ALU.mult)
        if which == 0:
            dma("sp", gatec, gatec.ap, mod_s, mod_s.ap[1:2, 2 * D:3 * D].partition_broadcast(128).rearrange("p a d -> p (a d)"))
            ts("dve", gatec, gatec.ap, gatec, gatec.ap, 0.5, None, ALU.mult)
        else:
            dma("sp", fgb, fgb.ap, None, fg_d.partition_broadcast(128))
        engs = ("dve", "pool", "act")
        n = 0
        for k in range(8):
            for hh in range(4):
                st = stg[n % 2]
                dma("sp", st, st.ap, None, fw1_d[which, k * 128:(k + 1) * 128, hh * 1408:(hh + 1) * 1408])
                cp(engs[n % 3], W1, W1.ap[:, k, hh * 1408:(hh + 1) * 1408], st, st.ap, part=True)
                n += 1
        for m2 in range(NFF):
            st = stg[n % 2]
            dma("sp", st, st.ap[:, 0:D], None, fw2_d[which, m2 * 128:(m2 + 1) * 128, :])
            cp(engs[n % 3], W2, W2.ap[:, m2, :], st, st.ap[:, 0:D], part=True)
            n += 1
        if which == 0:
            blocks = [(None, x_d, x1_s, b * 256, b * 256, gmL, 0, gate) for b in range(L // 256)]
            blocks.append((None, ctx_d, x1_s, 0, L, gmC, 1, gatec))
        else:
            blocks = [(x2_s, x2_s.ap, out_d, b * 256, b * 256, gmL, 0, gate) for b in range(L // 256)]
        ii = 0 if which == 0 else 2
        xbs = {0: front_load(fr, blocks[0][0], blocks[0][1][blocks[0][3]:blocks[0][3] + 256, :])}
        for bi, (src_t, src, dst_t, r0, w0, gm, col, gt) in enumerate(blocks):
            if bi + 1 < len(blocks):
                nb_ = blocks[bi + 1]
                xbs[bi + 1] = front_load(fr, nb_[0], nb_[1][nb_[3]:nb_[3] + 256, :])
            xb = xbs.pop(bi)
            if bi == 0:
                front(fr, xb, gm, ii, col, hTs[0])
            hT = hTs[bi % 2]
            for m in range(NFF):
                pg = PS[2 + 2 * (m % 2)]; pu = PS[3 + 2 * (m % 2)]
                for k in range(8):
                    mm(pg, pg.ap[:, 0:256], W1, W1.ap[:, k, m * 128:(m + 1) * 128], hT, hT.ap[:, k, :], k == 0, k == 7)
                for k in range(8):
                    mm(pu, pu.ap[:, 0:256], W1, W1.ap[:, k, FF + m * 128:FF + (m + 1) * 128], hT, hT.ap[:, k, :], k == 0, k == 7)
                s_ = sg[m % 2]
                act(s_, s_.ap, pg, pg.ap[:, 0:256], AF.Silu)
                tt("dve", GT, GT.ap[:, m, :], s_, s_.ap, pu, pu.ap[:, 0:256], ALU.mult, part=True)
            if bi + 1 < len(blocks):
                nb_ = blocks[bi + 1]
                front(fr, xbs[bi + 1], nb_[5], ii, nb_[6], hTs[(bi + 1) % 2])
            q = 0
            for s in range(2):
                for nh in range(2):
                    py = PS[6 + (q % 2)]
                    for m in range(NFF):
                        mm(py, py.ap, GT, GT.ap[:, m, s * 128:(s + 1) * 128], W2, W2.ap[:, m, nh * 512:(nh + 1) * 512], m == 0, m == NFF - 1)
                    yt = ytmp[q % 2]
                    tt("dve", yt, yt.ap, py, py.ap, gt, gt.ap[:, nh * 512:(nh + 1) * 512], ALU.mult)
                    tt("pool", xb, xb.ap[:, s, nh * 512:(nh + 1) * 512], xb, xb.ap[:, s, nh * 512:(nh + 1) * 512], yt, yt.ap, ALU.add)
                    q += 1
            if which == 1:
                for s in range(2):
                    act(fr["junk"], fr["junk"].ap, xb, xb.ap[:, s, :], AF.Square, accum_out=stat.ap[:, 8 + s:9 + s], extra_writes=(stat,))
                    act(stat, stat.ap[:, 10 + s:11 + s], stat, stat.ap[:, 8 + s:9 + s], AF.Sqrt, extra_reads=(epsT,), scale=1.0 / D, bias=epsT.ap[:, 0:1])
                    recip(stat, stat.ap[:, 12 + s:13 + s], stat, stat.ap[:, 10 + s:11 + s])
                    stt("dve", xb, xb.ap[:, s, :], xb, xb.ap[:, s, :], stat.ap[:, 12 + s:13 + s], fgb, fgb.ap, ALU.mult, ALU.mult, extra_reads=(stat,))
            dma("sp", dst_t, dst_t.ap[w0:w0 + 256, :].rearrange("(s p) d -> p s d", p=128), xb, xb.ap)
        P.barrier()

    def inproj_phase():
        cur[0] = PERSIST_END
        Win = sb([128, 8, 3072], BF16, "Win")
        fr = make_front()
        stg = [sb([128, 3072], F32, "stgB%d" % i) for i in range(2)]
        cosb = [sb([128, 256], F32, "cos%d" % i) for i in range(2)]
        sinb = [sb([128, 256], F32, "sin%d" % i) for i in range(2)]
        t1 = [sb([128, 256], F32, "t1_%d" % i) for i in range(2)]
        t2 = [sb([128, 256], F32, "t2_%d" % i) for i in range(2)]
        uo = [sb([128, 4, 256], BF16, "uo%d" % i) for i in range(2)]
        qo = [sb([128, 4, 256], BF16, "qo%d" % i) for i in range(2)]
        ko = [sb([128, 4, 256], BF16, "ko%d" % i) for i in range(2)]
        vo = [sb([128, 2, 512], BF16, "vo%d" % i) for i in range(2)]
        engs = ("dve", "pool", "act")
        for k in range(8):
            st = stg[k % 2]
            dma("sp", st, st.ap, None, win_d[k * 128:(k + 1) * 128, :])
            cp(engs[k % 3], Win, Win.ap[:, k, :], st, st.ap, part=True)
        nblk = L // 256

        def pre(b):
            r0_ = L if b == nblk else b * 256
            xb_ = front_load(fr, x1_s, x1_s.ap[r0_:r0_ + 256, :])
            if b != nblk:
                dma("sp", cosb[b % 2], cosb[b % 2].ap, None, cos_d[:, b * 256:(b + 1) * 256])
                dma("sp", sinb[b % 2], sinb[b % 2].ap, None, sin_d[:, b * 256:(b + 1) * 256])
            return xb_
        xbs = {0: pre(0)}
        for b in range(nblk + 1):
            isctx = b == nblk
            r0 = L if isctx else b * 256
            gm = gmC if isctx else gmL
            if b + 1 <= nblk:
                xbs[b + 1] = pre(b + 1)
            front(fr, xbs.pop(b), gm, 1, 1 if isctx else 0)
            hT = fr["hT"]
            sl = b % 2
            pcount = 0
            for j in range(4):
                ps = PS[2 + (pcount % 4)]; pcount += 1
                for k in range(8):
                    mm(ps, ps.ap[:, 0:256], Win, Win.ap[:, k, j * 128:(j + 1) * 128], hT, hT.ap[:, k, :], k == 0, k == 7)
                cp("act", uo[sl], uo[sl].ap[:, j, :], ps, ps.ap[:, 0:256], part=True)
            if isctx:
                dma("sp", uT_s, uT_s.ap[:, 0:LC].rearrange("(j p) t -> p j t", p=128), uo[sl], uo[sl].ap)
                dma("sp", uT_s, uT_s.ap[:, LC + L:LX].rearrange("(j p) t -> p j t", p=128), uo[sl], uo[sl].ap)
            else:
                dma("sp", uT_s, uT_s.ap[:, LC + b * 256:LC + (b + 1) * 256].rearrange("(j p) t -> p j t", p=128), uo[sl], uo[sl].ap)
            for (base, swb, ot, dst, isq) in ((512, 2048, qo, qT_s, True), (1024, 2560, ko, kT_s, False)):
                if isq and isctx:
                    continue
                for j in range(4):
                    pa = PS[2 + (pcount % 4)]; pcount += 1
                    for k in range(8):
                        mm(pa, pa.ap[:, 0:256], Win, Win.ap[:, k, base + j * 128:base + (j + 1) * 128], hT, hT.ap[:, k, :], k == 0, k == 7)
                    if isctx:
                        cp("act", ot[sl], ot[sl].ap[:, j, :], pa, pa.ap[:, 0:256], part=True)
                        continue
                    pb = PS[2 + (pcount % 4)]; pcount += 1
                    for k in range(8):
                        mm(pb, pb.ap[:, 0:256], Win, Win.ap[:, k, swb + j * 128:swb + (j + 1) * 128], hT, hT.ap[:, k, :], k == 0, k == 7)
                    a1 = t1[j % 2]; a2 = t2[j % 2]
                    tt("dve", a1, a1.ap, pa, pa.ap[:, 0:256], cosb[sl], cosb[sl].ap, ALU.mult)
                    tt("dve", a2, a2.ap, pb, pb.ap[:, 0:256], sinb[sl], sinb[sl].ap, ALU.mult)
                    tt("pool", ot[sl], ot[sl].ap[:, j, :], a1, a1.ap, a2, a2.ap, ALU.add, part=True)
                c0 = L if isctx else b * 256
                dma("sp", dst, dst.ap[:, c0:c0 + 256].rearrange("(j p) t -> p j t", p=128), ot[sl], ot[sl].ap)
            for s in range(2):
                ps = PS[6 + s]
                for k in range(8):
                    mm(ps, ps.ap, hT, hT.ap[:, k, s * 128:(s + 1) * 128], Win, Win.ap[:, k, 1536:2048], k == 0, k == 7)
                cp("act", vo[sl], vo[sl].ap[:, s, :], ps, ps.ap, part=True)
            dma("sp", v_s, v_s.ap[r0:r0 + 256, :].rearrange("(s p) d -> p s d", p=128), vo[sl], vo[sl].ap)
        P.barrier()

    def s5_phase():
        cur[0] = PERSIST_END
        NR = 64
        Bpad = sb([128, NR, 128], BF16, "Bpad"); Cpad = sb([128, NR, 128], BF16, "Cpad"); ABpad = sb([128, NR, 128], BF16, "ABpad")
        PW = sb([128, 10, 2, NR], F32, "PW")
        dsk = sb([128, 4], F32, "dsk")
        setup_off = cur[0]
        arow = sb([64, 2, 128], F32, "arow")
        prm = sb([128, 16, NR], F32, "prm")
        Ball = sb([64, 2, NR * 16], F32, "Ball")
        Bbar = sb([64, 2, NR * 16], F32, "Bbar")
        btmp = sb([64, 2, NR * 16], F32, "btmp")
        Bbar2 = sb([64, 2, NR * 16], F32, "Bbar2")
        BP = [sb([64, 2, 128], F32, "BP%d" % i) for i in range(2)]
        BP2 = [sb([64, 2, 128], F32, "BP2_%d" % i) for i in range(2)]
        Call = sb([16, NR, 128], F32, "Call")
        dma("sp", arow, arow.ap[:, 0, 0:64], None, are_d); dma("sp", arow, arow.ap[:, 0, 64:128], None, are_d)
        dma("sp", arow, arow.ap[:, 1, 0:64], None, aim_d); dma("sp", arow, arow.ap[:, 1, 64:128], None, aim_d)
        dma("sp", prm, prm.ap[:, 2, :], None, ldt_d.partition_broadcast(128))
        dma("sp", dsk, dsk.ap, None, dsk_d.rearrange("(c p) -> p c", p=128), **NC_)
        dma("sp", Ball, Ball.ap[:, 0, :].rearrange("p (r h) -> p r h", h=16), None, bre_d.rearrange("r p h -> p r h"))
        dma("sp", Ball, Ball.ap[:, 1, :].rearrange("p (r h) -> p r h", h=16), None, bim_d.rearrange("r p h -> p r h"))
        dma("sp", Call, Call.ap[:, :, 0:64], None, cre_d.rearrange("r h p -> h r p"))
        dma("sp", Call, Call.ap[:, :, 64:128], None, cim_d.rearrange("r h p -> h r p"))
        for i in range(2):
            mm(PS[0], PS[0].ap[:, i * 64:(i + 1) * 64], arow, arow.ap[:, i, :], idF, idF.ap[0:64, 0:64], True, True)
        cp("dve", prm, prm.ap[:, 0:2, :], PS[0], PS[0].ap[:, 0:128].rearrange("p (a b) -> p a b", b=64))
        Pm = lambda i: prm.ap[:, i, :]
        ARE, AIM, LDT, DT, XR, ANG, MAG, T0, T1_, COS, SIN, AL, BE = range(13)

        def e_tt(o, a, b, op):
            tt("dve", prm, Pm(o), prm, Pm(a), prm, Pm(b), op)

        def e_ts(o, a, s1, s2, op0, op1=None):
            ts("dve", prm, Pm(o), prm, Pm(a), s1, s2, op0, op1)

        act(prm, Pm(DT), prm, Pm(LDT), AF.Exp)
        e_tt(XR, DT, ARE, ALU.mult)
        e_tt(ANG, DT, AIM, ALU.mult)
        e_ts(MAG, XR, 1.0 / 7.0, 1.0, ALU.mult, ALU.add)
        for kk in (6.0, 5.0, 4.0, 3.0, 2.0, 1.0):
            e_tt(MAG, MAG, XR, ALU.mult)
            e_ts(MAG, MAG, 1.0 / kk, 1.0, ALU.mult, ALU.add)
        TWO_PI = float(2 * np.pi)
        for (dst, shift) in ((SIN, 0.0), (COS, float(np.pi / 2))):
            e_ts(T0, ANG, shift, None, ALU.add)
            e_ts(T1_, T0, 1.0 / TWO_PI, 12582912.0, ALU.mult, ALU.add)
            e_ts(T1_, T1_, -12582912.0, None, ALU.add)
            stt("dve", prm, Pm(T0), prm, Pm(T1_), -TWO_PI, prm, Pm(T0), ALU.mult, ALU.add)
            act(prm, Pm(dst), prm, Pm(T0), AF.Sin)
        e_tt(AL, MAG, COS, ALU.mult)
        e_tt(BE, MAG, SIN, ALU.mult)
        ZR, DEN, CR, CI = 13, 14, 15, 3
        e_ts(ZR, AL, -1.0, None, ALU.add)
        e_tt(DEN, ARE, ARE, ALU.mult); e_tt(T0, AIM, AIM, ALU.mult); e_tt(DEN, DEN, T0, ALU.add)
        recip(prm, Pm(DEN), prm, Pm(DEN))
        e_tt(CR, ZR, ARE, ALU.mult); e_tt(T0, BE, AIM, ALU.mult); e_tt(CR, CR, T0, ALU.add); e_tt(CR, CR, DEN, ALU.mult)
        e_tt(CI, BE, ARE, ALU.mult); e_tt(T0, ZR, AIM, ALU.mult); e_tt(CI, CI, T0, ALU.subtract); e_tt(CI, CI, DEN, ALU.mult)
        cp("dve", PW, PW.ap[:, 0, 0, :], prm, Pm(AL))
        ts("dve", PW, PW.ap[:, 0, 1, :], prm, Pm(BE), sgn.ap[:, 0:1], None, ALU.mult, extra_reads=(sgn,))
        cur_pow = 1
        kidx = 1
        while kidx < 9:
            e_tt(T0, AL, AL, ALU.mult); e_tt(T1_, BE, BE, ALU.mult)
            e_tt(BE, AL, BE, ALU.mult); e_ts(BE, BE, 2.0, None, ALU.mult)
            e_tt(AL, T0, T1_, ALU.subtract)
            cur_pow *= 2
            if cur_pow == 2:
                cp("dve", PW, PW.ap[:, 9, 0, :], prm, Pm(AL))
                ts("dve", PW, PW.ap[:, 9, 1, :], prm, Pm(BE), sgn.ap[:, 0:1], None, ALU.mult, extra_reads=(sgn,))
            if cur_pow >= TS:
                cp("dve", PW, PW.ap[:, kidx, 0, :], prm, Pm(AL))
                ts("dve", PW, PW.ap[:, kidx, 1, :], prm, Pm(BE), sgn.ap[:, 0:1], None, ALU.mult, extra_reads=(sgn,))
                kidx += 1
        crb = prm.ap[0:64, CR, :].unsqueeze(2).broadcast_to([64, NR, 16])
        cib = prm.ap[0:64, CI, :].unsqueeze(2).broadcast_to([64, NR, 16])
        B3 = lambda t, i: t.ap[:, i, :].rearrange("p (r h) -> p r h", h=16)
        tt("dve", Bbar, B3(Bbar, 0), Ball, B3(Ball, 0), prm, crb, ALU.mult)
        tt("dve", btmp, B3(btmp, 0), Ball, B3(Ball, 1), prm, cib, ALU.mult)
        tt("dve", Bbar, B3(Bbar, 0), Bbar, B3(Bbar, 0), btmp, B3(btmp, 0), ALU.subtract)
        tt("dve", Bbar, B3(Bbar, 1), Ball, B3(Ball, 1), prm, crb, ALU.mult)
        tt("dve", btmp, B3(btmp, 1), Ball, B3(Ball, 0), prm, cib, ALU.mult)
        tt("dve", Bbar, B3(Bbar, 1), Bbar, B3(Bbar, 1), btmp, B3(btmp, 1), ALU.add)
        alb = PW.ap[0:64, 0, 0, :].unsqueeze(2).broadcast_to([64, NR, 16])
        beb = PW.ap[0:64, 0, 1, :].unsqueeze(2).broadcast_to([64, NR, 16])
        tt("dve", Bbar2, B3(Bbar2, 0), Bbar, B3(Bbar, 0), PW, alb, ALU.mult)
        tt("dve", btmp, B3(btmp, 0), Bbar, B3(Bbar, 1), PW, beb, ALU.mult)
        tt("dve", Bbar2, B3(Bbar2, 0), Bbar2, B3(Bbar2, 0), btmp, B3(btmp, 0), ALU.subtract)
        tt("dve", Bbar2, B3(Bbar2, 1), Bbar, B3(Bbar, 1), PW, alb, ALU.mult)
        tt("dve", btmp, B3(btmp, 1), Bbar, B3(Bbar, 0), PW, beb, ALU.mult)
        tt("dve", Bbar2, B3(Bbar2, 1), Bbar2, B3(Bbar2, 1), btmp, B3(btmp, 1), ALU.add)
        ts("dve", Call, Call.ap[:, :, 64:128], Call, Call.ap[:, :, 64:128], -1.0, None, ALU.mult)
        memset("pool", Cpad, Cpad.ap, 0.0)
        for r in range(NR):
            gp = r % 8
            bp = BP[r % 2]
            memset("pool", bp, bp.ap, 0.0)
            for i in range(2):
                cp("pool", bp, bp.ap[:, i, gp * 16:(gp + 1) * 16], Bbar, Bbar.ap[:, i, r * 16:(r + 1) * 16], part=True)
            ps = PS[1 + (r % 2)]
            for i in range(2):
                mm(ps, ps.ap[:, i * 64:(i + 1) * 64], bp, bp.ap[:, i, :], idF, idF.ap[0:64, 0:64], True, True)
            cp("act", Bpad, Bpad.ap[:, r, :], ps, ps.ap[:, 0:128], part=True)
            bp2 = BP2[r % 2]
            memset("pool", bp2, bp2.ap, 0.0)
            for i in range(2):
                cp("pool", bp2, bp2.ap[:, i, gp * 16:(gp + 1) * 16], Bbar2, Bbar2.ap[:, i, r * 16:(r + 1) * 16], part=True)
            ps2 = PS[5 + (r % 2)]
            for i in range(2):
                mm(ps2, ps2.ap[:, i * 64:(i + 1) * 64], bp2, bp2.ap[:, i, :], idF, idF.ap[0:64, 0:64], True, True)
            cp("act", ABpad, ABpad.ap[:, r, :], ps2, ps2.ap[:, 0:128], part=True)
            pc = PS[3 + (r % 2)]
            mm(pc, pc.ap[:, 0:16], Call, Call.ap[:, r, :], idF, idF.ap[0:16, 0:16], True, True)
            cp("dve", Cpad, Cpad.ap[:, r, gp * 16:(gp + 1) * 16], pc, pc.ap[:, 0:16], part=True)
        P.barrier()
        cur[0] = setup_off
        utok = sb([128, LX], BF16, "utok")
        uJ = sb([128, TS, NCH], BF16, "uJ")
        yac = sb([128, TS, NCH], F32, "yac")
        Am = [sb([128, 128], F32, "Am%d" % i) for i in range(8)]
        A2m = [sb([128, 128], F32, "A2m%d" % i) for i in range(8)]
        Ap = [sb([128, 128], F32, "Ap%d" % i) for i in range(2)]
        hA = [sb([128, 2, NS], F32, "hA%d" % i) for i in range(4)]
        hB = [sb([128, 2, NS], F32, "hB%d" % i) for i in range(4)]
        hb16 = [[sb([128, 2, NS], BF16, "hb%d_%d" % (i, j)) for j in range(2)] for i in range(4)]
        HcU = [sb([128, 2, NS], F32, "Hc%d" % i) for i in range(4)]
        Hc = [T(HcU[g // 2].ap[:, g % 2, :], "Hcr%d" % g) for g in range(8)]
        gel = [sb([128, L // 2], F32, "gel%d" % i) for i in range(2)]
        gout = sb([128, L // 2], BF16, "gout")
        psH = [T(PS[u].ap[:, 0:2 * NS].rearrange("p (a b) -> p a b", b=NS), "psH%d" % u) for u in range(4)]
        psC = [T(PS[4 + i].ap[:, 0:NS], "psC%d" % i) for i in range(2)]
        psY = [T(PS[6 + i].ap[:, 0:NS], "psY%d" % i) for i in range(2)]
        evn = [0]

        def evac(out_t, out_ap, in_t, in_ap):
            e = ("act", "dve")[evn[0] % 2]; evn[0] += 1
            cp(e, out_t, out_ap, in_t, in_ap)

        def build_A(dst, k, r):
            ts("dve", dst, dst.ap, idF, idF.ap, PW.ap[:, k, 0, r:r + 1], None, ALU.mult, extra_reads=(PW,))
            stt("dve", dst, dst.ap, xsw, xsw.ap, PW.ap[:, k, 1, r:r + 1], dst, dst.ap, ALU.mult, ALU.add, extra_reads=(PW,))

        def readout(j, st, rows, c0, ycopy):
            py = psY[st % 2]
            for g in range(8):
                hb = hb16[g // 2][st % 2]
                mm(py, py.ap, Cpad, Cpad.ap[:, rows[g], :], hb, hb.ap[:, g % 2, :], g == 0, g == 7)
            if ycopy:
                evac(yac, yac.ap[:, j, c0:c0 + NS], py, py.ap)
            else:
                tt("dve", yac, yac.ap[:, j, c0:c0 + NS], yac, yac.ap[:, j, c0:c0 + NS], py, py.ap, ALU.add)

        for gc in range(4):
            dma("sp", utok, utok.ap, uT_s, uT_s.ap[gc * 128:(gc + 1) * 128, :])
            cp("pool", uJ, uJ.ap, utok, utok.ap.rearrange("p (c j) -> p j c", j=TS))
            for dr in range(2):
                c0 = 0 if dr == 0 else C0B
                rows = [dr * 32 + gc * 8 + g for g in range(8)]
                ycopy = (dr == 0)
                for g in range(8):
                    build_A(Am[g], 0, rows[g])
                    build_A(A2m[g], 9, rows[g])
                jorder = list(range(TS)) if dr == 0 else list(range(TS - 1, -1, -1))
                for pss in range(2):
                    if pss == 1:
                        for g in range(8):
                            H = Hc[g]
                            k = 1
                            sft = 1
                            while sft < NS:
                                ap_ = Ap[(g + k) % 2]
                                build_A(ap_, k, rows[g])
                                pc = psC[(g + k) % 2]
                                if dr == 0:
                                    mm(pc, pc.ap[:, 0:NS - sft], ap_, ap_.ap, H, H.ap[:, 0:NS - sft], True, True)
                                    tt("dve", H, H.ap[:, sft:NS], H, H.ap[:, sft:NS], pc, pc.ap[:, 0:NS - sft], ALU.add)
                                else:
                                    mm(pc, pc.ap[:, 0:NS - sft], ap_, ap_.ap, H, H.ap[:, sft:NS], True, True)
                                    tt("dve", H, H.ap[:, 0:NS - sft], H, H.ap[:, 0:NS - sft], pc, pc.ap[:, 0:NS - sft], ALU.add)
                                sft *= 2
                                k += 1
                        for u in range(4):
                            h0 = hA[u]
                            memset("pool", h0, h0.ap, 0.0)
                            for r2 in range(2):
                                H = Hc[2 * u + r2]
                                if dr == 0:
                                    cp("pool", h0, h0.ap[:, r2, 1:NS], H, H.ap[:, 0:NS - 1], part=True)
                                else:
                                    cp("pool", h0, h0.ap[:, r2, 0:NS - 1], H, H.ap[:, 1:NS], part=True)
                    hcur = list(hA); hnxt = list(hB)
                    pend = None
                    if pss == 0:
                        for t2 in range(TS // 2):
                            ja, jb = jorder[2 * t2], jorder[2 * t2 + 1]
                            for u in range(4):
                                for r2 in range(2):
                                    g = 2 * u + r2
                                    r = rows[g]
                                    if t2 > 0:
                                        mm(psH[u], psH[u].ap[:, r2, :], A2m[g], A2m[g].ap, hcur[u], hcur[u].ap[:, r2, :], True, False)
                                    mm(psH[u], psH[u].ap[:, r2, :], ABpad, ABpad.ap[:, r, :], uJ, uJ.ap[:, ja, c0:c0 + NS], t2 == 0, False)
                                    mm(psH[u], psH[u].ap[:, r2, :], Bpad, Bpad.ap[:, r, :], uJ, uJ.ap[:, jb, c0:c0 + NS], False, True)
                                if t2 == TS // 2 - 1:
                                    e = ("act", "dve")[evn[0] % 2]; evn[0] += 1
                                    if e == "act":
                                        P.op("act", lambda e_, o=HcU[u].ap, i=psH[u].ap: e_.copy(out=o, in_=i), reads=[psH[u].b], writes=[Hc[2 * u].b, Hc[2 * u + 1].b])
                                    else:
                                        P.op("dve", lambda e_, o=HcU[u].ap, i=psH[u].ap: e_.tensor_copy(out=o, in_=i), reads=[psH[u].b], writes=[Hc[2 * u].b, Hc[2 * u + 1].b])
                                else:
                                    evac(hnxt[u], hnxt[u].ap, psH[u], psH[u].ap)
                            hcur, hnxt = hnxt, hcur
                        continue
                    for st, j in enumerate(jorder):
                        for u in range(4):
                            first = (pss == 0 and st == 0)
                            for r2 in range(2):
                                g = 2 * u + r2
                                r = rows[g]
                                if not first:
                                    mm(psH[u], psH[u].ap[:, r2, :], Am[g], Am[g].ap, hcur[u], hcur[u].ap[:, r2, :], True, False)
                                mm(psH[u], psH[u].ap[:, r2, :], Bpad, Bpad.ap[:, r, :], uJ, uJ.ap[:, j, c0:c0 + NS], first, True)
                            if pss == 0 and st == TS - 1:
                                e = ("act", "dve")[evn[0] % 2]; evn[0] += 1
                                if e == "act":
                                    P.op("act", lambda e_, o=HcU[u].ap, i=psH[u].ap: e_.copy(out=o, in_=i), reads=[psH[u].b], writes=[Hc[2 * u].b, Hc[2 * u + 1].b])
                                else:
                                    P.op("dve", lambda e_, o=HcU[u].ap, i=psH[u].ap: e_.tensor_copy(out=o, in_=i), reads=[psH[u].b], writes=[Hc[2 * u].b, Hc[2 * u + 1].b])
                            else:
                                evac(hnxt[u], hnxt[u].ap, psH[u], psH[u].ap)
                            if pss == 1:
                                hb = hb16[u][st % 2]
                                cp("pool", hb, hb.ap, hnxt[u], hnxt[u].ap)
                        if pss == 1:
                            if pend is not None:
                                readout(pend[0], pend[1], rows, c0, ycopy)
                            pend = (j, st)
                        hcur, hnxt = hnxt, hcur
                    if pss == 1:
                        readout(pend[0], pend[1], rows, c0, ycopy)
            LH = L // 2
            for hf in range(2):
                CL = slice(C0B + hf * (LH // TS), C0B + (hf + 1) * (LH // TS))
                g0 = gel[0]; g1 = gel[1]
                v3 = lambda t: t.ap.rearrange("p (c j) -> p j c", j=TS)
                stt("dve", g0, v3(g0), uJ, uJ.ap[:, :, CL], dsk.ap[:, gc:gc + 1], yac, yac.ap[:, :, CL], ALU.mult, ALU.add, extra_reads=(dsk,))
                tt("pool", g1, g1.ap, g0, g0.ap, g0, g0.ap, ALU.mult)
                ts("dve", g1, g1.ap, g1, g1.ap, 0.044715, 1.0, ALU.mult, ALU.add)
                tt("pool", g1, g1.ap, g1, g1.ap, g0, g0.ap, ALU.mult)
                act(g1, g1.ap, g1, g1.ap, AF.Sigmoid, scale=1.5957691216057308)
                tt("dve", gout, gout.ap, g0, g0.ap, g1, g1.ap, ALU.mult)
                dma("sp", gT_s, gT_s.ap[gc * 128:(gc + 1) * 128, hf * LH:(hf + 1) * LH], gout, gout.ap)
        P.barrier()

    def emit_readout(j, st, rows, hb16, psY, Cpad, yac, c0, ycopy, evac):
        py = psY[st % 2]
        for g in range(4):
            hb = hb16[g][st % 2]
            mm(py, py.ap, Cpad, Cpad.ap[:, rows[g], :], hb, hb.ap, g == 0, g == 3)
        NSl = py.ap.shape[1]
        if ycopy:
            evac(yac, yac.ap[:, j, c0:c0 + NSl], py, py.ap)
        else:
            tt("dve", yac, yac.ap[:, j, c0:c0 + NSl], yac, yac.ap[:, j, c0:c0 + NSl], py, py.ap, ALU.add)

    def glu_phase():
        cur[0] = PERSIST_END
        Wg = sb([128, 4, 512], BF16, "Wg"); wst_ = sb([128, 4, 512], F32, "wgst")
        bg = sb([128, 4], F32, "bg")
        gb = [sb([128, 4, 512], BF16, "gb%d" % i) for i in range(2)]
        so = [sb([128, 4, 512], BF16, "so%d" % i) for i in range(2)]
        sgm = [sb([128, 512], F32, "sgm%d" % i) for i in range(2)]
        dma("sp", wst_, wst_.ap, None, wglu_d.rearrange("(k p) n -> p k n", p=128))
        cp("dve", Wg, Wg.ap, wst_, wst_.ap)
        dma("sp", bg, bg.ap, None, bglu_d.rearrange("(c p) -> p c", p=128), **NC_)
        def gload(tb):
            dma("sp", gb[tb % 2], gb[tb % 2].ap, gT_s, gT_s.ap[:, tb * 512:(tb + 1) * 512].rearrange("(k p) t -> p k t", p=128))
        gload(0)
        for tb in range(L // 512):
            g_ = gb[tb % 2]; s_ = so[tb % 2]
            if tb + 1 < L // 512:
                gload(tb + 1)
            for m in range(4):
                ps = PS[m % 2]
                for k in range(4):
                    mm(ps, ps.ap, Wg, Wg.ap[:, k, m * 128:(m + 1) * 128], g_, g_.ap[:, k, :], k == 0, k == 3)
                sm = sgm[m % 2]
                act(sm, sm.ap, ps, ps.ap, AF.Sigmoid, extra_reads=(bg,), bias=bg.ap[:, m:m + 1])
                tt("dve", s_, s_.ap[:, m, :], g_, g_.ap[:, m, :], sm, sm.ap, ALU.mult, part=True)
            dma("sp", sgT_s, sgT_s.ap[:, tb * 512:(tb + 1) * 512].rearrange("(k p) t -> p k t", p=128), s_, s_.ap)
        P.barrier()

    def attn_phase():
        cur[0] = PERSIST_END
        kT = sb([128, 4, LK], BF16, "kT")
        vA = sb([128, NKB, 512], BF16, "vA")
        Wo = sb([128, 8, D], BF16, "Wo")
        gate5 = sb([128, D], F32, "gate5")
        gsubT = sb([128, 1], F32, "gsubT")
        onesB = sb([128, 128], BF16, "onesB"); onesF = sb([128, 128], F32, "onesF")
        lam = sb([128, 8], F32, "lam"); lqk = sb([128, 2, 128], F32, "lqk")
        qT = [sb([128, 4, 512], BF16, "qT%d" % i) for i in range(2)]
        catT = sb([128, 8, 512], BF16, "catT")
        eT = [sb([128, 512], BF16, "eT%d" % i) for i in range(3)]
        w0 = sb([128, 512], F32, "w0"); w1 = sb([128, 512], F32, "w1"); w2 = sb([128, 512], F32, "w2")
        w3 = sb([128, 512], F32, "w3"); w4 = sb([128, 512], F32, "w4"); w5 = sb([128, 512], F32, "w5")
        x1t = [sb([128, D], F32, "x1t%d" % i) for i in range(2)]
        yall = sb([128, D], F32, "stgD0")
        ytmp = [T(yall.ap[:, 0:512], "ytmpD0"), T(yall.ap[:, 512:1024], "ytmpD1")]
        stg = [yall]
        dma("sp", gate5, gate5.ap, mod_s, mod_s.ap[0:1, 5 * D:6 * D].partition_broadcast(128).rearrange("p a d -> p (a d)"))
        dma("sp", gsubT, gsubT.ap, None, sg_d.rearrange("(p o) -> p o", o=1))
        ts("dve", gsubT, gsubT.ap, gsubT, gsubT.ap, 1.0 - LAM_INIT, None, ALU.mult)
        memset("pool", onesB, onesB.ap, 1.0)
        memset("pool", onesF, onesF.ap, 1.0 / 128.0)
        dma("sp", lqk, lqk.ap[:, 0, :], None, lq_d.partition_broadcast(128))
        dma("sp", lqk, lqk.ap[:, 1, :], None, lk_d.partition_broadcast(128))
        tt("dve", lqk, lqk.ap[:, 0, :], lqk, lqk.ap[:, 0, :], lqk, lqk.ap[:, 1, :], ALU.mult)
        P.op("dve", lambda e: e.reduce_sum(out=lam.ap[:, 0:2], in_=lqk.ap[:, 0, :].rearrange("p (a b) -> p a b", b=64), axis=mybir.AxisListType.X),
             reads=[lqk.b], writes=[lam.b])
        act(lam, lam.ap[:, 2:4], lam, lam.ap[:, 0:2], AF.Exp)
        tt("dve", lam, lam.ap[:, 4:5], lam, lam.ap[:, 2:3], lam, lam.ap[:, 3:4], ALU.subtract)
        ts("dve", lam, lam.ap[:, 5:6], lam, lam.ap[:, 4:5], LAM_INIT, -1.0, ALU.add, ALU.mult)
        for m2 in range(8):
            st = stg[0]
            dma("sp", st, st.ap, None, wout_d[m2 * 128:(m2 + 1) * 128, :])
            cp("dve", Wo, Wo.ap[:, m2, :], st, st.ap, part=True)
        for h in range(4):
            dma("sp", kT, kT.ap[:, h, :], kT_s, kT_s.ap[h * 128:(h + 1) * 128, :])
        for k0 in range(0, NKB, 6):
            k1 = min(NKB, k0 + 6)
            dma("sp", vA, vA.ap[:, k0:k1, :], v_s, v_s.ap[k0 * 128:k1 * 128, :].rearrange("(kb p) e -> p kb e", p=128))
        psS = [PS[0], PS[1]]
        psYd = PS[7]

        def banks(h, c):
            if c == 1:
                return PS[3], PS[5]
            return (PS[2], PS[4]) if h % 2 == 0 else (PS[6], PS[7])
        scnt = 0

        def qload(qb):
            dma("sp", qT[qb % 2], qT[qb % 2].ap, qT_s, qT_s.ap[:, qb * 512:(qb + 1) * 512].rearrange("(h p) t -> p h t", p=128))
        qload(0)
        for qb in range(L // 512):
            q_ = qT[qb % 2]
            if qb + 1 < L // 512:
                qload(qb + 1)
            dma("sp", catT, catT.ap[:, 0:4, :], sgT_s, sgT_s.ap[:, qb * 512:(qb + 1) * 512].rearrange("(k p) t -> p k t", p=128))
            its = [(h, c, kb) for h in range(4) for c in range(2) for kb in range(NKB)]

            def s_mm(i):
                h, c, kb = its[i]
                pl = slice(c * 64, (c + 1) * 64)
                pS = psS[(scnt0 + i) % 2]
                mm(pS, pS.ap, kT, kT.ap[pl, h, kb * 128:(kb + 1) * 128], q_, q_.ap[pl, h, :], True, True)
            scnt0 = scnt

            def warm(n):
                for _ in range(n):
                    mm(PS[6], PS[6].ap, onesB, onesB.ap, kT, kT.ap[:, 0, 0:512], True, True)
            warm(WARM_N)
            s_mm(0)
            pending = []
            for i, (h, c, kb) in enumerate(its):
                for pd in list(pending):
                    pd[0] -= 1
                    if pd[0] <= 0:
                        pd[1]()
                        pending.remove(pd)
                pS = psS[(scnt0 + i) % 2]; e_ = eT[(scnt0 + i) % 3]
                act(e_, e_.ap, pS, pS.ap, AF.Exp, scale=0.125)
                if i + 1 < len(its):
                    s_mm(i + 1)
                if kb % BURST_EVERY == BURST_EVERY // 2:
                    wb = PS[7] if h % 2 == 0 else PS[4]
                    for _ in range(BURST_N):
                        mm(wb, wb.ap, onesB, onesB.ap, kT, kT.ap[:, 0, 0:512], True, True)
                pO, pD = banks(h, c)
                mm(pO, pO.ap, vA, vA.ap[:, kb, h * 128:(h + 1) * 128], e_, e_.ap, kb == 0, kb == NKB - 1)
                mm(pD, pD.ap, onesB, onesB.ap, e_, e_.ap, kb == 0, kb == NKB - 1)
                if not (c == 1 and kb == NKB - 1):
                    continue
                pO0, pD0 = banks(h, 0)
                pO1, pD1 = banks(h, 1)
                cp("act", w0, w0.ap, pD0, pD0.ap)
                cp("dve", w4, w4.ap, pO0, pO0.ap)
                cp("act", w1, w1.ap, pD1, pD1.ap)
                cp("dve", w5, w5.ap, pO1, pO1.ap)
                recip(w0, w0.ap, w0, w0.ap)
                tt("dve", w4, w4.ap, w4, w4.ap, w0, w0.ap, ALU.mult)
                recip(w1, w1.ap, w1, w1.ap)
                tt("dve", w5, w5.ap, w5, w5.ap, w1, w1.ap, ALU.mult)
                stt("dve", w4, w4.ap, w5, w5.ap, lam.ap[:, 5:6], w4, w4.ap, ALU.mult, ALU.add, extra_reads=(lam,))
                tt("pool", w2, w2.ap, w4, w4.ap, w4, w4.ap, ALU.mult)

                def fin(h=h, psM=pO0):
                    mm(psM, psM.ap, onesF, onesF.ap, w2, w2.ap, True, True)
                    act(w3, w3.ap, psM, psM.ap, AF.Ln, extra_reads=(epsT,), bias=epsT.ap[:, 0:1])
                    act(w3, w3.ap, w3, w3.ap, AF.Exp, scale=-0.5)
                    stt("dve", catT, catT.ap[:, 4 + h, :], w4, w4.ap, gsubT.ap[:, 0:1], w3, w3.ap, ALU.mult, ALU.mult, extra_reads=(gsubT,), part=True)
                pending.append([DEFER_N, fin])
            for pd in pending:
                pd[1]()
            pending.clear()
            scnt += len(its)
            for s in range(4):
                xt = x1t[s % 2]
                r0 = qb * 512 + s * 128
                dma("sp", xt, xt.ap, x1_s, x1_s.ap[r0:r0 + 128, :])
                for nh in range(2):
                    for kc in range(8):
                        mm(psYd, psYd.ap, catT, catT.ap[:, kc, s * 128:(s + 1) * 128], Wo, Wo.ap[:, kc, nh * 512:(nh + 1) * 512], kc == 0, kc == 7)
                    yt = ytmp[nh]
                    tt("dve", yt, yt.ap, psYd, psYd.ap, gate5, gate5.ap[:, nh * 512:(nh + 1) * 512], ALU.mult)
                    tt("pool", xt, xt.ap[:, nh * 512:(nh + 1) * 512], xt, xt.ap[:, nh * 512:(nh + 1) * 512], yt, yt.ap, ALU.add)
                dma("sp", x2_s, x2_s.ap[r0:r0 + 128, :], xt, xt.ap)
        P.barrier()

    ffn_phase(0)
    inproj_phase()
    s5_phase()
    glu_phase()
    attn_phase()
    ffn_phase(1)
    P.emit()
    return nc, P


def host_consts(L):
    f32 = np.float32
    rows = L // 64
    row = np.repeat(np.arange(rows, dtype=f32), 64)
    col = np.tile(np.arange(64, dtype=f32), rows)
    inv_freq = np.power(f32(10000.0), -(np.arange(16, dtype=f32) / f32(16))).astype(f32)
    cos = np.zeros((128, L), f32); sin = np.zeros((128, L), f32)
    for c in range(2):
        for ax, pos in enumerate((row, col)):
            ang = (pos[:, None] * inv_freq[None, :]).astype(f32)
            for half in range(2):
                p0 = c * 64 + ax * 32 + half * 16
                cos[p0:p0 + 16, :] = np.cos(ang).T
                sin[p0:p0 + 16, :] = (np.sin(ang).T) * (f32(-1.0) if half == 0 else f32(1.0))
    ident = np.eye(128, dtype=f32)
    xsw = np.zeros((128, 128), f32)
    for k in range(128):
        xsw[k, (k + 64) % 128] = 1.0
    sgn = np.ones((128, 1), f32); sgn[64:] = -1.0
    return dict(rope_cos=cos, rope_sin=sin, ident=ident, xswap=xsw, sgn=sgn)


_CACHE = {}


def make_in_maps(inputs, L, ncores):
    f32 = np.float32
    A = lambda a: np.ascontiguousarray(np.asarray(a, dtype=f32))
    w_in = A(inputs["w_in"])[0]
    perm = np.arange(512) ^ 16
    w_in_ext = np.ascontiguousarray(np.concatenate([w_in, w_in[:, 512 + perm], w_in[:, 1024 + perm]], axis=1))
    shared = dict(
        c_ctx=A(inputs["c_ctx"]), w_mod=A(inputs["w_mod"])[0], b_mod=A(inputs["b_mod"])[0], norm_g=A(inputs["norm_g"])[0],
        ffn_w_in=A(inputs["ffn_w_in"])[0], ffn_w_out=A(inputs["ffn_w_out"])[0], w_in_ext=w_in_ext, w_out=A(inputs["w_out"])[0],
        ssm_a_re=A(inputs["ssm_a_re"])[0].reshape(64, 64), ssm_a_im=A(inputs["ssm_a_im"])[0].reshape(64, 64),
        ssm_log_dt=A(inputs["ssm_log_dt"])[0].reshape(64),
        ssm_b_re=A(inputs["ssm_b_re"])[0].reshape(64, 64, 16), ssm_b_im=A(inputs["ssm_b_im"])[0].reshape(64, 64, 16),
        ssm_c_re=A(inputs["ssm_c_re"])[0].reshape(64, 16, 64), ssm_c_im=A(inputs["ssm_c_im"])[0].reshape(64, 16, 64),
        ssm_d=A(inputs["ssm_d"])[0].reshape(512), w_glu=A(inputs["w_glu"])[0], b_glu=A(inputs["b_glu"])[0],
        lam_q=A(inputs["lam_q"])[0].reshape(128), lam_k=A(inputs["lam_k"])[0].reshape(128),
        subln_g=A(inputs["subln_g"])[0], final_g=A(inputs["final_g"]),
    )
    shared.update(host_consts(L))
    x = A(inputs["x"]); c = A(inputs["c"]); ctx = A(inputs["ctx"])
    maps = []
    for b in range(ncores):
        m = dict(shared)
        m["x"] = np.ascontiguousarray(x[b, :L]); m["c"] = np.ascontiguousarray(c[b]); m["ctx"] = np.ascontiguousarray(ctx[b])
        maps.append(m)
    return maps


def kernel(**inputs):
    x = np.asarray(inputs["x"])
    B, L, _ = x.shape
    if L not in _CACHE:
        _CACHE[L] = build(L)
    nc, _ = _CACHE[L]
    maps = make_in_maps(inputs, L, B)
    res = run_bass_kernel_spmd(nc, maps, core_ids=list(range(B)))
    out = np.stack([np.asarray(r["out"], dtype=np.float32) for r in res.results], axis=0)
    return out
```
